# Optimizing a Trainium2 kernel written in Bass

```python
import math
import jax
import jax.numpy as jnp
from jax import lax
import numpy as np

D_MODEL = 1024
BATCH = 8
SEQ = 2048
DEPTH = 4
DEC_BATCH = 128
DEC_SEQ = 1
PAST_LEN = 16384
PAGE_SIZE = 128

N_META = 16
NORM_EPS = 1e-6
CHUNK = 64
CONV_W = 4
N_BRANCH = 3

RW_HEADS = 8
RW_HD = 64
RW_W = RW_HEADS * RW_HD
RW_DECAY_LORA = 64
RW_AAA_LORA = 64
RW_SHIFT_W = 3 * RW_W + RW_DECAY_LORA + RW_AAA_LORA
RW_GN_EPS = 64e-5

SSM_HEADS = 8
SSM_HD = 64
SSM_W = SSM_HEADS * SSM_HD
SSM_STATE = 128
SSM_GROUPS = 2
SSM_CONV_CH = SSM_W + 2 * SSM_GROUPS * SSM_STATE

GDN_HEADS = 4
GDN_HD = 128
GDN_W = GDN_HEADS * GDN_HD
GDN_CONV_CH = 3 * GDN_W

OFF_RW = 0
OFF_RW_Z = OFF_RW + RW_SHIFT_W
OFF_SSM_Z = OFF_RW_Z + RW_W
OFF_SSM_XBC = OFF_SSM_Z + SSM_W
OFF_SSM_DT = OFF_SSM_XBC + SSM_CONV_CH
OFF_GDN_QKV = OFF_SSM_DT + SSM_HEADS
OFF_GDN_Z = OFF_GDN_QKV + GDN_CONV_CH
OFF_GDN_A = OFF_GDN_Z + GDN_W
OFF_GDN_B = OFF_GDN_A + GDN_HEADS
OFF_GATE = OFF_GDN_B + GDN_HEADS
D_IN_PROJ = OFF_GATE + N_BRANCH * D_MODEL

F32 = jnp.float32

kernel_name = 'hybrid_rwkv7_mamba2_gdn_decoder_step'


def rmsnorm(x, w, eps=NORM_EPS):
    xf = x.astype(F32)
    return xf * lax.rsqrt(jnp.mean(xf * xf, axis=-1, keepdims=True) + eps) * w.astype(F32)


def l2norm(x, eps=1e-6):
    return x * lax.rsqrt(jnp.sum(x * x, axis=-1, keepdims=True) + eps)


def chunk_len(L):
    return CHUNK if L % CHUNK == 0 else L


def causal_conv(u, buf, w):
    L = u.shape[1]
    full = jnp.concatenate([buf.astype(F32), u], axis=1)
    out = full[:, 0:L] * w[0]
    for i in range(1, CONV_W):
        out = out + full[:, i:i + L] * w[i]
    return out, full[:, L:]


def run_segments(fn, seqs, state, split):
    if split:
        y0, state = fn(*[s[:, :split] for s in seqs], state)
        y1, state = fn(*[s[:, split:] for s in seqs], state)
        return jnp.concatenate([y0, y1], axis=1), state
    return fn(*seqs, state)


def wkv7_scan(r, logw, k, v, a, b, s0):
    def step(S, inp):
        r_t, lw_t, k_t, v_t, a_t, b_t = inp
        sa = jnp.einsum('bhvk,bhk->bhv', S, a_t)
        S = (S * jnp.exp(lw_t)[:, :, None, :] + sa[..., None] * b_t[:, :, None, :]
             + v_t[..., None] * k_t[:, :, None, :])
        return S, jnp.einsum('bhvk,bhk->bhv', S, r_t)
    xs = tuple(jnp.moveaxis(t, 1, 0) for t in (r, logw, k, v, a, b))
    S, ys = lax.scan(step, s0, xs)
    return jnp.moveaxis(ys, 0, 1), S


def rwkv7_mix(p_rw, wkv0, prev0, mu, w0, w2, a0, a2, k_k, k_a, r_k, gn_w, gn_b):
    Bsz, L, _ = p_rw.shape
    shifted = jnp.concatenate([prev0.astype(F32)[:, None], p_rw[:, :-1]], axis=1)
    u = p_rw + (shifted - p_rw) * mu
    r = u[..., :RW_W]
    k = u[..., RW_W:2 * RW_W]
    v = u[..., 2 * RW_W:3 * RW_W]
    wd = u[..., 3 * RW_W:3 * RW_W + RW_DECAY_LORA]
    ad = u[..., 3 * RW_W + RW_DECAY_LORA:]
    w = -jax.nn.softplus(-(w0 + jnp.tanh(wd) @ w2)) - 0.5
    log_decay = -jnp.exp(w)
    a = jax.nn.sigmoid(a0 + ad @ a2)
    hs = lambda t: t.reshape(Bsz, L, RW_HEADS, RW_HD)
    kk = l2norm(hs(k * k_k))
    k = k * (1.0 + (a - 1.0) * k_a)
    r, k, v, a, log_decay = (hs(t) for t in (r, k, v, a, log_decay))
    y, wkv1 = wkv7_scan(r, log_decay, k, v, -kk, kk * a, wkv0.astype(F32))
    mean = jnp.mean(y, axis=-1, keepdims=True)
    var = jnp.mean(jnp.square(y - mean), axis=-1, keepdims=True)
    y = ((y - mean) * lax.rsqrt(var + RW_GN_EPS)).reshape(Bsz, L, RW_W) * gn_w + gn_b
    bonus = jnp.sum(r * k * r_k, axis=-1, keepdims=True) * v
    return y + bonus.reshape(Bsz, L, RW_W), wkv1, p_rw[:, -1]


def ssd_chunked(x, dt, a, bm, cm, h0):
    Bsz, L, H, P = x.shape
    R = H // SSM_GROUPS
    c = chunk_len(L)
    n = L // c
    x = x.reshape(Bsz, n, c, SSM_GROUPS, R, P)
    dt = dt.reshape(Bsz, n, c, SSM_GROUPS, R)
    bm = bm.reshape(Bsz, n, c, SSM_GROUPS, SSM_STATE)
    cm = cm.reshape(Bsz, n, c, SSM_GROUPS, SSM_STATE)
    acs = jnp.cumsum(dt * a.reshape(SSM_GROUPS, R), axis=2)
    xdt = x * dt[..., None]
    acs_t = jnp.moveaxis(acs, 2, -1)
    tri = jnp.tril(jnp.ones((c, c), dtype=bool))
    seg = jnp.exp(jnp.where(tri, acs_t[..., :, None] - acs_t[..., None, :], -jnp.inf))
    cb = jnp.einsum('bnigs,bnjgs->bngij', cm, bm)
    y_diag = jnp.einsum('bngrij,bnjgrp->bnigrp', cb[:, :, :, None] * seg, xdt)
    decay_end = jnp.exp(acs[:, :, -1:] - acs)
    st = jnp.einsum('bncgs,bncgr,bncgrp->bngrps', bm, decay_end, xdt)
    tot = jnp.exp(acs[:, :, -1])

    def step(h, inp):
        t, s = inp
        return h * t[..., None, None] + s, h
    hN, h_prev = lax.scan(step, h0.reshape(Bsz, SSM_GROUPS, R, P, SSM_STATE),
                          (jnp.moveaxis(tot, 1, 0), jnp.moveaxis(st, 1, 0)))
    h_prev = jnp.moveaxis(h_prev, 0, 1)
    y_off = jnp.einsum('bnigs,bngrps,bnigr->bnigrp', cm, h_prev, jnp.exp(acs))
    return (y_diag + y_off).reshape(Bsz, L, H, P), hN.reshape(Bsz, H, P, SSM_STATE)


def mamba2_mix(xbc_pre, dt_raw, z, ssm0, conv0, conv_w, conv_b, dt_bias, a_log, d_skip, norm_w, split):
    Bsz, L, _ = xbc_pre.shape
    xbc, conv1 = causal_conv(xbc_pre, conv0, conv_w)
    xbc = jax.nn.silu(xbc + conv_b)
    xs = xbc[..., :SSM_W].reshape(Bsz, L, SSM_HEADS, SSM_HD)
    bm = xbc[..., SSM_W:SSM_W + SSM_GROUPS * SSM_STATE].reshape(Bsz, L, SSM_GROUPS, SSM_STATE)
    cm = xbc[..., SSM_W + SSM_GROUPS * SSM_STATE:].reshape(Bsz, L, SSM_GROUPS, SSM_STATE)
    dt = jax.nn.softplus(dt_raw + dt_bias)
    a = -jnp.exp(a_log.astype(F32))
    y, ssm1 = run_segments(lambda x_, dt_, b_, c_, h_: ssd_chunked(x_, dt_, a, b_, c_, h_),
                           (xs, dt, bm, cm), ssm0.astype(F32), split)
    y = (y + xs * d_skip[:, None]).reshape(Bsz, L, SSM_W)
    g = (y * jax.nn.silu(z)).reshape(Bsz, L, SSM_GROUPS, SSM_W // SSM_GROUPS)
    g = g * lax.rsqrt(jnp.mean(g * g, axis=-1, keepdims=True) + 1e-5)
    return g.reshape(Bsz, L, SSM_W) * norm_w, ssm1, conv1


def gdn_chunked(q, k, v, g, beta, s0):
    Bsz, L, H, D = q.shape
    c = chunk_len(L)
    n = L // c
    blk = lambda t: t.reshape(Bsz, n, c, H, D).transpose(0, 3, 1, 2, 4)
    q, k, v = blk(q), blk(k), blk(v)
    g = g.reshape(Bsz, n, c, H).transpose(0, 3, 1, 2)
    beta = beta.reshape(Bsz, n, c, H).transpose(0, 3, 1, 2)
    gcs = jnp.cumsum(g, axis=-1)
    idx = jnp.arange(c)
    incl = idx[:, None] >= idx[None, :]
    strict = idx[:, None] > idx[None, :]
    decay = jnp.exp(jnp.where(incl, gcs[..., :, None] - gcs[..., None, :], -jnp.inf))
    kb = k * beta[..., None]
    a_mat = jnp.where(strict, jnp.einsum('bhnid,bhnjd->bhnij', kb, k) * decay, 0.0)
    rhs = jnp.concatenate([v * beta[..., None], kb * jnp.exp(gcs)[..., None]], axis=-1)
    sol = lax.linalg.triangular_solve(a_mat, rhs, left_side=True, lower=True, unit_diagonal=True)
    u, w_cum = sol[..., :D], sol[..., D:]
    qk = jnp.where(incl, jnp.einsum('bhnid,bhnjd->bhnij', q, k) * decay, 0.0)
    q_dec = q * jnp.exp(gcs)[..., None]
    k_dec = k * jnp.exp(gcs[..., -1:] - gcs)[..., None]
    g_tot = jnp.exp(gcs[..., -1])

    def step(S, inp):
        u_c, w_c, qd_c, qk_c, kd_c, gt_c = inp
        v_new = u_c - jnp.einsum('bhck,bhkv->bhcv', w_c, S)
        o_c = jnp.einsum('bhck,bhkv->bhcv', qd_c, S) + jnp.einsum('bhij,bhjv->bhiv', qk_c, v_new)
        S = S * gt_c[..., None, None] + jnp.einsum('bhck,bhcv->bhkv', kd_c, v_new)
        return S, o_c
    xs = tuple(jnp.moveaxis(t, 2, 0) for t in (u, w_cum, q_dec, qk, k_dec, g_tot))
    S, o = lax.scan(step, s0, xs)
    return o.transpose(1, 0, 3, 2, 4).reshape(Bsz, L, H, D), S


def gdn_mix(qkv_pre, a_raw, b_raw, z, gdn0, conv0, conv_w, dt_bias, a_log, norm_w, split):
    Bsz, L, _ = qkv_pre.shape
    qkv, conv1 = causal_conv(qkv_pre, conv0, conv_w)
    qkv = jax.nn.silu(qkv)
    hs = lambda t: t.reshape(Bsz, L, GDN_HEADS, GDN_HD)
    q = l2norm(hs(qkv[..., :GDN_W])) * (GDN_HD ** -0.5)
    k = l2norm(hs(qkv[..., GDN_W:2 * GDN_W]))
    v = hs(qkv[..., 2 * GDN_W:])
    g = -jnp.exp(a_log.astype(F32)) * jax.nn.softplus(a_raw + dt_bias)
    beta = jax.nn.sigmoid(b_raw)
    o, gdn1 = run_segments(gdn_chunked, (q, k, v, g, beta), gdn0.astype(F32), split)
    o = o * lax.rsqrt(jnp.mean(o * o, axis=-1, keepdims=True) + 1e-6) * norm_w
    return o.reshape(Bsz, L, GDN_W) * jax.nn.silu(z), gdn1, conv1


def mixer_layer(x, st, p, split):
    wkv0, prev0, ssm0, sconv0, gdn0, gconv0 = st
    Bsz, T, _ = x.shape
    h = rmsnorm(x, p['norm_w'])
    proj = h @ p['w_in']
    rw, wkv1, prev1 = rwkv7_mix(proj[..., OFF_RW:OFF_RW + RW_SHIFT_W], wkv0, prev0, p['rw_mu'],
                                p['rw_w0'], p['rw_w2'], p['rw_a0'], p['rw_a2'], p['rw_k_k'],
                                p['rw_k_a'], p['rw_r_k'], p['rw_gn_w'], p['rw_gn_b'])
    rw = rw * jax.nn.silu(proj[..., OFF_RW_Z:OFF_RW_Z + RW_W])
    sm, ssm1, sconv1 = mamba2_mix(proj[..., OFF_SSM_XBC:OFF_SSM_XBC + SSM_CONV_CH],
                                  proj[..., OFF_SSM_DT:OFF_SSM_DT + SSM_HEADS],
                                  proj[..., OFF_SSM_Z:OFF_SSM_Z + SSM_W], ssm0, sconv0,
                                  p['ssm_conv_w'], p['ssm_conv_b'], p['ssm_dt_bias'],
                                  p['ssm_a_log'], p['ssm_d'], p['ssm_norm_w'], split)
    gd, gdn1, gconv1 = gdn_mix(proj[..., OFF_GDN_QKV:OFF_GDN_QKV + GDN_CONV_CH],
                               proj[..., OFF_GDN_A:OFF_GDN_A + GDN_HEADS],
                               proj[..., OFF_GDN_B:OFF_GDN_B + GDN_HEADS],
                               proj[..., OFF_GDN_Z:OFF_GDN_Z + GDN_W], gdn0, gconv0,
                               p['gdn_conv_w'], p['gdn_dt_bias'], p['gdn_a_log'],
                               p['gdn_norm_w'], split)
    gates = jax.nn.sigmoid(proj[..., OFF_GATE:].reshape(Bsz, T, N_BRANCH, D_MODEL))
    merged = (gates[..., 0, :] * (rw @ p['w_rw_out'])
              + gates[..., 1, :] * (sm @ p['w_ssm_out'])
              + gates[..., 2, :] * (gd @ p['w_gdn_out']))
    x = x + (merged @ p['w_out']).astype(x.dtype)
    new_st = (wkv1.astype(wkv0.dtype), prev1.astype(prev0.dtype), ssm1.astype(ssm0.dtype),
              sconv1.astype(sconv0.dtype), gdn1.astype(gdn0.dtype), gconv1.astype(gconv0.dtype))
    return x, new_st


def trunk(x, states, layer_params, final_norm_w, split):
    new_states = [[] for _ in states]
    for l in range(DEPTH):
        p = {name: arr[l] for name, arr in layer_params.items()}
        x, st = mixer_layer(x, tuple(s[l] for s in states), p, split)
        for acc, s in zip(new_states, st):
            acc.append(s)
    y = rmsnorm(x, final_norm_w).astype(x.dtype)
    return y, tuple(jnp.stack(acc) for acc in new_states)


def setup_inputs(seed: int = 0) -> dict:
    key = jax.random.key(seed)
    keys = iter(jax.random.split(key, 40))

    def nrm(shape, scale):
        return jax.random.normal(next(keys), shape, F32) * scale

    def unif(shape, lo, hi):
        return jax.random.uniform(next(keys), shape, F32, lo, hi)

    def gain(shape):
        return 1.0 + nrm(shape, 0.02)

    def dt_bias(shape):
        dt = jnp.exp(unif(shape, math.log(1e-3), math.log(1e-1)))
        return dt + jnp.log(-jnp.expm1(-dt))

    return {
        'x_prompt': nrm((BATCH, SEQ, D_MODEL), 1.0),
        'x_sample': nrm((DEC_BATCH, DEC_SEQ, D_MODEL), 1.0),
        'state_rwkv_wkv': nrm((DEPTH, DEC_BATCH, RW_HEADS, RW_HD, RW_HD), 0.3),
        'state_rwkv_shift': nrm((DEPTH, DEC_BATCH, RW_SHIFT_W), 1.0),
        'state_ssm': nrm((DEPTH, DEC_BATCH, SSM_HEADS, SSM_HD, SSM_STATE), 0.3),
        'state_ssm_conv': nrm((DEPTH, DEC_BATCH, CONV_W - 1, SSM_CONV_CH), 1.0),
        'state_gdn': nrm((DEPTH, DEC_BATCH, GDN_HEADS, GDN_HD, GDN_HD), 0.1),
        'state_gdn_conv': nrm((DEPTH, DEC_BATCH, CONV_W - 1, GDN_CONV_CH), 1.0),
        'meta_tokens': nrm((N_META, D_MODEL), 1.0),
        'norm_w': gain((DEPTH, D_MODEL)),
        'w_in': nrm((DEPTH, D_MODEL, D_IN_PROJ), D_MODEL ** -0.5),
        'rw_mu': unif((DEPTH, RW_SHIFT_W), 0.0, 1.0),
        'rw_w0': unif((DEPTH, RW_W), -6.0, 1.0),
        'rw_w2': nrm((DEPTH, RW_DECAY_LORA, RW_W), 0.1 * RW_DECAY_LORA ** -0.5),
        'rw_a0': nrm((DEPTH, RW_W), 0.1),
        'rw_a2': nrm((DEPTH, RW_AAA_LORA, RW_W), RW_AAA_LORA ** -0.5),
        'rw_k_k': 0.85 + nrm((DEPTH, RW_W), 0.02),
        'rw_k_a': gain((DEPTH, RW_W)),
        'rw_r_k': nrm((DEPTH, RW_HEADS, RW_HD), 0.1),
        'rw_gn_w': gain((DEPTH, RW_W)),
        'rw_gn_b': nrm((DEPTH, RW_W), 0.02),
        'ssm_conv_w': nrm((DEPTH, CONV_W, SSM_CONV_CH), CONV_W ** -0.5),
        'ssm_conv_b': nrm((DEPTH, SSM_CONV_CH), 0.02),
        'ssm_dt_bias': dt_bias((DEPTH, SSM_HEADS)),
        'ssm_a_log': jnp.log(unif((DEPTH, SSM_HEADS), 1.0, 16.0)),
        'ssm_d': 1.0 + nrm((DEPTH, SSM_HEADS), 0.1),
        'ssm_norm_w': gain((DEPTH, SSM_W)),
        'gdn_conv_w': nrm((DEPTH, CONV_W, GDN_CONV_CH), CONV_W ** -0.5),
        'gdn_dt_bias': dt_bias((DEPTH, GDN_HEADS)),
        'gdn_a_log': jnp.log(unif((DEPTH, GDN_HEADS), 1.0, 16.0)),
        'gdn_norm_w': gain((DEPTH, GDN_HD)),
        'w_rw_out': nrm((DEPTH, RW_W, D_MODEL), RW_W ** -0.5),
        'w_ssm_out': nrm((DEPTH, SSM_W, D_MODEL), SSM_W ** -0.5),
        'w_gdn_out': nrm((DEPTH, GDN_W, D_MODEL), GDN_W ** -0.5),
        'w_out': nrm((DEPTH, D_MODEL, D_MODEL), D_MODEL ** -0.5),
        'final_norm_w': gain((D_MODEL,)),
    }


def reference(x_prompt, x_sample, state_rwkv_wkv, state_rwkv_shift, state_ssm, state_ssm_conv,
              state_gdn, state_gdn_conv, meta_tokens, norm_w, w_in, rw_mu, rw_w0, rw_w2, rw_a0,
              rw_a2, rw_k_k, rw_k_a, rw_r_k, rw_gn_w, rw_gn_b, ssm_conv_w, ssm_conv_b,
              ssm_dt_bias, ssm_a_log, ssm_d, ssm_norm_w, gdn_conv_w, gdn_dt_bias, gdn_a_log,
              gdn_norm_w, w_rw_out, w_ssm_out, w_gdn_out, w_out, final_norm_w):
    layer_params = {
        'norm_w': norm_w, 'w_in': w_in, 'rw_mu': rw_mu, 'rw_w0': rw_w0, 'rw_w2': rw_w2,
        'rw_a0': rw_a0, 'rw_a2': rw_a2, 'rw_k_k': rw_k_k, 'rw_k_a': rw_k_a, 'rw_r_k': rw_r_k,
        'rw_gn_w': rw_gn_w, 'rw_gn_b': rw_gn_b, 'ssm_conv_w': ssm_conv_w,
        'ssm_conv_b': ssm_conv_b, 'ssm_dt_bias': ssm_dt_bias, 'ssm_a_log': ssm_a_log,
        'ssm_d': ssm_d, 'ssm_norm_w': ssm_norm_w, 'gdn_conv_w': gdn_conv_w,
        'gdn_dt_bias': gdn_dt_bias, 'gdn_a_log': gdn_a_log, 'gdn_norm_w': gdn_norm_w,
        'w_rw_out': w_rw_out, 'w_ssm_out': w_ssm_out, 'w_gdn_out': w_gdn_out, 'w_out': w_out,
    }
    sample_states = (state_rwkv_wkv, state_rwkv_shift, state_ssm, state_ssm_conv,
                     state_gdn, state_gdn_conv)
    bp = x_prompt.shape[0]
    prompt_states = tuple(jnp.zeros((DEPTH, bp) + s.shape[2:], x_prompt.dtype) for s in sample_states)
    meta = jnp.broadcast_to(meta_tokens.astype(x_prompt.dtype)[None], (bp, N_META, D_MODEL))
    x_full = jnp.concatenate([meta, x_prompt], axis=1)
    y_full, (p_wkv, p_shift, p_ssm, p_ssm_conv, p_gdn, p_gdn_conv) = trunk(
        x_full, prompt_states, layer_params, final_norm_w, N_META)
    y_prompt = y_full[:, N_META:]
    y_sample, (s_wkv, s_shift, s_ssm, s_ssm_conv, s_gdn, s_gdn_conv) = trunk(
        x_sample, sample_states, layer_params, final_norm_w, 0)
    return (y_prompt, y_sample, p_wkv, p_shift, p_ssm, p_ssm_conv, p_gdn, p_gdn_conv,
            s_wkv, s_shift, s_ssm, s_ssm_conv, s_gdn, s_gdn_conv)
```

```python
import numpy as np
import concourse.bass as bass
import concourse.mybir as mybir
from concourse.bass_utils import run_bass_kernel_spmd

F32 = mybir.dt.float32
BF16 = mybir.dt.bfloat16
ALU = mybir.AluOpType
AF = mybir.ActivationFunctionType
AX = mybir.AxisListType

DEPTH = 4
D = 1024
NMETA = 16
SEQ = 2048
TP = NMETA + SEQ
NS = 16
TT = TP + NS
DIN = 8848
C_RW, C_SSM, C_GDN, C_GATE = 0, 2176, 3720, 5776
EXPM05 = float(np.exp(-0.5))

SEM_EPOCH = 20000
N_DMA_SLOTS = 6


class _Sem:
    __slots__ = ("h", "val")

    def __init__(self, h):
        self.h = h
        self.val = 0


class _Buf:
    __slots__ = ("w", "r")

    def __init__(self):
        self.w = None
        self.r = {}


class _Eng:
    def __init__(self, name):
        self.name = name
        self.sem = None
        self.prog = []
        self.known = {}
        self.slots = []
        self.slot_i = 0


class Sched:
    def __init__(self, nc):
        self.nc = nc
        self.bufs = {}
        self.E = {n: _Eng(n) for n in ("pe", "dve", "act", "pool", "sp")}
        self.nsem = 0
        for e in self.E.values():
            e.sem = self._new_sem()
        for qn in ("sp", "act", "pool"):
            self.E[qn].slots = [self._new_sem() for _ in range(N_DMA_SLOTS)]

    def _new_sem(self):
        h = self.nc.semaphore(f"s{self.nsem}").__enter__()
        self.nsem += 1
        return _Sem(h)

    def _buf(self, ap):
        key = ap.tensor.name
        b = self.bufs.get(key)
        if b is None:
            b = self.bufs[key] = _Buf()
        return b

    def _need(self, eng, deps, sem, val, same_ok):
        if sem is eng.sem and same_ok:
            return
        if eng.known.get(sem, 0) >= val:
            return
        deps[sem] = max(deps.get(sem, 0), val)

    def _deps(self, eng, reads, writes, same_ok=True):
        deps = {}
        for ap in reads:
            b = self._buf(ap)
            if b.w is not None:
                self._need(eng, deps, b.w[0], b.w[1], False)
        for ap in writes:
            b = self._buf(ap)
            if b.w is not None:
                self._need(eng, deps, b.w[0], b.w[1], same_ok)
            for s, v in b.r.items():
                self._need(eng, deps, s, v, same_ok)
        for s, v in deps.items():
            eng.prog.append(("wait", s, v))
            eng.known[s] = v

    def op(self, en, fn, reads=(), writes=()):
        eng = self.E[en]
        self._deps(eng, reads, writes)
        if eng.sem.val >= SEM_EPOCH:
            eng.sem = self._new_sem()
        s = eng.sem
        s.val += 1
        eng.prog.append(("inst", fn, s, 1))
        for ap in reads:
            self._buf(ap).r[s] = s.val
        for ap in writes:
            b = self._buf(ap)
            b.w = (s, s.val)
            b.r = {}

    def dma(self, qn, out, in_, **kw):
        eng = self.E[qn]
        slot = eng.slots[eng.slot_i % N_DMA_SLOTS]
        eng.slot_i += 1
        if slot.val > 0 and eng.known.get(slot, 0) < slot.val:
            eng.prog.append(("wait", slot, slot.val))
            eng.known[slot] = slot.val
        self._deps(eng, [in_], [out], same_ok=False)
        slot.val += 16
        v = slot.val

        def fn(e, out=out, in_=in_, kw=kw):
            return e.dma_start(out=out, in_=in_, **kw)
        eng.prog.append(("inst", fn, slot, 16))
        self._buf(in_).r[slot] = v
        b = self._buf(out)
        b.w = (slot, v)
        b.r = {}

    def finish(self):
        sp = self.E["sp"]
        for e in self.E.values():
            for s in e.slots:
                if s.val > 0 and sp.known.get(s, 0) < s.val:
                    sp.prog.append(("wait", s, s.val))
                    sp.known[s] = s.val
        for e in self.E.values():
            if e is not sp and e.sem.val > 0:
                sp.prog.append(("wait", e.sem, e.sem.val))

    def emit(self):
        E = self.E

        def replay(eng, h):
            for it in eng.prog:
                if it[0] == "wait":
                    h.wait_ge(it[1].h, it[2])
                else:
                    it[1](h).then_inc(it[2].h, it[3])

        with self.nc.Block() as block:
            @block.tensor
            def _(h):
                replay(E["pe"], h)

            @block.vector
            def _(h):
                replay(E["dve"], h)

            @block.scalar
            def _(h):
                replay(E["act"], h)

            @block.gpsimd
            def _(h):
                replay(E["pool"], h)

            @block.sync
            def _(h):
                replay(E["sp"], h)

    def stats(self):
        return ({n: sum(1 for i in e.prog if i[0] == "inst") for n, e in self.E.items()},
                {n: sum(1 for i in e.prog if i[0] == "wait") for n, e in self.E.items()}, self.nsem)


def _pt_layout():
    cols = {}
    n = 0

    def add(name, k):
        nonlocal n
        cols[name] = n
        n += k
    add("norm_w", 8)
    add("mu", 26); add("w0", 8); add("a0", 8); add("k_k", 8); add("k_a", 8); add("r_k", 8)
    add("gn_w", 8); add("gn_b", 8)
    add("sconv_w", 32); add("sconv_b", 8); add("snorm_w", 4); add("dt_bias", 1); add("a_log", 1)
    add("ssm_d", 4)
    add("gconv_w", 48); add("gdt_bias", 1); add("ga_log", 1); add("gnorm_w", 1)
    return cols, n


PTC, NPC = _pt_layout()


def _build_pt(inp, l):
    pt = np.zeros((128, NPC), np.float32)

    def put128(name, v):
        m = v.size // 128
        pt[:, PTC[name]:PTC[name] + m] = v.reshape(m, 128).T

    def put64(name, v):
        m = v.size // 64
        pt[0:64, PTC[name]:PTC[name] + m] = v.reshape(m, 64).T

    put128("norm_w", inp["norm_w"][l])
    put64("mu", inp["rw_mu"][l]); put64("w0", inp["rw_w0"][l]); put64("a0", inp["rw_a0"][l])
    put64("k_k", inp["rw_k_k"][l]); put64("k_a", inp["rw_k_a"][l]); put64("r_k", inp["rw_r_k"][l].reshape(-1))
    put64("gn_w", inp["rw_gn_w"][l]); put64("gn_b", inp["rw_gn_b"][l])
    put128("sconv_w", inp["ssm_conv_w"][l].reshape(-1))
    put128("sconv_b", inp["ssm_conv_b"][l]); put128("snorm_w", inp["ssm_norm_w"][l])
    pt[0:8, PTC["dt_bias"]] = inp["ssm_dt_bias"][l]
    pt[0:8, PTC["a_log"]] = inp["ssm_a_log"][l]
    put128("ssm_d", np.repeat(inp["ssm_d"][l], 64))
    put128("gconv_w", inp["gdn_conv_w"][l].reshape(-1))
    pt[0:4, PTC["gdt_bias"]] = inp["gdn_dt_bias"][l]
    pt[0:4, PTC["ga_log"]] = inp["gdn_a_log"][l]
    pt[:, PTC["gnorm_w"]] = inp["gdn_norm_w"][l]
    return pt


def build(depth=DEPTH, branches=("rw", "ssm", "gdn"), debug=False, with_samples=True):
    nc = bass.Bass("TRN2", target_bir_lowering=False)
    S = Sched(nc)

    def din(name, shape):
        return nc.dram_tensor(name, list(shape), F32, kind="ExternalInput").ap()

    def dout(name, shape):
        return nc.dram_tensor(name, list(shape), F32, kind="ExternalOutput").ap()

    def dscr(name, shape):
        return nc.dram_tensor(name, list(shape), F32).ap()

    xin = din("xin", [TT, D])
    w_in = din("w_in", [DEPTH, D, DIN])
    w_rwo = din("w_rw_out", [DEPTH, 512, D])
    w_sso = din("w_ssm_out", [DEPTH, 512, D])
    w_gdo = din("w_gdn_out", [DEPTH, 512, D])
    w_out = din("w_out", [DEPTH, D, D])
    w2 = din("rw_w2", [DEPTH, 64, 512])
    a2 = din("rw_a2", [DEPTH, 64, 512])
    ptab = din("ptab", [DEPTH, 128, NPC])
    fnw = din("final_norm_w", [1, D])
    st_wkv = din("st_wkv", [DEPTH, NS, 8, 64, 64])
    st_shift = din("st_shift", [DEPTH, NS, 1664])
    st_ssm = din("st_ssm", [DEPTH, NS, 8, 64, 128])
    st_sconv = din("st_sconv", [DEPTH, NS, 3, 1024])
    st_gdn = din("st_gdn", [DEPTH, NS, 4, 128, 128])
    st_gconv = din("st_gconv", [DEPTH, NS, 3, 1536])

    y_out = dout("y_out", [TT, D])
    p_wkv = dout("p_wkv", [DEPTH, 8, 64, 64])
    p_shift = dout("p_shift", [DEPTH, 1664])
    p_ssm = dout("p_ssm", [DEPTH, 8, 64, 128])
    p_sconv = dout("p_sconv", [DEPTH, 3, 1024])
    p_gdn = dout("p_gdn", [DEPTH, 4, 128, 128])
    p_gconv = dout("p_gconv", [DEPTH, 3, 1536])
    s_wkv = dout("s_wkv", [DEPTH, NS, 8, 64, 64])
    s_shift = dout("s_shift", [DEPTH, NS, 1664])
    s_ssm = dout("s_ssm", [DEPTH, NS, 8, 64, 128])
    s_sconv = dout("s_sconv", [DEPTH, NS, 3, 1024])
    s_gdn = dout("s_gdn", [DEPTH, NS, 4, 128, 128])
    s_gconv = dout("s_gconv", [DEPTH, NS, 3, 1536])
    dbg = dout("dbg", [3, 512, TT]) if debug else None

    xs = [dscr("xs0", [TT, D]), dscr("xs1", [TT, D])]
    projT = dscr("projT", [DIN, TT])
    yb = [dscr(f"yb{i}", [512, TT]) for i in range(3)]

    _n = [0]

    def sb(shape, dt=F32, name=None):
        _n[0] += 1
        return nc.alloc_sbuf_tensor(name or f"t{_n[0]}", list(shape), dt).ap()

    PS = [nc.alloc_psum_tensor(f"ps{i}", [128, 512], F32).ap() for i in range(8)]
    _pi = [0]

    def P():
        _pi[0] += 1
        return PS[_pi[0] % 8]

    def mm(out, lhsT, rhs, start=True, stop=True):
        S.op("pe", lambda e: e.matmul(out, lhsT=lhsT, rhs=rhs, start=start, stop=stop),
             reads=[lhsT, rhs], writes=[out])

    def tr(out, in_, n):
        S.op("pe", lambda e: e.transpose(out, in_, ident[0:n, 0:n]), reads=[in_, ident], writes=[out])

    def act(out, in_, func, bias=None, scale=1.0, accum=None, eng="act"):
        rd = [in_]
        kw = {}
        if bias is not None:
            kw["bias"] = bias
            rd.append(bias)
        if not isinstance(scale, float):
            rd.append(scale)
        if accum is not None:
            kw["accum_out"] = accum
        wr = [out] + ([accum] if accum is not None else [])
        S.op("act", lambda e: e.activation(out=out, in_=in_, func=func, scale=scale, **kw), reads=rd, writes=wr)

    def tt(out, a, b, op, eng="dve"):
        S.op(eng, lambda e: e.tensor_tensor(out=out, in0=a, in1=b, op=op), reads=[a, b], writes=[out])

    def ts(out, a, s1, op0, s2=None, op1=None, eng="dve"):
        rd = [a] + [s for s in (s1, s2) if s is not None and not isinstance(s, float)]
        if op1 is None:
            S.op(eng, lambda e: e.tensor_scalar(out=out, in0=a, scalar1=s1, scalar2=None, op0=op0), reads=rd, writes=[out])
        else:
            S.op(eng, lambda e: e.tensor_scalar(out=out, in0=a, scalar1=s1, scalar2=s2, op0=op0, op1=op1), reads=rd, writes=[out])

    def stt(out, a, s, b, op0, op1, eng="dve"):
        rd = [a, b] + ([] if isinstance(s, float) else [s])
        S.op(eng, lambda e: e.scalar_tensor_tensor(out=out, in0=a, scalar=s, in1=b, op0=op0, op1=op1), reads=rd, writes=[out])

    def cp(out, in_, eng="dve"):
        if eng == "act":
            S.op("act", lambda e: e.copy(out=out, in_=in_), reads=[in_], writes=[out])
        else:
            S.op(eng, lambda e: e.tensor_copy(out=out, in_=in_), reads=[in_], writes=[out])

    def mset(t, v, eng="pool"):
        S.op(eng, lambda e: e.memset(t, v), writes=[t])

    def asel(t, pattern, cmp, fill, base, cm):
        S.op("pool", lambda e: e.affine_select(out=t, in_=t, pattern=pattern, compare_op=cmp, fill=fill,
                                               base=base, channel_multiplier=cm), reads=[t], writes=[t])

    def bc(ap, shape):
        return ap.to_broadcast(list(shape))

    ident = sb([128, 128], name="ident")
    mset(ident, 0.0)
    asel(ident, [[-1, 128]], ALU.not_equal, 1.0, 0, 1)
    ones = sb([128, 128], name="ones")
    mset(ones, 1.0)
    triu = sb([64, 64], name="triu")
    mset(triu, 1.0)
    asel(triu, [[1, 64]], ALU.is_ge, 0.0, 0, -1)
    trius = sb([64, 64], name="trius")
    mset(trius, 1.0)
    asel(trius, [[1, 64]], ALU.is_gt, 0.0, 0, -1)
    trils = sb([64, 64], name="trils")
    mset(trils, 1.0)
    asel(trils, [[-1, 64]], ALU.is_gt, 0.0, 0, 1)
    negU = sb([64, 64], name="negU")
    mset(negU, 0.0)
    asel(negU, [[1, 64]], ALU.is_ge, -30000.0, 0, -1)
    posLs = sb([64, 64], name="posLs")
    mset(posLs, 0.0)
    asel(posLs, [[-1, 64]], ALU.is_gt, 30000.0, 0, 1)
    epsc = sb([128, 4], name="epsc")
    mset(epsc[:, 0:1], 1e-6)
    mset(epsc[:, 1:2], 1.0)
    mset(epsc[:, 2:3], 1e-5)
    mset(epsc[:, 3:4], 64e-5)
    negU8 = sb([64, 8, 64], name="negU8")
    cp(negU8, bc(negU.unsqueeze(1), [64, 8, 64]), eng="pool")
    posL4 = sb([64, 4, 64], name="posL4")
    cp(posL4, bc(posLs.unsqueeze(1), [64, 4, 64]), eng="pool")

    hT_all = sb([128, 8, TT], BF16, name="hT_all")
    WA = sb([128, 8, 1024], BF16, name="WA")
    WB = sb([128, 8, 1024], BF16, name="WB")
    Wo = hT_all[:, :, 0:1024]
    ysb = [sb([128, 4, 512], BF16, name=f"ysb{i}") for i in range(3)]
    mT = sb([128, 8, 512], BF16, name="mT")
    dtraw = sb([8, 64], name="dtraw")
    garaw = sb([4, 64], name="garaw")
    gbraw = sb([4, 64], name="gbraw")
    pt = sb([128, NPC], name="pt")
    xt = sb([128, D], name="xt")
    xr = sb([128, D], name="xr")
    junk = xr
    stat = sb([128, 4], name="stat")
    NG = 32
    G = [sb([128, 512], name=f"g{i}") for i in range(NG)]
    B0 = sb([128, 1690], name="B0"); B1 = sb([128, 1664], name="B1"); B2 = sb([128, 1664], name="B2"); B3 = sb([128, 768], name="B3")

    fnw_bc = B1[:, 0:D]

    def g8(i, rows=128, b=64):
        return G[i][0:rows, :].rearrange("p (a b) -> p a b", b=b)

    def g4(i, rows=128, b=64):
        return G[i][0:rows, 0:4 * b].rearrange("p (a b) -> p a b", b=b)

    def pc(name, k=0, rows=128):
        c0 = PTC[name] + k
        return pt[0:rows, c0:c0 + 1]

    def pcs(name, k, rows=128):
        c0 = PTC[name]
        return pt[0:rows, c0:c0 + k]

    ttiles = [(i * 128, 128) for i in range(16)] + [(2048, 32)]
    chunks = [(i * 64, 64) for i in range(32)] + [(2048, 16)]

    def norm_to_hT(t0, n, l):
        act(junk[0:n], xt[0:n], AF.Square, accum=stat[0:n, 0:1])
        act(stat[0:n, 1:2], stat[0:n, 0:1], AF.Ln, bias=epsc[0:n, 0:1], scale=1.0 / D)
        act(stat[0:n, 2:3], stat[0:n, 1:2], AF.Exp, scale=-0.5)
        ts(xr[0:n], xt[0:n], stat[0:n, 2:3], ALU.mult)
        for half in range(2):
            p = P()
            pv = p.rearrange("p (a b) -> p a b", b=128)
            for j in range(4):
                k = half * 4 + j
                tr(pv[:, j, 0:n], xr[0:n, k * 128:(k + 1) * 128], n)
            tt(hT_all[:, half * 4:half * 4 + 4, t0:t0 + n], pv[:, :, 0:n],
               bc(pcs("norm_w", 8)[:, half * 4:half * 4 + 4].unsqueeze(2), [128, 4, n]), ALU.mult)

    def final_norm(t0, n):
        act(junk[0:n], xt[0:n], AF.Square, accum=stat[0:n, 0:1])
        act(stat[0:n, 1:2], stat[0:n, 0:1], AF.Ln, bias=epsc[0:n, 0:1], scale=1.0 / D)
        act(stat[0:n, 2:3], stat[0:n, 1:2], AF.Exp, scale=-0.5)
        stt(xr[0:n], xt[0:n], stat[0:n, 2:3], fnw_bc[0:n], ALU.mult, ALU.mult)
        S.dma("sp", y_out[t0:t0 + n, :], xr[0:n])

    cbs = B0[:, 0:536].rearrange("p (a b) -> p a b", b=67)
    cacc = g8(0); ctmp = g8(1); xsT = g8(2)
    dtT = sb([8, 2, 64], name="dtT")
    acol = sb([8, 2], name="acol")
    tok8 = sb([64, 16], name="tok8")
    acs = sb([64, 40], name="acs")
    totb = sb([128, 8], name="totb")
    Rt = g8(3, 64); segT = g8(4, 64); xTok = g8(5, 64); xdt = g8(6, 64); xdtd = g8(7, 64)
    BTok = G[8][0:64, 0:256]
    MT = g8(9, 64)
    y1s = G[10][0:64, :]
    ytok = g8(11, 64)
    hst = g8(12)
    zT = g4(13); yT = g4(14); sq = g4(15)
    rs = G[16][:, 0:128].rearrange("p (a b) -> p a b", b=64)
    ybf = sb([128, 4, 64], BF16, name="ybf")
    cur_src = [xin]

    supers = [(0, 512), (512, 512), (1024, 512), (1536, 512), (2048, 32)]
    _stg = [0]

    def phase1(l):
        src = cur_src[0]
        for (t0, n) in ttiles:
            S.dma("sp", xt[0:n], src[t0:t0 + n, :])
            norm_to_hT(t0, n, l)
        gi = 0
        for c0 in range(0, DIN, 1024):
            ncol = min(1024, DIN - c0)
            W = WA if gi % 2 == 0 else WB
            gi += 1
            S.dma("pool", W[:, :, 0:ncol], w_in[l, :, c0:c0 + ncol].rearrange("(k p) c -> p k c", p=128))
            for (s0, sn) in supers:
                for j0 in range(0, ncol, 128):
                    w = min(128, ncol - j0)
                    p = P()
                    for k in range(8):
                        mm(p[0:w, 0:sn], W[:, k, j0:j0 + w], hT_all[:, k, s0:s0 + sn], start=(k == 0), stop=(k == 7))
                    _stg[0] += 1
                    stg = G[_stg[0] % 8]
                    cp(stg[0:w, 0:sn], p[0:w, 0:sn], eng=("act" if _stg[0] % 2 else "dve"))
                    S.dma("sp", projT[c0 + j0:c0 + j0 + w, s0:s0 + sn], stg[0:w, 0:sn])

    def phase3(l):
        wsrc = [w_rwo, w_sso, w_gdo]
        Wb = [WA[:, 0:4, :], WA[:, 4:8, :], WB[:, 0:4, :]]
        for b in range(3):
            S.dma("pool", Wb[b], wsrc[b][l].rearrange("(k p) c -> p k c", p=128))
        S.dma("pool", Wo, w_out[l].rearrange("(k p) c -> p k c", p=128))
        if l == depth - 1:
            S.dma("sp", fnw_bc, fnw.partition_broadcast(128))
        src = cur_src[0]
        dst = xs[(l + 1) % 2]
        gq = 0
        for (s0, sn) in supers:
            for b in range(3):
                S.dma("pool", ysb[b][:, :, 0:sn], yb[b][:, s0:s0 + sn].rearrange("(k p) t -> p k t", p=128))
            for cc in range(8):
                macc = G[8 + (cc % 2)]
                for b in range(3):
                    gq += 1
                    gt = G[10 + (gq % 4)]
                    r0 = C_GATE + b * 1024 + cc * 128
                    S.dma("sp", gt[:, 0:sn], projT[r0:r0 + 128, s0:s0 + sn])
                    act(gt[:, 0:sn], gt[:, 0:sn], AF.Sigmoid)
                    po = P()
                    for k in range(4):
                        mm(po[:, 0:sn], Wb[b][:, k, cc * 128:(cc + 1) * 128], ysb[b][:, k, 0:sn], start=(k == 0), stop=(k == 3))
                    if b == 0:
                        tt(macc[:, 0:sn], po[:, 0:sn], gt[:, 0:sn], ALU.mult)
                    else:
                        tt(gt[:, 0:sn], po[:, 0:sn], gt[:, 0:sn], ALU.mult)
                        if b == 1:
                            tt(macc[:, 0:sn], macc[:, 0:sn], gt[:, 0:sn], ALU.add)
                        else:
                            tt(mT[:, cc, 0:sn], macc[:, 0:sn], gt[:, 0:sn], ALU.add)
            for (t0, n) in ttiles:
                if not (s0 <= t0 < s0 + sn):
                    continue
                o_ = t0 - s0
                S.dma("sp", xt[0:n], src[t0:t0 + n, :])
                for half in range(2):
                    p = P()
                    for k in range(8):
                        mm(p[0:n, :], mT[:, k, o_:o_ + n], Wo[:, k, half * 512:(half + 1) * 512], start=(k == 0), stop=(k == 7))
                    tt(xt[0:n, half * 512:(half + 1) * 512], xt[0:n, half * 512:(half + 1) * 512], p[0:n, :], ALU.add)
                if l == depth - 1:
                    final_norm(t0, n)
                else:
                    S.dma("sp", dst[t0:t0 + n, :], xt[0:n])

    pf = {"ssd": False, "gdn": False, "rw": False}

    def nxt_of(ci):
        if ci + 1 < len(chunks):
            return chunks[ci + 1]
        return (TP, NS) if with_samples else None

    def ssd_pre(l, t0, c, part=None):
        part = part or ("z" if pf["ssd"] else "all")
        if part in ("all", "in"):
            S.dma("sp", cbs[:, :, 3:3 + c], projT[C_SSM + 512:C_SSM + 1536, t0:t0 + c].rearrange("(k p) t -> p k t", p=128))
            S.dma("sp", dtraw[:, 0:c], projT[C_SSM + 1536:C_SSM + 1544, t0:t0 + c])
        if part in ("all", "z"):
            S.dma("sp", zT[:, :, 0:c], projT[C_SSM:C_SSM + 512, t0:t0 + c].rearrange("(k p) t -> p k t", p=128))
            pf["ssd"] = False
        if part == "in":
            pf["ssd"] = True
        dt_ps[0] = dtraw

    dt_ps = [None]

    def ssd_dt(c):
        p = dt_ps[0]
        act(dtT[:, 0, 0:c], p[0:8, 0:c], AF.Exp, bias=pc("dt_bias", 0, 8))
        act(dtT[:, 0, 0:c], dtT[:, 0, 0:c], AF.Ln, bias=epsc[0:8, 1:2])
        ts(dtT[:, 1, 0:c], dtT[:, 0, 0:c], acol[:, 1:2], ALU.mult)

    def conv_silu(cb, w_name, nk, c, accv, tmpv, outv, bias_name=None):
        W4 = pcs(w_name, 4 * nk)
        for i in range(4):
            wv = bc(W4[:, i * nk:(i + 1) * nk].unsqueeze(2), [128, nk, c])
            if i == 0:
                tt(accv, cb[:, :, 0:c], wv, ALU.mult)
            else:
                tt(tmpv, cb[:, :, i:i + c], wv, ALU.mult, eng="pool")
                tt(accv, accv, tmpv, ALU.add)
        if bias_name is not None:
            tt(accv, accv, bc(pcs(bias_name, nk).unsqueeze(2), [128, nk, c]), ALU.add)
        act(outv, accv, AF.Silu)

    def ssd_chunk(l, t0, c, first, nxt=None):
        ssd_pre(l, t0, c)
        conv_silu(cbs, "sconv_w", 8, c, cacc[:, :, 0:c], ctmp[:, :, 0:c], xsT[:, :, 0:c], "sconv_b")
        ssd_dt(c)
        cp(ctmp[:, :, 0:3], cbs[:, :, c:c + 3], eng="pool")
        cp(cbs[:, :, 0:3], ctmp[:, :, 0:3], eng="pool")
        if nxt is not None:
            ssd_pre(l, nxt[0], nxt[1], "in")
        ssd_core(c, 0, first)

    def ssd_core(c, o, first):
        p = P()
        tr(p[0:c, 0:8], dtT[:, 0, o:o + c], 8)
        tr(p[0:c, 8:16], dtT[:, 1, o:o + c], 8)
        cp(tok8[0:c], p[0:c, 0:16])
        p = P()
        pv = p.rearrange("p (a b) -> p a b", b=128)
        for j in range(4):
            tr(pv[0:c, j, :], xsT[:, j, o:o + c], 128)
        cp(xTok[0:c].rearrange("p a b -> p (a b)"), p[0:c, :], eng="act")
        p = P()
        for j in range(2):
            tr(p[0:c, j * 128:(j + 1) * 128], xsT[:, 4 + j, o:o + c], 128)
        cp(BTok[0:c], p[0:c, 0:256], eng="act")
        p = P()
        mm(p[0:c, 0:8], triu[0:c, 0:c], tok8[0:c, 8:16])
        cp(acs[0:c, 0:8], p[0:c, 0:8])
        ts(acs[0:c, 8:16], p[0:c, 0:8], -1.0, ALU.mult)
        act(acs[0:c, 16:24], p[0:c, 0:8], AF.Exp)
        tt(Rt[0:c, :, 0:c], bc(tok8[0:c, 8:16].unsqueeze(2), [c, 8, c]),
           bc(triu[0:c, 0:c].unsqueeze(1), [c, 8, c]), ALU.mult)
        pa = P()
        pav = pa.rearrange("p (a b) -> p a b", b=64)
        if c == 64:
            mm(pa[:, :], ones[0:c, :], Rt[0:c].rearrange("p a b -> p (a b)"), start=True, stop=False)
            mm(pa[0:64, :], ident[0:64, 0:64], negU8.rearrange("p a b -> p (a b)"), start=False, stop=True)
        else:
            for h in range(8):
                mm(pav[:, h, 0:c], ones[0:c, :], Rt[0:c, h, 0:c], start=True, stop=False)
                mm(pav[0:c, h, 0:c], ident[0:c, 0:c], negU[0:c, 0:c], start=False, stop=True)
        tt(segT[0:c, :, 0:c], pav[0:c, :, 0:c], bc(acs[0:c, 8:16].unsqueeze(2), [c, 8, c]), ALU.add)
        act(segT[0:c, :, 0:c], segT[0:c, :, 0:c], AF.Exp)
        act(totb[:, :], pav[:, :, c - 1], AF.Exp)
        tt(acs[0:c, 24:32], pav[0:c, :, c - 1], acs[0:c, 8:16], ALU.add)
        act(acs[0:c, 24:32], acs[0:c, 24:32], AF.Exp)
        tt(xdt[0:c], xTok[0:c], bc(tok8[0:c, 0:8].unsqueeze(2), [c, 8, 64]), ALU.mult)
        tt(xdtd[0:c], xdt[0:c], bc(acs[0:c, 24:32].unsqueeze(2), [c, 8, 64]), ALU.mult, eng="pool")
        pc_ = P()
        pcv = pc_.rearrange("p (a b) -> p a b", b=64)
        for g in range(2):
            mm(pcv[0:c, g, 0:c], xsT[:, 4 + g, o:o + c], xsT[:, 6 + g, o:o + c])
        for g in range(2):
            tt(MT[0:c, g * 4:(g + 1) * 4, 0:c], segT[0:c, g * 4:(g + 1) * 4, 0:c],
               bc(pcv[0:c, g:g + 1, 0:c], [c, 4, c]), ALU.mult)
        p1 = P()
        p1v = p1.rearrange("p (a b) -> p a b", b=64)
        for h in range(8):
            mm(p1v[0:c, h, :], MT[0:c, h, 0:c], xdt[0:c, h, :])
        cp(y1s[0:c], p1[0:c, :], eng="act")
        p2 = P()
        if not first:
            for g in range(2):
                mm(p2[0:c, g * 256:(g + 1) * 256], xsT[:, 6 + g, o:o + c],
                   hst[:, g * 4:(g + 1) * 4, :].rearrange("p a b -> p (a b)"))
            tt(ytok[0:c], p2[0:c, :].rearrange("p (a b) -> p a b", b=64),
               bc(acs[0:c, 16:24].unsqueeze(2), [c, 8, 64]), ALU.mult)
            tt(ytok[0:c], ytok[0:c], y1s[0:c].rearrange("p (a b) -> p a b", b=64), ALU.add)
            ysrc = ytok
        else:
            ysrc = y1s.rearrange("p (a b) -> p a b", b=64)
        p3 = P()
        for g in range(2):
            mm(p3[:, g * 256:(g + 1) * 256], BTok[0:c, g * 128:(g + 1) * 128],
               xdtd[0:c, g * 4:(g + 1) * 4, :].rearrange("p a b -> p (a b)"))
        if first:
            cp(hst.rearrange("p a b -> p (a b)"), p3[:, :])
        else:
            tt(hst, hst, bc(totb.unsqueeze(2), [128, 8, 64]), ALU.mult)
            tt(hst.rearrange("p a b -> p (a b)"), hst.rearrange("p a b -> p (a b)"), p3[:, :], ALU.add)
        p4 = P()
        p4v = p4.rearrange("p (a b) -> p a b", b=64)
        for j in range(4):
            tr(p4v[:, j, 0:c], ysrc[0:c, 2 * j:2 * j + 2, :].rearrange("p a b -> p (a b)"), c)
        cp(yT[:, :, o:o + c], p4v[:, 0:4, 0:c], eng="act")

    def ssd_post(l, t0, c, bi):
        act(zT[:, :, 0:c], zT[:, :, 0:c], AF.Silu)
        tt(sq[:, :, 0:c], xsT[:, 0:4, 0:c], bc(pcs("ssm_d", 4).unsqueeze(2), [128, 4, c]), ALU.mult)
        tt(yT[:, :, 0:c], yT[:, :, 0:c], sq[:, :, 0:c], ALU.add)
        tt(yT[:, :, 0:c], yT[:, :, 0:c], zT[:, :, 0:c], ALU.mult)
        tt(sq[:, :, 0:c], yT[:, :, 0:c], yT[:, :, 0:c], ALU.mult)
        p = P()
        pv = p.rearrange("p (a b) -> p a b", b=64)
        for g in range(2):
            mm(pv[:, g, 0:c], ones[:, :], sq[:, 2 * g, 0:c], start=True, stop=False)
            mm(pv[:, g, 0:c], ones[:, :], sq[:, 2 * g + 1, 0:c], start=False, stop=True)
        act(rs[:, :, 0:c], pv[:, 0:2, 0:c], AF.Ln, bias=epsc[:, 2:3], scale=1.0 / 256)
        act(rs[:, :, 0:c], rs[:, :, 0:c], AF.Exp, scale=-0.5)
        for g in range(2):
            tt(yT[:, 2 * g:2 * g + 2, 0:c], yT[:, 2 * g:2 * g + 2, 0:c], bc(rs[:, g:g + 1, 0:c], [128, 2, c]), ALU.mult)
        tt(yT[:, :, 0:c], yT[:, :, 0:c], bc(pcs("snorm_w", 4).unsqueeze(2), [128, 4, c]), ALU.mult)
        S.dma("sp", yb[bi][:, t0:t0 + c].rearrange("(k p) t -> p k t", p=128), yT[:, :, 0:c])

    def ssd_branch(l, bi):
        act(acol[:, 0:1], pc("a_log", 0, 8), AF.Exp)
        ts(acol[:, 1:2], acol[:, 0:1], -1.0, ALU.mult)
        mset(cbs[:, :, 0:3], 0.0)
        for ci, (t0, c) in enumerate(chunks):
            ssd_chunk(l, t0, c, ci == 0, nxt_of(ci))
            ssd_post(l, t0, c, bi)
        ssd_state_out(p_ssm[l], p_sconv[l])
        if with_samples:
            c = NS
            ssd_pre(l, TP, c)
            S.dma("sp", B1[0:48, 0:1024], st_sconv[l].rearrange("b i c -> (b i) c"))
            p = P()
            for k in range(8):
                tr(p[:, k * 48:(k + 1) * 48], B1[0:48, k * 128:(k + 1) * 128], 48)
            sbT = B2[:, 0:384].rearrange("p (k b i) -> p k b i", k=8, i=3)
            cp(B2[:, 0:384], p[:, 0:384])
            W4 = pcs("sconv_w", 32)
            for i in range(4):
                src = sbT[:, :, :, i] if i < 3 else cbs[:, :, 3:3 + c]
                wv = bc(W4[:, i * 8:(i + 1) * 8].unsqueeze(2), [128, 8, c])
                if i == 0:
                    tt(cacc[:, :, 0:c], src, wv, ALU.mult)
                else:
                    tt(ctmp[:, :, 0:c], src, wv, ALU.mult, eng="pool")
                    tt(cacc[:, :, 0:c], cacc[:, :, 0:c], ctmp[:, :, 0:c], ALU.add)
            tt(cacc[:, :, 0:c], cacc[:, :, 0:c], bc(pcs("sconv_b", 8).unsqueeze(2), [128, 8, c]), ALU.add)
            act(xsT[:, :, 0:c], cacc[:, :, 0:c], AF.Silu)
            ssd_dt(c)
            S.dma("sp", s_sconv[l, :, 0:2, :], st_sconv[l, :, 1:3, :])
            for half in range(2):
                p = P()
                for j in range(4):
                    tr(p[0:c, j * 128:(j + 1) * 128], cbs[:, half * 4 + j, 3:3 + c], 128)
                cp(B1[0:c, half * 512:(half + 1) * 512], p[0:c, :], eng="act")
            S.dma("sp", s_sconv[l, :, 2, :], B1[0:c, 0:1024])
            stg = g8(17, 64, 128)
            stg2 = g8(18, 64, 128)

            def ld_state(b):
                S.dma("sp", stg[:, 0:4, :], st_ssm[l, b, 0:4].rearrange("h p n -> p h n"))
                S.dma("sp", stg2[:, 0:4, :], st_ssm[l, b, 4:8].rearrange("h p n -> p h n"))
            ld_state(0)
            for b in range(NS):
                p = P()
                pv = p.rearrange("p (a b) -> p a b", b=64)
                for h in range(4):
                    tr(pv[:, h, :], stg[:, h, :], 64)
                    tr(pv[:, 4 + h, :], stg2[:, h, :], 64)
                cp(hst, pv)
                if b + 1 < NS:
                    ld_state(b + 1)
                ssd_core(1, b, False)
                ssd_state_out(s_ssm[l, b], None, b)
            ssd_post(l, TP, c, bi)

    def ssd_state_out(dst_st, dst_conv, par=0):
        for half in range(2):
            p = P()
            for h in range(4):
                tr(p[0:64, h * 128:(h + 1) * 128], hst[:, half * 4 + h, :], 128)
            so = G[21 + 2 * (par % 2) + half][0:64, :]
            cp(so, p[0:64, :], eng=("act" if half else "dve"))
            S.dma("sp", dst_st[half * 4:(half + 1) * 4].rearrange("h p n -> p h n"), so.rearrange("p (h n) -> p h n", n=128))
        if dst_conv is None:
            return
        for k in range(8):
            p = P()
            tr(p[0:3, 0:128], cbs[:, k, 0:3], 128)
            cp(y1s[0:3, 0:128], p[0:3, 0:128])
            S.dma("sp", dst_conv[:, k * 128:(k + 1) * 128], y1s[0:3, 0:128])

    def nsteps_for(c):
        n = 0
        while (1 << (n + 1)) < c:
            n += 1
        return n

    def cs_bc(R, nh, c, mask_nh, mask2d):
        p = P()
        pv = p.rearrange("p (a b) -> p a b", b=64)
        if c == 64:
            mm(p[:, 0:nh * 64], ones[0:c, :], R[0:c].rearrange("p a b -> p (a b)"), start=True, stop=False)
            mm(p[0:64, 0:nh * 64], ident[0:64, 0:64], mask_nh.rearrange("p a b -> p (a b)"), start=False, stop=True)
        else:
            for h in range(nh):
                mm(pv[:, h, 0:c], ones[0:c, :], R[0:c, h, 0:c], start=True, stop=False)
                mm(pv[0:c, h, 0:c], ident[0:c, 0:c], mask2d[0:c, 0:c], start=False, stop=True)
        return pv

    def solve(X, XT, Xb, XTb, Sm, nh, c):
        tt(Sm[0:c, :, 0:c], XT[0:c, :, 0:c], bc(ident[0:c, 0:c].unsqueeze(1), [c, nh, c]), ALU.add)
        ns = nsteps_for(c)
        cur, curT, nxt, nxtT = X, XT, Xb, XTb
        for s_ in range(ns):
            last = s_ == ns - 1
            pX = P()
            pXv = pX.rearrange("p (a b) -> p a b", b=64)
            for h in range(nh):
                mm(pXv[0:c, h, 0:c], curT[0:c, h, 0:c], cur[0:c, h, 0:c])
            cp(nxt[0:c, :, 0:c], pXv[0:c, 0:nh, 0:c], eng="act")
            if not last:
                pT = P()
                pTv = pT.rearrange("p (a b) -> p a b", b=64)
                for h in range(nh):
                    mm(pTv[0:c, h, 0:c], cur[0:c, h, 0:c], curT[0:c, h, 0:c])
                cp(nxtT[0:c, :, 0:c], pTv[0:c, 0:nh, 0:c])
            pS = P()
            pSv = pS.rearrange("p (a b) -> p a b", b=64)
            for h in range(nh):
                mm(pSv[0:c, h, 0:c], nxt[0:c, h, 0:c], Sm[0:c, h, 0:c])
            tt(Sm[0:c, :, 0:c], Sm[0:c, :, 0:c], pSv[0:c, 0:nh, 0:c], ALU.add)
            cur, curT, nxt, nxtT = nxt, nxtT, cur, curT

    XA = g8(21, 64); XTA = g8(22, 64); XB = g8(23, 64); XTB = g8(24, 64); SM = g8(25, 64)

    cbg = B0[:, 0:804].rearrange("p (a b) -> p a b", b=67)
    gacc = B1[:, 0:768].rearrange("p (a b) -> p a b", b=64)
    gtm = B2[:, 0:768].rearrange("p (a b) -> p a b", b=64)
    qkv = B3[:, 0:768].rearrange("p (a b) -> p a b", b=64)
    gbT = sb([4, 2, 64], name="gbT")
    gacol = sb([4, 2], name="gacol")
    gbTok = sb([64, 8], name="gbTok")
    gst = sb([64, 32], name="gst")
    gtotb = sb([128, 4], name="gtotb")
    Rg = g4(0, 64); decT = g4(1, 64); dec2 = g4(2, 64); qkT = g4(3, 64)
    kTok = g8(4, 64, 128); vTok = g8(5, 64, 128); rvk = g8(6, 64, 128); rkk = g8(7, 64, 128); kdk = g8(8, 64, 128)
    usb = g8(9, 64, 128); vnew = g8(11, 64, 128); osb = g8(12, 64, 128); o2s = g8(16, 64, 128)
    wcT = g4(17)
    Sg = g8(18, 128, 128)
    gsq = g8(19); grs = g8(20)

    def gdn_chunk(l, t0, c, first, nxt=None):
        gdn_pre(l, t0, c)
        gdn_chunk2(l, t0, c, first, nxt)

    def gdn_pre(l, t0, c, part=None):
        part = part or ("z" if pf["gdn"] else "all")
        if part in ("all", "in"):
            S.dma("sp", cbg[:, :, 3:3 + c], projT[C_GDN:C_GDN + 1536, t0:t0 + c].rearrange("(k p) t -> p k t", p=128))
            S.dma("sp", garaw[:, 0:c], projT[C_GDN + 2048:C_GDN + 2052, t0:t0 + c])
            S.dma("sp", gbraw[:, 0:c], projT[C_GDN + 2052:C_GDN + 2056, t0:t0 + c])
        if part in ("all", "z"):
            S.dma("sp", zT[:, :, 0:c], projT[C_GDN + 1536:C_GDN + 2048, t0:t0 + c].rearrange("(k p) t -> p k t", p=128))
            pf["gdn"] = False
        if part == "in":
            pf["gdn"] = True
        g_ps[0] = garaw
        g_ps[1] = gbraw

    g_ps = [None, None]

    def gdn_gb(c):
        act(gbT[:, 1, 0:c], g_ps[1][0:4, 0:c], AF.Sigmoid)
        p = g_ps[0]
        act(gbT[:, 0, 0:c], p[0:4, 0:c], AF.Exp, bias=pc("gdt_bias", 0, 4))
        act(gbT[:, 0, 0:c], gbT[:, 0, 0:c], AF.Ln, bias=epsc[0:4, 1:2])
        ts(gbT[:, 0, 0:c], gbT[:, 0, 0:c], gacol[:, 1:2], ALU.mult)

    def gdn_chunk2(l, t0, c, first, nxt=None):
        conv_silu(cbg, "gconv_w", 12, c, gacc[:, :, 0:c], gtm[:, :, 0:c], qkv[:, :, 0:c], None)
        gdn_gb(c)
        cp(gtm[:, :, 0:3], cbg[:, :, c:c + 3], eng="pool")
        cp(cbg[:, :, 0:3], gtm[:, :, 0:3], eng="pool")
        if nxt is not None:
            gdn_pre(l, nxt[0], nxt[1], "in")
        gdn_norm(c)
        gdn_core(c, 0, first)

    def gdn_norm(c):
        tt(gsq[:, :, 0:c], qkv[:, 0:8, 0:c], qkv[:, 0:8, 0:c], ALU.mult)
        p = P()
        pv = p.rearrange("p (a b) -> p a b", b=64)
        for j in range(8):
            mm(pv[:, j, 0:c], ones[:, :], gsq[:, j, 0:c])
        act(grs[:, :, 0:c], pv[:, :, 0:c], AF.Ln, bias=epsc[:, 0:1])
        act(grs[:, :, 0:c], grs[:, :, 0:c], AF.Exp, scale=-0.5)
        tt(qkv[:, 0:8, 0:c], qkv[:, 0:8, 0:c], grs[:, :, 0:c], ALU.mult)
        ts(qkv[:, 0:4, 0:c], qkv[:, 0:4, 0:c], float(128 ** -0.5), ALU.mult)

    def gdn_core(c, o, first):
        p = P()
        pv = p.rearrange("p (a b) -> p a b", b=128)
        for h in range(4):
            tr(pv[0:c, h, :], qkv[:, 4 + h, o:o + c], 128)
        cp(kTok[0:c], pv[0:c], eng="act")
        p = P()
        pv = p.rearrange("p (a b) -> p a b", b=128)
        for h in range(4):
            tr(pv[0:c, h, :], qkv[:, 8 + h, o:o + c], 128)
        cp(vTok[0:c], pv[0:c], eng="act")
        p = P()
        tr(p[0:c, 0:4], gbT[:, 0, o:o + c], 4)
        tr(p[0:c, 4:8], gbT[:, 1, o:o + c], 4)
        cp(gbTok[0:c], p[0:c, 0:8])
        p = P()
        mm(p[0:c, 0:4], triu[0:c, 0:c], gbTok[0:c, 0:4])
        cp(gst[0:c, 0:4], p[0:c, 0:4])
        ts(gst[0:c, 4:8], p[0:c, 0:4], -1.0, ALU.mult)
        act(gst[0:c, 8:12], p[0:c, 0:4], AF.Exp)
        tt(gst[0:c, 16:20], gst[0:c, 8:12], gbTok[0:c, 4:8], ALU.mult)
        ts(gst[0:c, 20:24], gbTok[0:c, 4:8], -1.0, ALU.mult)
        tt(Rg[0:c, :, 0:c], bc(gbTok[0:c, 0:4].unsqueeze(2), [c, 4, c]), bc(triu[0:c, 0:c].unsqueeze(1), [c, 4, c]), ALU.mult)
        pa = cs_bc(Rg, 4, c, negU8[:, 0:4, :], negU)
        tt(decT[0:c, :, 0:c], pa[0:c, 0:4, 0:c], bc(gst[0:c, 4:8].unsqueeze(2), [c, 4, c]), ALU.add)
        act(decT[0:c, :, 0:c], decT[0:c, :, 0:c], AF.Exp)
        act(gtotb[:, :], pa[:, 0:4, c - 1], AF.Exp)
        tt(gst[0:c, 12:16], pa[0:c, 0:4, c - 1], gst[0:c, 4:8], ALU.add)
        act(gst[0:c, 12:16], gst[0:c, 12:16], AF.Exp)
        pb = cs_bc(Rg, 4, c, posL4, posLs)
        tt(dec2[0:c, :, 0:c], pb[0:c, 0:4, 0:c], bc(gst[0:c, 4:8].unsqueeze(2), [c, 4, c]), ALU.add)
        act(dec2[0:c, :, 0:c], dec2[0:c, :, 0:c], AF.Exp, scale=-1.0)
        pG = P()
        pGv = pG.rearrange("p (a b) -> p a b", b=64)
        pQ = P()
        pQv = pQ.rearrange("p (a b) -> p a b", b=64)
        for h in range(4):
            mm(pGv[0:c, h, 0:c], qkv[:, 4 + h, o:o + c], qkv[:, 4 + h, o:o + c])
            mm(pQv[0:c, h, 0:c], qkv[:, 4 + h, o:o + c], qkv[:, h, o:o + c])
        tt(qkT[0:c, :, 0:c], pQv[0:c, 0:4, 0:c], decT[0:c, :, 0:c], ALU.mult)
        tt(XA[0:c, 0:4, 0:c], pGv[0:c, 0:4, 0:c], dec2[0:c, :, 0:c], ALU.mult)
        tt(XA[0:c, 0:4, 0:c], XA[0:c, 0:4, 0:c], bc(gst[0:c, 20:24].unsqueeze(2), [c, 4, c]), ALU.mult)
        p = P()
        pv = p.rearrange("p (a b) -> p a b", b=64)
        for h in range(4):
            tr(pv[0:c, h, 0:c], XA[0:c, h, 0:c], c)
        cp(XTA[0:c, 0:4, 0:c], pv[0:c, 0:4, 0:c])
        solve(XA[:, 0:4], XTA[:, 0:4], XB[:, 0:4], XTB[:, 0:4], SM[:, 0:4], 4, c)
        tt(rvk[0:c], vTok[0:c], bc(gbTok[0:c, 4:8].unsqueeze(2), [c, 4, 128]), ALU.mult)
        tt(rkk[0:c], kTok[0:c], bc(gst[0:c, 16:20].unsqueeze(2), [c, 4, 128]), ALU.mult, eng="pool")
        tt(kdk[0:c], kTok[0:c], bc(gst[0:c, 12:16].unsqueeze(2), [c, 4, 128]), ALU.mult, eng="pool")
        pU = P()
        pUv = pU.rearrange("p (a b) -> p a b", b=128)
        for h in range(4):
            mm(pUv[0:c, h, :], SM[0:c, h, 0:c], rvk[0:c, h, :])
        cp(usb[0:c], pUv[0:c], eng="act")
        pW = P()
        pWv = pW.rearrange("p (a b) -> p a b", b=64)
        for h in range(4):
            mm(pWv[:, h, 0:c], rkk[0:c, h, :], SM[0:c, h, 0:c])
        cp(wcT[:, :, 0:c], pWv[:, 0:4, 0:c], eng="act")
        if not first:
            pws = P()
            pwv = pws.rearrange("p (a b) -> p a b", b=128)
            for h in range(4):
                mm(pwv[0:c, h, :], wcT[:, h, 0:c], Sg[:, h, :])
            tt(vnew[0:c], usb[0:c], pwv[0:c], ALU.subtract)
            vn = vnew
        else:
            vn = usb
        pO2 = P()
        pO2v = pO2.rearrange("p (a b) -> p a b", b=128)
        for h in range(4):
            mm(pO2v[0:c, h, :], qkT[0:c, h, 0:c], vn[0:c, h, :])
        if not first:
            cp(o2s[0:c], pO2v[0:c], eng="act")
            pO1 = P()
            pO1v = pO1.rearrange("p (a b) -> p a b", b=128)
            for h in range(4):
                mm(pO1v[0:c, h, :], qkv[:, h, o:o + c], Sg[:, h, :])
            tt(osb[0:c], pO1v[0:c], bc(gst[0:c, 8:12].unsqueeze(2), [c, 4, 128]), ALU.mult)
            tt(osb[0:c], osb[0:c], o2s[0:c], ALU.add)
        else:
            cp(osb[0:c], pO2v[0:c], eng="act")
        pS = P()
        pSv = pS.rearrange("p (a b) -> p a b", b=128)
        for h in range(4):
            mm(pSv[:, h, :], kdk[0:c, h, :], vn[0:c, h, :])
        if first:
            cp(Sg, pSv)
        else:
            tt(Sg, Sg, bc(gtotb.unsqueeze(2), [128, 4, 128]), ALU.mult)
            tt(Sg, Sg, pSv, ALU.add)
        p4 = P()
        p4v = p4.rearrange("p (a b) -> p a b", b=64)
        for h in range(4):
            tr(p4v[:, h, 0:c], osb[0:c, h, :], c)
        cp(yT[:, :, o:o + c], p4v[:, 0:4, 0:c], eng="act")

    def gdn_post(l, t0, c, bi):
        act(zT[:, :, 0:c], zT[:, :, 0:c], AF.Silu)
        tt(sq[:, :, 0:c], yT[:, :, 0:c], yT[:, :, 0:c], ALU.mult)
        p = P()
        pv = p.rearrange("p (a b) -> p a b", b=64)
        for h in range(4):
            mm(pv[:, h, 0:c], ones[:, :], sq[:, h, 0:c])
        act(sq[:, :, 0:c], pv[:, 0:4, 0:c], AF.Ln, bias=epsc[:, 0:1], scale=1.0 / 128)
        act(sq[:, :, 0:c], sq[:, :, 0:c], AF.Exp, scale=-0.5)
        tt(yT[:, :, 0:c], yT[:, :, 0:c], sq[:, :, 0:c], ALU.mult)
        tt(yT[:, :, 0:c], yT[:, :, 0:c], zT[:, :, 0:c], ALU.mult)
        ts(yT[:, :, 0:c], yT[:, :, 0:c], pc("gnorm_w"), ALU.mult)
        S.dma("sp", yb[bi][:, t0:t0 + c].rearrange("(k p) t -> p k t", p=128), yT[:, :, 0:c])

    def gdn_branch(l, bi):
        nonlocal Sg
        act(gacol[:, 0:1], pc("ga_log", 0, 4), AF.Exp)
        ts(gacol[:, 1:2], gacol[:, 0:1], -1.0, ALU.mult)
        mset(cbg[:, :, 0:3], 0.0)
        for ci, (t0, c) in enumerate(chunks):
            gdn_chunk(l, t0, c, ci == 0, nxt_of(ci))
            gdn_post(l, t0, c, bi)
        gdn_state_out(p_gdn[l], p_gconv[l])
        if with_samples:
            c = NS
            gdn_pre(l, TP, c)
            S.dma("sp", B1[0:48, 0:1536], st_gconv[l].rearrange("b i c -> (b i) c"))
            sbT = B2[:, 0:576].rearrange("p (k b i) -> p k b i", k=12, i=3)
            for half in range(2):
                p = P()
                for k in range(6):
                    kk_ = half * 6 + k
                    tr(p[:, k * 48:(k + 1) * 48], B1[0:48, kk_ * 128:(kk_ + 1) * 128], 48)
                cp(B2[:, half * 288:(half + 1) * 288], p[:, 0:288])
            W4 = pcs("gconv_w", 48)
            g16 = G[26][:, 0:192].rearrange("p (a b) -> p a b", b=16)
            t16 = G[27][:, 0:192].rearrange("p (a b) -> p a b", b=16)
            for i in range(4):
                src = sbT[:, :, :, i] if i < 3 else cbg[:, :, 3:3 + c]
                wv = bc(W4[:, i * 12:(i + 1) * 12].unsqueeze(2), [128, 12, c])
                if i == 0:
                    tt(g16, src, wv, ALU.mult)
                else:
                    tt(t16, src, wv, ALU.mult, eng="pool")
                    tt(g16, g16, t16, ALU.add)
            act(qkv[:, :, 0:c], g16, AF.Silu)
            gdn_gb(c)
            S.dma("sp", s_gconv[l, :, 0:2, :], st_gconv[l, :, 1:3, :])
            for g3 in range(3):
                p = P()
                for j in range(4):
                    tr(p[0:c, j * 128:(j + 1) * 128], cbg[:, g3 * 4 + j, 3:3 + c], 128)
                cp(B1[0:c, g3 * 512:(g3 + 1) * 512], p[0:c, :], eng="act")
            S.dma("sp", s_gconv[l, :, 2, :], B1[0:c, 0:1536])
            gdn_norm(c)
            SGS = [g8(18, 128, 128), g8(28, 128, 128)]
            S.dma("sp", SGS[0], st_gdn[l, 0].rearrange("h k v -> k h v"))
            for b in range(NS):
                if b + 1 < NS:
                    S.dma("sp", SGS[(b + 1) % 2], st_gdn[l, b + 1].rearrange("h k v -> k h v"))
                Sg = SGS[b % 2]
                gdn_core(1, b, False)
                gdn_state_out(s_gdn[l, b], None)
            Sg = SGS[0]
            gdn_post(l, TP, c, bi)

    def gdn_state_out(dst_st, dst_conv):
        for h in range(4):
            S.dma("sp", dst_st[h], Sg[:, h, :])
        if dst_conv is None:
            return
        for k in range(12):
            p = P()
            tr(p[0:3, 0:128], cbg[:, k, 0:3], 128)
            cp(y1s[0:3, 0:128], p[0:3, 0:128])
            S.dma("sp", dst_conv[:, k * 128:(k + 1) * 128], y1s[0:3, 0:128])

    prw = B0[0:64, 0:1690].rearrange("p (a b) -> p a b", b=65)
    uT = B1[0:64, 0:1664].rearrange("p (a b) -> p a b", b=64)
    dsh = B2[0:64, 0:1664].rearrange("p (a b) -> p a b", b=64)
    zr = g8(0, 64)
    w2sb = G[30][0:64, :]
    a2sb = G[31][0:64, :]
    thw = sb([64, 64], name="thw")
    ldT = g8(1, 64); aT = g8(2, 64); kkT = g8(3, 64); kT_ = g8(4, 64); t8a = g8(5, 64); t8b = g8(6, 64)
    bon = g8(7, 64); lw = g8(8, 64)
    rmask = sb([64, 8, 64], name="rmask")
    mset(rmask, 1.0)
    mset(rmask[:, :, 0:1], 0.0)
    Wt = g8(9, 64); rt = g8(11, 64); at = g8(12, 64); bt = g8(13, 64); kt = g8(14, 64); bh = g8(15, 64); kh = g8(16, 64)
    BhTok = g8(17, 64); KhTok = g8(18, 64); VTok = g8(19, 64); AtTok = g8(20, 64)
    LakT = g8(26, 64); MrbT = g8(27, 64); MrkT = g8(28, 64); STs = g8(29, 64)
    Zsb = g8(5, 64); U1 = g8(6, 64); Usb = g8(1, 64); PTs = g8(2, 64); ytk = g8(3, 64); yrT = g8(4, 64)
    ybf8 = sb([64, 8, 64], BF16, name="ybf8")

    def mm8(pv, lhs, rhs, c, msk, out, o=0):
        for h in range(8):
            mm(pv[0:c, h, 0:c], lhs[:, h, o:o + c], rhs[:, h, o:o + c])
        tt(out[0:c, :, 0:c], pv[0:c, :, 0:c], bc(msk[0:c, 0:c].unsqueeze(1), [c, 8, c]), ALU.mult)

    def to_tok(dst, src, c, o=0):
        p = P()
        pv = p.rearrange("p (a b) -> p a b", b=64)
        for h in range(8):
            tr(pv[0:c, h, :], src[:, h, o:o + c], 64)
        cp(dst[0:c], pv[0:c], eng="act")

    def rw_chunk(l, t0, c, first, nxt=None):
        rw_pre(l, t0, c)
        rw_chunk2(l, t0, c, first, nxt)

    def rw_pre(l, t0, c, part=None):
        part = part or ("z" if pf["rw"] else "all")
        if part in ("all", "in"):
            S.dma("sp", prw[:, :, 1:1 + c], projT[0:1664, t0:t0 + c].rearrange("(j p) t -> p j t", p=64))
        if part in ("all", "z"):
            S.dma("sp", zr[:, :, 0:c], projT[1664:2176, t0:t0 + c].rearrange("(j p) t -> p j t", p=64))
            pf["rw"] = False
        if part == "in":
            pf["rw"] = True

    def rw_chunk2(l, t0, c, first, nxt=None):
        rw_shift(c, None)
        if nxt is not None:
            rw_pre(l, nxt[0], nxt[1], "in")
        rw_elem(c, False)
        rw_core(c, 0, first)

    def rw_shift(c, shT):
        tt(dsh[:, :, 0:c], (prw[:, :, 0:c] if shT is None else shT), prw[:, :, 1:1 + c], ALU.subtract)
        tt(dsh[:, :, 0:c], dsh[:, :, 0:c], bc(pcs("mu", 26, 64).unsqueeze(2), [64, 26, c]), ALU.mult)
        tt(uT[:, :, 0:c], dsh[:, :, 0:c], prw[:, :, 1:1 + c], ALU.add)
        cp(dsh[:, :, 0:1], prw[:, :, c:c + 1], eng="pool")
        cp(prw[:, :, 0:1], dsh[:, :, 0:1], eng="pool")

    def rw_elem(c, sample):
        r_, k0, v_ = uT[:, 0:8], uT[:, 8:16], uT[:, 16:24]
        act(thw[:, 0:c], uT[:, 24, 0:c], AF.Tanh)
        p = P()
        pv = p.rearrange("p (a b) -> p a b", b=64)
        for h in range(8):
            mm(pv[0:64, h, 0:c], w2sb[:, h * 64:(h + 1) * 64], thw[:, 0:c])
        tt(ldT[:, :, 0:c], pv[0:64, :, 0:c], bc(pcs("w0", 8, 64).unsqueeze(2), [64, 8, c]), ALU.add)
        act(ldT[:, :, 0:c], ldT[:, :, 0:c], AF.Sigmoid)
        ts(ldT[:, :, 0:c], ldT[:, :, 0:c], -EXPM05, ALU.mult)
        p = P()
        pv = p.rearrange("p (a b) -> p a b", b=64)
        for h in range(8):
            mm(pv[0:64, h, 0:c], a2sb[:, h * 64:(h + 1) * 64], uT[:, 25, 0:c])
        tt(aT[:, :, 0:c], pv[0:64, :, 0:c], bc(pcs("a0", 8, 64).unsqueeze(2), [64, 8, c]), ALU.add)
        act(aT[:, :, 0:c], aT[:, :, 0:c], AF.Sigmoid)
        tt(kkT[:, :, 0:c], k0[:, :, 0:c], bc(pcs("k_k", 8, 64).unsqueeze(2), [64, 8, c]), ALU.mult)
        tt(t8a[:, :, 0:c], kkT[:, :, 0:c], kkT[:, :, 0:c], ALU.mult)
        p = P()
        pv = p.rearrange("p (a b) -> p a b", b=64)
        for h in range(8):
            mm(pv[0:64, h, 0:c], ones[0:64, 0:64], t8a[:, h, 0:c])
        act(t8a[:, :, 0:c], pv[0:64, :, 0:c], AF.Ln, bias=epsc[0:64, 0:1])
        act(t8a[:, :, 0:c], t8a[:, :, 0:c], AF.Exp, scale=-0.5)
        tt(kkT[:, :, 0:c], kkT[:, :, 0:c], t8a[:, :, 0:c], ALU.mult)
        ts(t8a[:, :, 0:c], aT[:, :, 0:c], -1.0, ALU.add)
        tt(t8a[:, :, 0:c], t8a[:, :, 0:c], bc(pcs("k_a", 8, 64).unsqueeze(2), [64, 8, c]), ALU.mult)
        stt(kT_[:, :, 0:c], t8a[:, :, 0:c], 1.0, k0[:, :, 0:c], ALU.add, ALU.mult)
        tt(t8a[:, :, 0:c], r_[:, :, 0:c], kT_[:, :, 0:c], ALU.mult)
        tt(t8a[:, :, 0:c], t8a[:, :, 0:c], bc(pcs("r_k", 8, 64).unsqueeze(2), [64, 8, c]), ALU.mult)
        p = P()
        pv = p.rearrange("p (a b) -> p a b", b=64)
        for h in range(8):
            mm(pv[0:64, h, 0:c], ones[0:64, 0:64], t8a[:, h, 0:c])
        tt(bon[:, :, 0:c], pv[0:64, :, 0:c], v_[:, :, 0:c], ALU.mult)
        if sample:
            cp(lw[:, :, 0:c], ldT[:, :, 0:c])
            act(Wt[:, :, 0:c], lw[:, :, 0:c], AF.Exp)
            tt(rt[:, :, 0:c], r_[:, :, 0:c], Wt[:, :, 0:c], ALU.mult)
            act(t8a[:, :, 0:c], lw[:, :, 0:c], AF.Exp, scale=-1.0)
            tt(t8b[:, :, 0:c], kkT[:, :, 0:c], aT[:, :, 0:c], ALU.mult)
            tt(bt[:, :, 0:c], t8b[:, :, 0:c], t8a[:, :, 0:c], ALU.mult)
            tt(kt[:, :, 0:c], kT_[:, :, 0:c], t8a[:, :, 0:c], ALU.mult)
            ts(at[:, :, 0:c], kkT[:, :, 0:c], -1.0, ALU.mult)
            cp(bh[:, :, 0:c], t8b[:, :, 0:c])
            cp(kh[:, :, 0:c], kT_[:, :, 0:c])
            return
        if c == 64:
            S.op("dve", lambda e: e.tensor_tensor_scan(out=lw.rearrange("p a b -> p (a b)"), data0=rmask.rearrange("p a b -> p (a b)"),
                                                       data1=ldT.rearrange("p a b -> p (a b)"), initial=0.0,
                                                       op0=ALU.mult, op1=ALU.add), reads=[rmask, ldT], writes=[lw])
        else:
            for h in range(8):
                S.op("dve", lambda e, h=h: e.tensor_tensor_scan(out=lw[:, h, 0:c], data0=rmask[:, h, 0:c], data1=ldT[:, h, 0:c],
                                                                initial=0.0, op0=ALU.mult, op1=ALU.add), reads=[rmask, ldT], writes=[lw])
        act(Wt[:, :, 0:c], lw[:, :, 0:c], AF.Exp)
        tt(rt[:, :, 0:c], r_[:, :, 0:c], Wt[:, :, 0:c], ALU.mult)
        act(t8a[:, :, 0:c], lw[:, :, 0:c], AF.Exp, scale=-1.0)
        tt(t8b[:, :, 0:c], kkT[:, :, 0:c], aT[:, :, 0:c], ALU.mult)
        tt(bt[:, :, 0:c], t8b[:, :, 0:c], t8a[:, :, 0:c], ALU.mult)
        tt(kt[:, :, 0:c], kT_[:, :, 0:c], t8a[:, :, 0:c], ALU.mult)
        tt(t8a[:, :, 0:c], lw[:, :, 0:c], ldT[:, :, 0:c], ALU.subtract)
        act(t8a[:, :, 0:c], t8a[:, :, 0:c], AF.Exp)
        stt(at[:, :, 0:c], kkT[:, :, 0:c], -1.0, t8a[:, :, 0:c], ALU.mult, ALU.mult)
        tt(t8a[:, :, 0:c], bc(lw[:, :, c - 1:c], [64, 8, c]), lw[:, :, 0:c], ALU.subtract)
        act(t8a[:, :, 0:c], t8a[:, :, 0:c], AF.Exp)
        tt(bh[:, :, 0:c], t8b[:, :, 0:c], t8a[:, :, 0:c], ALU.mult)
        tt(kh[:, :, 0:c], kT_[:, :, 0:c], t8a[:, :, 0:c], ALU.mult)

    def rw_core(c, o, first):
        r_ = uT[:, 0:8]
        v_ = uT[:, 16:24]
        to_tok(BhTok, bh, c, o)
        to_tok(KhTok, kh, c, o)
        to_tok(VTok, v_, c, o)
        to_tok(AtTok, at, c, o)
        pv = P().rearrange("p (a b) -> p a b", b=64)
        mm8(pv, at, bt, c, trils, XA, o)
        pv = P().rearrange("p (a b) -> p a b", b=64)
        mm8(pv, bt, at, c, trius, XTA, o)
        pv = P().rearrange("p (a b) -> p a b", b=64)
        mm8(pv, kt, at, c, trius, LakT, o)
        pv = P().rearrange("p (a b) -> p a b", b=64)
        mm8(pv, bt, rt, c, triu, MrbT, o)
        pv = P().rearrange("p (a b) -> p a b", b=64)
        mm8(pv, kt, rt, c, triu, MrkT, o)
        solve(XA, XTA, XB, XTB, SM, 8, c)
        pv = P().rearrange("p (a b) -> p a b", b=64)
        for h in range(8):
            mm(pv[0:c, h, :], LakT[0:c, h, 0:c], VTok[0:c, h, :])
        cp(Zsb[0:c], pv[0:c], eng="act")
        pv = P().rearrange("p (a b) -> p a b", b=64)
        for h in range(8):
            mm(pv[0:c, h, :], SM[0:c, h, 0:c], Zsb[0:c, h, :])
        cp(U1[0:c], pv[0:c], eng="act")
        pv = P().rearrange("p (a b) -> p a b", b=64)
        for h in range(8):
            mm(pv[0:64, h, 0:c], AtTok[0:c, h, :], SM[0:c, h, 0:c])
        cp(PTs[:, :, 0:c], pv[0:64, :, 0:c], eng="act")
        if not first:
            pv = P().rearrange("p (a b) -> p a b", b=64)
            for h in range(8):
                mm(pv[0:c, h, :], PTs[:, h, 0:c], STs[:, h, :])
            tt(Usb[0:c], U1[0:c], pv[0:c], ALU.add)
            Uc = Usb
        else:
            Uc = U1
        pY = P().rearrange("p (a b) -> p a b", b=64)
        for h in range(8):
            if not first:
                mm(pY[0:c, h, :], rt[:, h, o:o + c], STs[:, h, :], start=True, stop=False)
            mm(pY[0:c, h, :], MrbT[0:c, h, 0:c], Uc[0:c, h, :], start=first, stop=False)
            mm(pY[0:c, h, :], MrkT[0:c, h, 0:c], VTok[0:c, h, :], start=False, stop=True)
        cp(ytk[0:c], pY[0:c], eng="act")
        pS = P().rearrange("p (a b) -> p a b", b=64)
        for h in range(8):
            mm(pS[0:64, h, :], BhTok[0:c, h, :], Uc[0:c, h, :], start=True, stop=False)
            mm(pS[0:64, h, :], KhTok[0:c, h, :], VTok[0:c, h, :], start=False, stop=True)
        if first:
            cp(STs, pS[0:64])
        else:
            tt(STs, STs, bc(Wt[:, :, o + c - 1:o + c], [64, 8, 64]), ALU.mult)
            tt(STs, STs, pS[0:64], ALU.add)
        pv = P().rearrange("p (a b) -> p a b", b=64)
        for h in range(8):
            tr(pv[0:64, h, 0:c], ytk[0:c, h, :], c)
        cp(yrT[:, :, o:o + c], pv[0:64, :, 0:c], eng="act")

    def rw_post(l, t0, c, bi):
        act(zr[:, :, 0:c], zr[:, :, 0:c], AF.Silu)
        pv = P().rearrange("p (a b) -> p a b", b=64)
        for h in range(8):
            mm(pv[0:64, h, 0:c], ones[0:64, 0:64], yrT[:, h, 0:c])
        stt(yrT[:, :, 0:c], pv[0:64, :, 0:c], -1.0 / 64, yrT[:, :, 0:c], ALU.mult, ALU.add)
        tt(t8a[:, :, 0:c], yrT[:, :, 0:c], yrT[:, :, 0:c], ALU.mult)
        pv = P().rearrange("p (a b) -> p a b", b=64)
        for h in range(8):
            mm(pv[0:64, h, 0:c], ones[0:64, 0:64], t8a[:, h, 0:c])
        act(t8a[:, :, 0:c], pv[0:64, :, 0:c], AF.Ln, bias=epsc[0:64, 3:4], scale=1.0 / 64)
        act(t8a[:, :, 0:c], t8a[:, :, 0:c], AF.Exp, scale=-0.5)
        tt(yrT[:, :, 0:c], yrT[:, :, 0:c], t8a[:, :, 0:c], ALU.mult)
        tt(yrT[:, :, 0:c], yrT[:, :, 0:c], bc(pcs("gn_w", 8, 64).unsqueeze(2), [64, 8, c]), ALU.mult)
        tt(yrT[:, :, 0:c], yrT[:, :, 0:c], bc(pcs("gn_b", 8, 64).unsqueeze(2), [64, 8, c]), ALU.add)
        tt(yrT[:, :, 0:c], yrT[:, :, 0:c], bon[:, :, 0:c], ALU.add)
        tt(yrT[:, :, 0:c], yrT[:, :, 0:c], zr[:, :, 0:c], ALU.mult)
        S.dma("sp", yb[bi][:, t0:t0 + c].rearrange("(h p) t -> p h t", p=64), yrT[:, :, 0:c])

    def rw_branch(l, bi):
        S.dma("sp", w2sb, w2[l])
        S.dma("sp", a2sb, a2[l])
        mset(prw[:, :, 0:1], 0.0)
        for ci, (t0, c) in enumerate(chunks):
            rw_chunk(l, t0, c, ci == 0, nxt_of(ci))
            rw_post(l, t0, c, bi)
        rw_state_out(p_wkv[l], p_shift[l])
        if with_samples:
            c = NS
            rw_pre(l, TP, c)
            S.dma("sp", B2[64:64 + c, 0:1664], st_shift[l]) if False else None
            stt_ = G[10][0:c, :]
            shT = g8(29, 64)[:, :, 0:32].rearrange("p a b -> p (a b)")[:, 0:0] if False else None
            shT = B2[64:128, 0:416].rearrange("p (a b) -> p a b", b=16) if False else None
            shT = G[25][0:64, 0:416].rearrange("p (a b) -> p a b", b=16)
            for q in range(4):
                nj = 8 if q < 3 else 2
                S.dma("sp", stt_[:, 0:nj * 64], st_shift[l, :, q * 512:q * 512 + nj * 64])
                p = P()
                for j in range(nj):
                    tr(p[0:64, j * 16:(j + 1) * 16], stt_[:, j * 64:(j + 1) * 64], c)
                cp(shT[:, q * 8:q * 8 + nj, :], p[0:64, 0:nj * 16].rearrange("p (a b) -> p a b", b=16))
            rw_shift(c, shT)
            for q in range(4):
                nj = 8 if q < 3 else 2
                p = P()
                for j in range(nj):
                    tr(p[0:c, j * 64:(j + 1) * 64], prw[:, q * 8 + j, 1:1 + c], 64)
                cp(stt_[:, 0:nj * 64], p[0:c, 0:nj * 64], eng="act")
                S.dma("sp", s_shift[l, :, q * 512:q * 512 + nj * 64], stt_[:, 0:nj * 64])
            rw_elem(c, True)
            stg = B2[0:64, 0:512].rearrange("p (a b) -> p a b", b=64)
            S.dma("sp", stg, st_wkv[l, 0].rearrange("h v k -> v h k"))
            for b in range(NS):
                pv = P().rearrange("p (a b) -> p a b", b=64)
                for h in range(8):
                    tr(pv[0:64, h, :], stg[:, h, :], 64)
                cp(STs, pv[0:64])
                if b + 1 < NS:
                    S.dma("sp", stg, st_wkv[l, b + 1].rearrange("h v k -> v h k"))
                rw_core(1, b, False)
                rw_state_out(s_wkv[l, b], None)
            rw_post(l, TP, c, bi)

    def rw_state_out(dst_st, dst_shift):
        pv = P().rearrange("p (a b) -> p a b", b=64)
        for h in range(8):
            tr(pv[0:64, h, :], STs[:, h, :], 64)
        cp(ytk, pv[0:64])
        S.dma("sp", dst_st.rearrange("h v k -> v h k"), ytk)
        if dst_shift is None:
            return
        p = P()
        tr(p[0:26, 0:64], prw[:, :, 0:1].rearrange("p a b -> p (a b)"), 64)
        cp(y1s[0:26, 0:64], p[0:26, 0:64])
        S.dma("sp", dst_shift.rearrange("(j p) -> j p", p=64), y1s[0:26, 0:64])

    for l in range(depth):
        S.dma("sp", pt, ptab[l])
        cur_src[0] = xin if l == 0 else xs[l % 2]
        phase1(l)
        bi = 0
        for b in branches:
            if b == "ssm":
                ssd_branch(l, bi)
                bi += 1
            if b == "gdn":
                gdn_branch(l, bi)
                bi += 1
            if b == "rw":
                rw_branch(l, bi)
                bi += 1
        phase3(l)

    S.finish()
    S.emit()
    return nc, S


_CACHE = {}


def kernel(**inp):
    inp = {k: np.asarray(v) for k, v in inp.items()}
    if "nc" not in _CACHE:
        _CACHE["nc"] = build()[0]
    nc = _CACHE["nc"]
    return _run(nc, inp, 8)


def _in_maps(inp, ncores):
    ptab = np.stack([_build_pt(inp, l) for l in range(DEPTH)])
    maps = []
    for c in range(ncores):
        xin = np.concatenate([inp["meta_tokens"], inp["x_prompt"][c], inp["x_sample"][16 * c:16 * c + 16, 0]], axis=0)
        sl = slice(16 * c, 16 * c + 16)
        maps.append({
            "xin": np.ascontiguousarray(xin, dtype=np.float32),
            "w_in": inp["w_in"], "w_rw_out": inp["w_rw_out"], "w_ssm_out": inp["w_ssm_out"],
            "w_gdn_out": inp["w_gdn_out"], "w_out": inp["w_out"], "rw_w2": inp["rw_w2"], "rw_a2": inp["rw_a2"],
            "ptab": ptab, "final_norm_w": inp["final_norm_w"].reshape(1, D),
            "st_wkv": np.ascontiguousarray(inp["state_rwkv_wkv"][:, sl]),
            "st_shift": np.ascontiguousarray(inp["state_rwkv_shift"][:, sl]),
            "st_ssm": np.ascontiguousarray(inp["state_ssm"][:, sl]),
            "st_sconv": np.ascontiguousarray(inp["state_ssm_conv"][:, sl]),
            "st_gdn": np.ascontiguousarray(inp["state_gdn"][:, sl]),
            "st_gconv": np.ascontiguousarray(inp["state_gdn_conv"][:, sl]),
        })
    return maps


def _run(nc, inp, ncores):
    maps = _in_maps(inp, ncores)
    res = run_bass_kernel_spmd(nc, maps, core_ids=list(range(ncores)))
    R = res.results
    y_prompt = np.stack([R[c]["y_out"][NMETA:TP] for c in range(ncores)])
    y_sample = np.concatenate([R[c]["y_out"][TP:TT] for c in range(ncores)])[:, None, :]

    def pst(name):
        return np.stack([R[c][name] for c in range(ncores)], axis=1)

    def sst(name):
        return np.concatenate([R[c][name] for c in range(ncores)], axis=1)
    return (y_prompt, y_sample, pst("p_wkv"), pst("p_shift"), pst("p_ssm"), pst("p_sconv"), pst("p_gdn"),
            pst("p_gconv"), sst("s_wkv"), sst("s_shift"), sst("s_ssm"), sst("s_sconv"), sst("s_gdn"), sst("s_gconv"))
```

```python
import numpy as np
import concourse.bass as bass
import concourse.mybir as mybir
from concourse.bass_utils import run_bass_kernel_spmd

F32 = mybir.dt.float32
BF16 = mybir.dt.bfloat16
ALU = mybir.AluOpType
AF = mybir.ActivationFunctionType
AX = mybir.AxisListType

DEPTH = 4
D = 1024
NMETA = 16
SEQ = 2048
TP = NMETA + SEQ
NS = 16
TT = TP + NS
DIN = 8848
C_RW, C_SSM, C_GDN, C_GATE = 0, 2176, 3720, 5776
EXPM05 = float(np.exp(-0.5))

SEM_EPOCH = 20000
N_DMA_SLOTS = 6


class _Sem:
    __slots__ = ("h", "val")

    def __init__(self, h):
        self.h = h
        self.val = 0


class _Buf:
    __slots__ = ("w", "r")

    def __init__(self):
        self.w = None
        self.r = {}


class _Eng:
    def __init__(self, name):
        self.name = name
        self.sem = None
        self.prog = []
        self.known = {}
        self.slots = []
        self.slot_i = 0


class Sched:
    def __init__(self, nc):
        self.nc = nc
        self.bufs = {}
        self.E = {n: _Eng(n) for n in ("pe", "dve", "act", "pool", "sp")}
        self.nsem = 0
        for e in self.E.values():
            e.sem = self._new_sem()
        for qn in ("sp", "act", "pool"):
            self.E[qn].slots = [self._new_sem() for _ in range(N_DMA_SLOTS)]

    def _new_sem(self):
        h = self.nc.semaphore(f"s{self.nsem}").__enter__()
        self.nsem += 1
        return _Sem(h)

    def _buf(self, ap):
        key = ap.tensor.name
        b = self.bufs.get(key)
        if b is None:
            b = self.bufs[key] = _Buf()
        return b

    def _need(self, eng, deps, sem, val, same_ok):
        if sem is eng.sem and same_ok:
            return
        if eng.known.get(sem, 0) >= val:
            return
        deps[sem] = max(deps.get(sem, 0), val)

    def _deps(self, eng, reads, writes, same_ok=True):
        deps = {}
        for ap in reads:
            b = self._buf(ap)
            if b.w is not None:
                self._need(eng, deps, b.w[0], b.w[1], False)
        for ap in writes:
            b = self._buf(ap)
            if b.w is not None:
                self._need(eng, deps, b.w[0], b.w[1], same_ok)
            for s, v in b.r.items():
                self._need(eng, deps, s, v, same_ok)
        for s, v in deps.items():
            eng.prog.append(("wait", s, v))
            eng.known[s] = v

    def op(self, en, fn, reads=(), writes=(), inc=True):
        eng = self.E[en]
        self._deps(eng, reads, writes)
        if eng.sem.val >= SEM_EPOCH:
            eng.sem = self._new_sem()
        s = eng.sem
        if inc:
            s.val += 1
            v = s.val
            eng.prog.append(("inst", fn, s, 1))
        else:
            v = s.val + 1
            eng.prog.append(("inst", fn, s, 0))
        for ap in reads:
            self._buf(ap).r[s] = v
        for ap in writes:
            b = self._buf(ap)
            b.w = (s, v)
            b.r = {}

    def dma(self, qn, out, in_, **kw):
        eng = self.E[qn]
        slot = eng.slots[eng.slot_i % N_DMA_SLOTS]
        eng.slot_i += 1
        if slot.val > 0 and eng.known.get(slot, 0) < slot.val:
            eng.prog.append(("wait", slot, slot.val))
            eng.known[slot] = slot.val
        self._deps(eng, [in_], [out], same_ok=False)
        slot.val += 16
        v = slot.val

        def fn(e, out=out, in_=in_, kw=kw):
            return e.dma_start(out=out, in_=in_, **kw)
        eng.prog.append(("inst", fn, slot, 16))
        self._buf(in_).r[slot] = v
        b = self._buf(out)
        b.w = (slot, v)
        b.r = {}

    def finish(self):
        sp = self.E["sp"]
        for e in self.E.values():
            for s in e.slots:
                if s.val > 0 and sp.known.get(s, 0) < s.val:
                    sp.prog.append(("wait", s, s.val))
                    sp.known[s] = s.val
        for e in self.E.values():
            if e is not sp and e.sem.val > 0:
                sp.prog.append(("wait", e.sem, e.sem.val))

    def emit(self):
        E = self.E

        def replay(eng, h):
            for it in eng.prog:
                if it[0] == "wait":
                    h.wait_ge(it[1].h, it[2])
                elif it[3]:
                    it[1](h).then_inc(it[2].h, it[3])
                else:
                    it[1](h)

        with self.nc.Block() as block:
            @block.tensor
            def _(h):
                replay(E["pe"], h)

            @block.vector
            def _(h):
                replay(E["dve"], h)

            @block.scalar
            def _(h):
                replay(E["act"], h)

            @block.gpsimd
            def _(h):
                replay(E["pool"], h)

            @block.sync
            def _(h):
                replay(E["sp"], h)

    def stats(self):
        return ({n: sum(1 for i in e.prog if i[0] == "inst") for n, e in self.E.items()},
                {n: sum(1 for i in e.prog if i[0] == "wait") for n, e in self.E.items()}, self.nsem)


def _pt_layout():
    cols = {}
    n = 0

    def add(name, k):
        nonlocal n
        cols[name] = n
        n += k
    add("norm_w", 8)
    add("mu", 26); add("w0", 8); add("a0", 8); add("k_k", 8); add("k_a", 8); add("r_k", 8)
    add("gn_w", 8); add("gn_b", 8)
    add("sconv_w", 32); add("sconv_b", 8); add("snorm_w", 4); add("dt_bias", 1); add("a_log", 1)
    add("ssm_d", 4)
    add("gconv_w", 48); add("gdt_bias", 1); add("ga_log", 1); add("gnorm_w", 1)
    return cols, n


PTC, NPC = _pt_layout()


def _build_pt(inp, l):
    pt = np.zeros((128, NPC), np.float32)

    def put128(name, v):
        m = v.size // 128
        pt[:, PTC[name]:PTC[name] + m] = v.reshape(m, 128).T

    def put64(name, v):
        m = v.size // 64
        pt[0:64, PTC[name]:PTC[name] + m] = v.reshape(m, 64).T

    put128("norm_w", inp["norm_w"][l])
    put64("mu", inp["rw_mu"][l]); put64("w0", inp["rw_w0"][l]); put64("a0", inp["rw_a0"][l])
    put64("k_k", inp["rw_k_k"][l]); put64("k_a", inp["rw_k_a"][l]); put64("r_k", inp["rw_r_k"][l].reshape(-1))
    put64("gn_w", inp["rw_gn_w"][l]); put64("gn_b", inp["rw_gn_b"][l])
    put128("sconv_w", inp["ssm_conv_w"][l].reshape(-1))
    put128("sconv_b", inp["ssm_conv_b"][l]); put128("snorm_w", inp["ssm_norm_w"][l])
    pt[0:8, PTC["dt_bias"]] = inp["ssm_dt_bias"][l]
    pt[0:8, PTC["a_log"]] = inp["ssm_a_log"][l]
    put128("ssm_d", np.repeat(inp["ssm_d"][l], 64))
    put128("gconv_w", inp["gdn_conv_w"][l].reshape(-1))
    pt[0:4, PTC["gdt_bias"]] = inp["gdn_dt_bias"][l]
    pt[0:4, PTC["ga_log"]] = inp["gdn_a_log"][l]
    pt[:, PTC["gnorm_w"]] = inp["gdn_norm_w"][l]
    return pt


def build(depth=DEPTH, branches=("rw", "ssm", "gdn"), debug=False, with_samples=True):
    nc = bass.Bass("TRN2", target_bir_lowering=False)
    S = Sched(nc)

    def din(name, shape):
        return nc.dram_tensor(name, list(shape), F32, kind="ExternalInput").ap()

    def dout(name, shape):
        return nc.dram_tensor(name, list(shape), F32, kind="ExternalOutput").ap()

    def dscr(name, shape):
        return nc.dram_tensor(name, list(shape), F32).ap()

    xin = din("xin", [TT, D])
    w_in = din("w_in", [DEPTH, D, DIN])
    w_rwo = din("w_rw_out", [DEPTH, 512, D])
    w_sso = din("w_ssm_out", [DEPTH, 512, D])
    w_gdo = din("w_gdn_out", [DEPTH, 512, D])
    w_out = din("w_out", [DEPTH, D, D])
    w2 = din("rw_w2", [DEPTH, 64, 512])
    a2 = din("rw_a2", [DEPTH, 64, 512])
    ptab = din("ptab", [DEPTH, 128, NPC])
    fnw = din("final_norm_w", [1, D])
    st_wkv = din("st_wkv", [DEPTH, NS, 8, 64, 64])
    st_shift = din("st_shift", [DEPTH, NS, 1664])
    st_ssm = din("st_ssm", [DEPTH, NS, 8, 64, 128])
    st_sconv = din("st_sconv", [DEPTH, NS, 3, 1024])
    st_gdn = din("st_gdn", [DEPTH, NS, 4, 128, 128])
    st_gconv = din("st_gconv", [DEPTH, NS, 3, 1536])

    y_out = dout("y_out", [TT, D])
    p_wkv = dout("p_wkv", [DEPTH, 8, 64, 64])
    p_shift = dout("p_shift", [DEPTH, 1664])
    p_ssm = dout("p_ssm", [DEPTH, 8, 64, 128])
    p_sconv = dout("p_sconv", [DEPTH, 3, 1024])
    p_gdn = dout("p_gdn", [DEPTH, 4, 128, 128])
    p_gconv = dout("p_gconv", [DEPTH, 3, 1536])
    s_wkv = dout("s_wkv", [DEPTH, NS, 8, 64, 64])
    s_shift = dout("s_shift", [DEPTH, NS, 1664])
    s_ssm = dout("s_ssm", [DEPTH, NS, 8, 64, 128])
    s_sconv = dout("s_sconv", [DEPTH, NS, 3, 1024])
    s_gdn = dout("s_gdn", [DEPTH, NS, 4, 128, 128])
    s_gconv = dout("s_gconv", [DEPTH, NS, 3, 1536])
    dbg = dout("dbg", [3, 512, TT]) if debug else None

    xs = [dscr("xs0", [TT, D]), dscr("xs1", [TT, D])]
    projT = dscr("projT", [DIN, TT])
    yb = [dscr(f"yb{i}", [512, TT]) for i in range(3)]

    _n = [0]

    def sb(shape, dt=F32, name=None):
        _n[0] += 1
        return nc.alloc_sbuf_tensor(name or f"t{_n[0]}", list(shape), dt).ap()

    PS = [nc.alloc_psum_tensor(f"ps{i}", [128, 512], F32).ap() for i in range(8)]
    _pi = [0]

    def P():
        _pi[0] += 1
        return PS[_pi[0] % 8]

    _last = [True]

    def GRP(it):
        it = list(it)
        for i, x in enumerate(it):
            _last[0] = (i == len(it) - 1)
            yield x
        _last[0] = True

    def mm(out, lhsT, rhs, start=True, stop=True):
        S.op("pe", lambda e: e.matmul(out, lhsT=lhsT, rhs=rhs, start=start, stop=stop),
             reads=[lhsT, rhs], writes=[out], inc=(stop and _last[0]))

    def tr(out, in_, n):
        S.op("pe", lambda e: e.transpose(out, in_, ident[0:n, 0:n]), reads=[in_, ident], writes=[out], inc=_last[0])

    def act(out, in_, func, bias=None, scale=1.0, accum=None, eng="act"):
        rd = [in_]
        kw = {}
        if bias is not None:
            kw["bias"] = bias
            rd.append(bias)
        if not isinstance(scale, float):
            rd.append(scale)
        if accum is not None:
            kw["accum_out"] = accum
        wr = [out] + ([accum] if accum is not None else [])
        S.op("act", lambda e: e.activation(out=out, in_=in_, func=func, scale=scale, **kw), reads=rd, writes=wr)

    def tt(out, a, b, op, eng="dve"):
        S.op(eng, lambda e: e.tensor_tensor(out=out, in0=a, in1=b, op=op), reads=[a, b], writes=[out])

    def ts(out, a, s1, op0, s2=None, op1=None, eng="dve"):
        rd = [a] + [s for s in (s1, s2) if s is not None and not isinstance(s, float)]
        if op1 is None:
            S.op(eng, lambda e: e.tensor_scalar(out=out, in0=a, scalar1=s1, scalar2=None, op0=op0), reads=rd, writes=[out])
        else:
            S.op(eng, lambda e: e.tensor_scalar(out=out, in0=a, scalar1=s1, scalar2=s2, op0=op0, op1=op1), reads=rd, writes=[out])

    def stt(out, a, s, b, op0, op1, eng="dve"):
        rd = [a, b] + ([] if isinstance(s, float) else [s])
        S.op(eng, lambda e: e.scalar_tensor_tensor(out=out, in0=a, scalar=s, in1=b, op0=op0, op1=op1), reads=rd, writes=[out])

    def cp(out, in_, eng="dve"):
        if eng == "act":
            S.op("act", lambda e: e.copy(out=out, in_=in_), reads=[in_], writes=[out])
        else:
            S.op(eng, lambda e: e.tensor_copy(out=out, in_=in_), reads=[in_], writes=[out])

    def mset(t, v, eng="pool"):
        S.op(eng, lambda e: e.memset(t, v), writes=[t])

    def asel(t, pattern, cmp, fill, base, cm):
        S.op("pool", lambda e: e.affine_select(out=t, in_=t, pattern=pattern, compare_op=cmp, fill=fill,
                                               base=base, channel_multiplier=cm), reads=[t], writes=[t])

    def bc(ap, shape):
        return ap.to_broadcast(list(shape))

    ident = sb([128, 128], name="ident")
    mset(ident, 0.0)
    asel(ident, [[-1, 128]], ALU.not_equal, 1.0, 0, 1)
    ones = sb([128, 128], name="ones")
    mset(ones, 1.0)
    triu = sb([64, 64], name="triu")
    mset(triu, 1.0)
    asel(triu, [[1, 64]], ALU.is_ge, 0.0, 0, -1)
    trius = sb([64, 64], name="trius")
    mset(trius, 1.0)
    asel(trius, [[1, 64]], ALU.is_gt, 0.0, 0, -1)
    trils = sb([64, 64], name="trils")
    mset(trils, 1.0)
    asel(trils, [[-1, 64]], ALU.is_gt, 0.0, 0, 1)
    negU = sb([64, 64], name="negU")
    mset(negU, 0.0)
    asel(negU, [[1, 64]], ALU.is_ge, -30000.0, 0, -1)
    posLs = sb([64, 64], name="posLs")
    mset(posLs, 0.0)
    asel(posLs, [[-1, 64]], ALU.is_gt, 30000.0, 0, 1)
    epsc = sb([128, 4], name="epsc")
    mset(epsc[:, 0:1], 1e-6)
    mset(epsc[:, 1:2], 1.0)
    mset(epsc[:, 2:3], 1e-5)
    mset(epsc[:, 3:4], 64e-5)
    negU8 = sb([64, 8, 64], name="negU8")
    cp(negU8, bc(negU.unsqueeze(1), [64, 8, 64]), eng="pool")
    posL4 = sb([64, 4, 64], name="posL4")
    cp(posL4, bc(posLs.unsqueeze(1), [64, 4, 64]), eng="pool")

    hT_all = sb([128, 8, TT], BF16, name="hT_all")
    WA = sb([128, 8, 1024], BF16, name="WA")
    WB = sb([128, 8, 1024], BF16, name="WB")
    Wo = hT_all[:, :, 0:1024]
    ysb = [sb([128, 4, 512], BF16, name=f"ysb{i}") for i in range(3)]
    mT = sb([128, 8, 512], BF16, name="mT")
    dtraw = sb([8, 64], name="dtraw")
    garaw = sb([4, 64], name="garaw")
    gbraw = sb([4, 64], name="gbraw")
    pt = sb([128, NPC], name="pt")
    xt = sb([128, D], name="xt")
    xr = sb([128, D], name="xr")
    junk = xr
    stat = sb([128, 4], name="stat")
    NG = 32
    G = [sb([128, 512], name=f"g{i}") for i in range(NG)]
    B0 = sb([128, 1690], name="B0"); B1 = sb([128, 1664], name="B1"); B2 = sb([128, 1664], name="B2"); B3 = sb([128, 768], name="B3")

    fnw_bc = B1[:, 0:D]

    def g8(i, rows=128, b=64):
        return G[i][0:rows, :].rearrange("p (a b) -> p a b", b=b)

    def g4(i, rows=128, b=64):
        return G[i][0:rows, 0:4 * b].rearrange("p (a b) -> p a b", b=b)

    def pc(name, k=0, rows=128):
        c0 = PTC[name] + k
        return pt[0:rows, c0:c0 + 1]

    def pcs(name, k, rows=128):
        c0 = PTC[name]
        return pt[0:rows, c0:c0 + k]

    ttiles = [(i * 128, 128) for i in range(16)] + [(2048, 32)]
    chunks = [(i * 64, 64) for i in range(32)] + [(2048, 16)]

    def norm_to_hT(t0, n, l):
        act(junk[0:n], xt[0:n], AF.Square, accum=stat[0:n, 0:1])
        act(stat[0:n, 1:2], stat[0:n, 0:1], AF.Ln, bias=epsc[0:n, 0:1], scale=1.0 / D)
        act(stat[0:n, 2:3], stat[0:n, 1:2], AF.Exp, scale=-0.5)
        ts(xr[0:n], xt[0:n], stat[0:n, 2:3], ALU.mult)
        for half in range(2):
            p = P()
            pv = p.rearrange("p (a b) -> p a b", b=128)
            for j in range(4):
                k = half * 4 + j
                tr(pv[:, j, 0:n], xr[0:n, k * 128:(k + 1) * 128], n)
            tt(hT_all[:, half * 4:half * 4 + 4, t0:t0 + n], pv[:, :, 0:n],
               bc(pcs("norm_w", 8)[:, half * 4:half * 4 + 4].unsqueeze(2), [128, 4, n]), ALU.mult)

    def final_norm(t0, n):
        act(junk[0:n], xt[0:n], AF.Square, accum=stat[0:n, 0:1])
        act(stat[0:n, 1:2], stat[0:n, 0:1], AF.Ln, bias=epsc[0:n, 0:1], scale=1.0 / D)
        act(stat[0:n, 2:3], stat[0:n, 1:2], AF.Exp, scale=-0.5)
        stt(xr[0:n], xt[0:n], stat[0:n, 2:3], fnw_bc[0:n], ALU.mult, ALU.mult)
        S.dma("sp", y_out[t0:t0 + n, :], xr[0:n])

    cbs = B0[:, 0:536].rearrange("p (a b) -> p a b", b=67)
    cacc = g8(0); ctmp = g8(1); xsT = g8(2)
    dtT = sb([8, 2, 64], name="dtT")
    acol = sb([8, 2], name="acol")
    tok8 = sb([64, 16], name="tok8")
    acs = sb([64, 40], name="acs")
    totb = sb([128, 8], name="totb")
    Rt = g8(3, 64); segT = g8(4, 64); xTok = g8(5, 64); xdt = g8(6, 64); xdtd = g8(7, 64)
    BTok = G[8][0:64, 0:256]
    MT = g8(9, 64)
    y1s = G[10][0:64, :]
    ytok = g8(11, 64)
    hst = g8(12)
    zT = g4(13); yT = g4(14); sq = g4(15)
    rs = G[16][:, 0:128].rearrange("p (a b) -> p a b", b=64)
    ybf = sb([128, 4, 64], BF16, name="ybf")
    cur_src = [xin]

    supers = [(0, 512), (512, 512), (1024, 512), (1536, 512), (2048, 32)]
    _stg = [0]

    def phase1(l):
        src = cur_src[0]
        for (t0, n) in ttiles:
            S.dma("sp", xt[0:n], src[t0:t0 + n, :])
            norm_to_hT(t0, n, l)
        gi = 0
        for c0 in range(0, DIN, 1024):
            ncol = min(1024, DIN - c0)
            W = WA if gi % 2 == 0 else WB
            gi += 1
            S.dma("pool", W[:, :, 0:ncol], w_in[l, :, c0:c0 + ncol].rearrange("(k p) c -> p k c", p=128))
            for (s0, sn) in supers:
                for j0 in range(0, ncol, 128):
                    w = min(128, ncol - j0)
                    p = P()
                    for k in GRP(range(8)):
                        mm(p[0:w, 0:sn], W[:, k, j0:j0 + w], hT_all[:, k, s0:s0 + sn], start=(k == 0), stop=(k == 7))
                    _stg[0] += 1
                    stg = G[_stg[0] % 8]
                    cp(stg[0:w, 0:sn], p[0:w, 0:sn], eng=("act" if _stg[0] % 2 else "dve"))
                    S.dma("sp", projT[c0 + j0:c0 + j0 + w, s0:s0 + sn], stg[0:w, 0:sn])

    def phase3(l):
        wsrc = [w_rwo, w_sso, w_gdo]
        Wb = [WA[:, 0:4, :], WA[:, 4:8, :], WB[:, 0:4, :]]
        for b in range(3):
            S.dma("pool", Wb[b], wsrc[b][l].rearrange("(k p) c -> p k c", p=128))
        S.dma("pool", Wo, w_out[l].rearrange("(k p) c -> p k c", p=128))
        if l == depth - 1:
            S.dma("sp", fnw_bc, fnw.partition_broadcast(128))
        src = cur_src[0]
        dst = xs[(l + 1) % 2]
        gq = 0
        for (s0, sn) in supers:
            for b in range(3):
                S.dma("pool", ysb[b][:, :, 0:sn], yb[b][:, s0:s0 + sn].rearrange("(k p) t -> p k t", p=128))
            for cc in range(8):
                macc = G[8 + (cc % 2)]
                for b in range(3):
                    gq += 1
                    gt = G[10 + (gq % 4)]
                    r0 = C_GATE + b * 1024 + cc * 128
                    S.dma("sp", gt[:, 0:sn], projT[r0:r0 + 128, s0:s0 + sn])
                    act(gt[:, 0:sn], gt[:, 0:sn], AF.Sigmoid)
                    po = P()
                    for k in GRP(range(4)):
                        mm(po[:, 0:sn], Wb[b][:, k, cc * 128:(cc + 1) * 128], ysb[b][:, k, 0:sn], start=(k == 0), stop=(k == 3))
                    if b == 0:
                        tt(macc[:, 0:sn], po[:, 0:sn], gt[:, 0:sn], ALU.mult)
                    else:
                        tt(gt[:, 0:sn], po[:, 0:sn], gt[:, 0:sn], ALU.mult)
                        if b == 1:
                            tt(macc[:, 0:sn], macc[:, 0:sn], gt[:, 0:sn], ALU.add)
                        else:
                            tt(mT[:, cc, 0:sn], macc[:, 0:sn], gt[:, 0:sn], ALU.add)
            for (t0, n) in ttiles:
                if not (s0 <= t0 < s0 + sn):
                    continue
                o_ = t0 - s0
                S.dma("sp", xt[0:n], src[t0:t0 + n, :])
                for half in range(2):
                    p = P()
                    for k in GRP(range(8)):
                        mm(p[0:n, :], mT[:, k, o_:o_ + n], Wo[:, k, half * 512:(half + 1) * 512], start=(k == 0), stop=(k == 7))
                    tt(xt[0:n, half * 512:(half + 1) * 512], xt[0:n, half * 512:(half + 1) * 512], p[0:n, :], ALU.add)
                if l == depth - 1:
                    final_norm(t0, n)
                else:
                    S.dma("sp", dst[t0:t0 + n, :], xt[0:n])

    pf = {"ssd": False, "gdn": False, "rw": False}

    def nxt_of(ci):
        if ci + 1 < len(chunks):
            return chunks[ci + 1]
        return (TP, NS) if with_samples else None

    def ssd_pre(l, t0, c, part=None):
        part = part or ("z" if pf["ssd"] else "all")
        if part in ("all", "in"):
            S.dma("sp", cbs[:, :, 3:3 + c], projT[C_SSM + 512:C_SSM + 1536, t0:t0 + c].rearrange("(k p) t -> p k t", p=128))
            S.dma("sp", dtraw[:, 0:c], projT[C_SSM + 1536:C_SSM + 1544, t0:t0 + c])
        if part in ("all", "z"):
            S.dma("sp", zT[:, :, 0:c], projT[C_SSM:C_SSM + 512, t0:t0 + c].rearrange("(k p) t -> p k t", p=128))
            act(zT[:, :, 0:c], zT[:, :, 0:c], AF.Silu)
            pf["ssd"] = False
        if part == "in":
            pf["ssd"] = True
        dt_ps[0] = dtraw

    dt_ps = [None]

    def ssd_dt(c):
        p = dt_ps[0]
        act(dtT[:, 0, 0:c], p[0:8, 0:c], AF.Exp, bias=pc("dt_bias", 0, 8))
        act(dtT[:, 0, 0:c], dtT[:, 0, 0:c], AF.Ln, bias=epsc[0:8, 1:2])
        ts(dtT[:, 1, 0:c], dtT[:, 0, 0:c], acol[:, 1:2], ALU.mult)

    def conv_silu(cb, w_name, nk, c, accv, tmpv, outv, bias_name=None):
        W4 = pcs(w_name, 4 * nk)
        for i in range(4):
            wv = bc(W4[:, i * nk:(i + 1) * nk].unsqueeze(2), [128, nk, c])
            if i == 0:
                tt(accv, cb[:, :, 0:c], wv, ALU.mult)
            else:
                tt(tmpv, cb[:, :, i:i + c], wv, ALU.mult, eng="pool")
                tt(accv, accv, tmpv, ALU.add)
        if bias_name is not None:
            tt(accv, accv, bc(pcs(bias_name, nk).unsqueeze(2), [128, nk, c]), ALU.add)
        act(outv, accv, AF.Silu)

    def ssd_chunk(l, t0, c, first, nxt=None):
        ssd_pre(l, t0, c)
        conv_silu(cbs, "sconv_w", 8, c, cacc[:, :, 0:c], ctmp[:, :, 0:c], xsT[:, :, 0:c], "sconv_b")
        ssd_dt(c)
        cp(ctmp[:, :, 0:3], cbs[:, :, c:c + 3], eng="pool")
        cp(cbs[:, :, 0:3], ctmp[:, :, 0:3], eng="pool")
        if nxt is not None:
            ssd_pre(l, nxt[0], nxt[1], "in")
        ssd_core(c, 0, first)

    def ssd_core(c, o, first):
        p = P()
        tr(p[0:c, 0:8], dtT[:, 0, o:o + c], 8)
        tr(p[0:c, 8:16], dtT[:, 1, o:o + c], 8)
        cp(tok8[0:c], p[0:c, 0:16])
        p = P()
        pv = p.rearrange("p (a b) -> p a b", b=128)
        for j in GRP(range(4)):
            tr(pv[0:c, j, :], xsT[:, j, o:o + c], 128)
        cp(xTok[0:c].rearrange("p a b -> p (a b)"), p[0:c, :], eng="act")
        p = P()
        for j in GRP(range(2)):
            tr(p[0:c, j * 128:(j + 1) * 128], xsT[:, 4 + j, o:o + c], 128)
        cp(BTok[0:c], p[0:c, 0:256], eng="act")
        p = P()
        mm(p[0:c, 0:8], triu[0:c, 0:c], tok8[0:c, 8:16])
        cp(acs[0:c, 0:8], p[0:c, 0:8])
        ts(acs[0:c, 8:16], p[0:c, 0:8], -1.0, ALU.mult)
        act(acs[0:c, 16:24], p[0:c, 0:8], AF.Exp)
        tt(Rt[0:c, :, 0:c], bc(tok8[0:c, 8:16].unsqueeze(2), [c, 8, c]),
           bc(triu[0:c, 0:c].unsqueeze(1), [c, 8, c]), ALU.mult)
        pa = P()
        pav = pa.rearrange("p (a b) -> p a b", b=64)
        if c == 64:
            mm(pa[:, :], ones[0:c, :], Rt[0:c].rearrange("p a b -> p (a b)"), start=True, stop=False)
            mm(pa[0:64, :], ident[0:64, 0:64], negU8.rearrange("p a b -> p (a b)"), start=False, stop=True)
        else:
            for h in range(8):
                mm(pav[:, h, 0:c], ones[0:c, :], Rt[0:c, h, 0:c], start=True, stop=False)
                mm(pav[0:c, h, 0:c], ident[0:c, 0:c], negU[0:c, 0:c], start=False, stop=True)
        tt(segT[0:c, :, 0:c], pav[0:c, :, 0:c], bc(acs[0:c, 8:16].unsqueeze(2), [c, 8, c]), ALU.add)
        act(segT[0:c, :, 0:c], segT[0:c, :, 0:c], AF.Exp)
        act(totb[:, :], pav[:, :, c - 1], AF.Exp)
        tt(acs[0:c, 24:32], pav[0:c, :, c - 1], acs[0:c, 8:16], ALU.add)
        act(acs[0:c, 24:32], acs[0:c, 24:32], AF.Exp)
        tt(xdt[0:c], xTok[0:c], bc(tok8[0:c, 0:8].unsqueeze(2), [c, 8, 64]), ALU.mult)
        tt(xdtd[0:c], xdt[0:c], bc(acs[0:c, 24:32].unsqueeze(2), [c, 8, 64]), ALU.mult, eng="pool")
        pc_ = P()
        pcv = pc_.rearrange("p (a b) -> p a b", b=64)
        for g in GRP(range(2)):
            mm(pcv[0:c, g, 0:c], xsT[:, 4 + g, o:o + c], xsT[:, 6 + g, o:o + c])
        for g in range(2):
            tt(MT[0:c, g * 4:(g + 1) * 4, 0:c], segT[0:c, g * 4:(g + 1) * 4, 0:c],
               bc(pcv[0:c, g:g + 1, 0:c], [c, 4, c]), ALU.mult)
        p1 = P()
        p1v = p1.rearrange("p (a b) -> p a b", b=64)
        for h in GRP(range(8)):
            mm(p1v[0:c, h, :], MT[0:c, h, 0:c], xdt[0:c, h, :])
        cp(y1s[0:c], p1[0:c, :], eng="act")
        p2 = P()
        if not first:
            for g in range(2):
                mm(p2[0:c, g * 256:(g + 1) * 256], xsT[:, 6 + g, o:o + c],
                   hst[:, g * 4:(g + 1) * 4, :].rearrange("p a b -> p (a b)"))
            tt(ytok[0:c], p2[0:c, :].rearrange("p (a b) -> p a b", b=64),
               bc(acs[0:c, 16:24].unsqueeze(2), [c, 8, 64]), ALU.mult)
            tt(ytok[0:c], ytok[0:c], y1s[0:c].rearrange("p (a b) -> p a b", b=64), ALU.add)
            ysrc = ytok
        else:
            ysrc = y1s.rearrange("p (a b) -> p a b", b=64)
        p3 = P()
        for g in range(2):
            mm(p3[:, g * 256:(g + 1) * 256], BTok[0:c, g * 128:(g + 1) * 128],
               xdtd[0:c, g * 4:(g + 1) * 4, :].rearrange("p a b -> p (a b)"))
        if first:
            cp(hst.rearrange("p a b -> p (a b)"), p3[:, :])
        else:
            tt(hst, hst, bc(totb.unsqueeze(2), [128, 8, 64]), ALU.mult)
            tt(hst.rearrange("p a b -> p (a b)"), hst.rearrange("p a b -> p (a b)"), p3[:, :], ALU.add)
        p4 = P()
        p4v = p4.rearrange("p (a b) -> p a b", b=64)
        for j in GRP(range(4)):
            tr(p4v[:, j, 0:c], ysrc[0:c, 2 * j:2 * j + 2, :].rearrange("p a b -> p (a b)"), c)
        cp(yT[:, :, o:o + c], p4v[:, 0:4, 0:c], eng="act")

    def ssd_post(l, t0, c, bi):
        tt(sq[:, :, 0:c], xsT[:, 0:4, 0:c], bc(pcs("ssm_d", 4).unsqueeze(2), [128, 4, c]), ALU.mult)
        tt(yT[:, :, 0:c], yT[:, :, 0:c], sq[:, :, 0:c], ALU.add)
        tt(yT[:, :, 0:c], yT[:, :, 0:c], zT[:, :, 0:c], ALU.mult)
        tt(sq[:, :, 0:c], yT[:, :, 0:c], yT[:, :, 0:c], ALU.mult)
        p = P()
        pv = p.rearrange("p (a b) -> p a b", b=64)
        for g in range(2):
            mm(pv[:, g, 0:c], ones[:, :], sq[:, 2 * g, 0:c], start=True, stop=False)
            mm(pv[:, g, 0:c], ones[:, :], sq[:, 2 * g + 1, 0:c], start=False, stop=True)
        act(rs[:, :, 0:c], pv[:, 0:2, 0:c], AF.Ln, bias=epsc[:, 2:3], scale=1.0 / 256)
        act(rs[:, :, 0:c], rs[:, :, 0:c], AF.Exp, scale=-0.5)
        for g in range(2):
            tt(yT[:, 2 * g:2 * g + 2, 0:c], yT[:, 2 * g:2 * g + 2, 0:c], bc(rs[:, g:g + 1, 0:c], [128, 2, c]), ALU.mult)
        tt(yT[:, :, 0:c], yT[:, :, 0:c], bc(pcs("snorm_w", 4).unsqueeze(2), [128, 4, c]), ALU.mult)
        S.dma("sp", yb[bi][:, t0:t0 + c].rearrange("(k p) t -> p k t", p=128), yT[:, :, 0:c])

    def ssd_branch(l, bi):
        act(acol[:, 0:1], pc("a_log", 0, 8), AF.Exp)
        ts(acol[:, 1:2], acol[:, 0:1], -1.0, ALU.mult)
        mset(cbs[:, :, 0:3], 0.0)
        for ci, (t0, c) in enumerate(chunks):
            ssd_chunk(l, t0, c, ci == 0, nxt_of(ci))
            ssd_post(l, t0, c, bi)
        ssd_state_out(p_ssm[l], p_sconv[l])
        if with_samples:
            c = NS
            ssd_pre(l, TP, c)
            S.dma("sp", B1[0:48, 0:1024], st_sconv[l].rearrange("b i c -> (b i) c"))
            p = P()
            for k in GRP(range(8)):
                tr(p[:, k * 48:(k + 1) * 48], B1[0:48, k * 128:(k + 1) * 128], 48)
            sbT = B2[:, 0:384].rearrange("p (k b i) -> p k b i", k=8, i=3)
            cp(B2[:, 0:384], p[:, 0:384])
            W4 = pcs("sconv_w", 32)
            for i in range(4):
                src = sbT[:, :, :, i] if i < 3 else cbs[:, :, 3:3 + c]
                wv = bc(W4[:, i * 8:(i + 1) * 8].unsqueeze(2), [128, 8, c])
                if i == 0:
                    tt(cacc[:, :, 0:c], src, wv, ALU.mult)
                else:
                    tt(ctmp[:, :, 0:c], src, wv, ALU.mult, eng="pool")
                    tt(cacc[:, :, 0:c], cacc[:, :, 0:c], ctmp[:, :, 0:c], ALU.add)
            tt(cacc[:, :, 0:c], cacc[:, :, 0:c], bc(pcs("sconv_b", 8).unsqueeze(2), [128, 8, c]), ALU.add)
            act(xsT[:, :, 0:c], cacc[:, :, 0:c], AF.Silu)
            ssd_dt(c)
            S.dma("sp", s_sconv[l, :, 0:2, :], st_sconv[l, :, 1:3, :])
            for half in range(2):
                p = P()
                for j in GRP(range(4)):
                    tr(p[0:c, j * 128:(j + 1) * 128], cbs[:, half * 4 + j, 3:3 + c], 128)
                cp(B1[0:c, half * 512:(half + 1) * 512], p[0:c, :], eng="act")
            S.dma("sp", s_sconv[l, :, 2, :], B1[0:c, 0:1024])
            stg = g8(17, 64, 128)
            stg2 = g8(18, 64, 128)

            def ld_state(b):
                S.dma("sp", stg[:, 0:4, :], st_ssm[l, b, 0:4].rearrange("h p n -> p h n"))
                S.dma("sp", stg2[:, 0:4, :], st_ssm[l, b, 4:8].rearrange("h p n -> p h n"))
            ld_state(0)
            for b in range(NS):
                p = P()
                pv = p.rearrange("p (a b) -> p a b", b=64)
                for h in range(4):
                    tr(pv[:, h, :], stg[:, h, :], 64)
                    tr(pv[:, 4 + h, :], stg2[:, h, :], 64)
                cp(hst, pv)
                if b + 1 < NS:
                    ld_state(b + 1)
                ssd_core(1, b, False)
                ssd_state_out(s_ssm[l, b], None, b)
            ssd_post(l, TP, c, bi)

    def ssd_state_out(dst_st, dst_conv, par=0):
        for half in range(2):
            p = P()
            for h in GRP(range(4)):
                tr(p[0:64, h * 128:(h + 1) * 128], hst[:, half * 4 + h, :], 128)
            so = G[21 + 2 * (par % 2) + half][0:64, :]
            cp(so, p[0:64, :], eng=("act" if half else "dve"))
            S.dma("sp", dst_st[half * 4:(half + 1) * 4].rearrange("h p n -> p h n"), so.rearrange("p (h n) -> p h n", n=128))
        if dst_conv is None:
            return
        for k in range(8):
            p = P()
            tr(p[0:3, 0:128], cbs[:, k, 0:3], 128)
            cp(y1s[0:3, 0:128], p[0:3, 0:128])
            S.dma("sp", dst_conv[:, k * 128:(k + 1) * 128], y1s[0:3, 0:128])

    def nsteps_for(c):
        n = 0
        while (1 << (n + 1)) < c:
            n += 1
        return n

    def cs_bc(R, nh, c, mask_nh, mask2d):
        p = P()
        pv = p.rearrange("p (a b) -> p a b", b=64)
        if c == 64:
            mm(p[:, 0:nh * 64], ones[0:c, :], R[0:c].rearrange("p a b -> p (a b)"), start=True, stop=False)
            mm(p[0:64, 0:nh * 64], ident[0:64, 0:64], mask_nh.rearrange("p a b -> p (a b)"), start=False, stop=True)
        else:
            for h in range(nh):
                mm(pv[:, h, 0:c], ones[0:c, :], R[0:c, h, 0:c], start=True, stop=False)
                mm(pv[0:c, h, 0:c], ident[0:c, 0:c], mask2d[0:c, 0:c], start=False, stop=True)
        return pv

    def solve(X, XT, Xb, XTb, Sm, nh, c):
        tt(Sm[0:c, :, 0:c], XT[0:c, :, 0:c], bc(ident[0:c, 0:c].unsqueeze(1), [c, nh, c]), ALU.add)
        ns = nsteps_for(c)
        cur, curT, nxt, nxtT = X, XT, Xb, XTb
        for s_ in range(ns):
            last = s_ == ns - 1
            pX = P()
            pXv = pX.rearrange("p (a b) -> p a b", b=64)
            for h in GRP(range(nh)):
                mm(pXv[0:c, h, 0:c], curT[0:c, h, 0:c], cur[0:c, h, 0:c])
            cp(nxt[0:c, :, 0:c], pXv[0:c, 0:nh, 0:c], eng="act")
            if not last:
                pT = P()
                pTv = pT.rearrange("p (a b) -> p a b", b=64)
                for h in GRP(range(nh)):
                    mm(pTv[0:c, h, 0:c], cur[0:c, h, 0:c], curT[0:c, h, 0:c])
                cp(nxtT[0:c, :, 0:c], pTv[0:c, 0:nh, 0:c])
            pS = P()
            pSv = pS.rearrange("p (a b) -> p a b", b=64)
            for h in GRP(range(nh)):
                mm(pSv[0:c, h, 0:c], nxt[0:c, h, 0:c], Sm[0:c, h, 0:c])
            tt(Sm[0:c, :, 0:c], Sm[0:c, :, 0:c], pSv[0:c, 0:nh, 0:c], ALU.add)
            cur, curT, nxt, nxtT = nxt, nxtT, cur, curT

    XA = g8(21, 64); XTA = g8(22, 64); XB = g8(23, 64); XTB = g8(24, 64); SM = g8(25, 64)

    cbg = B0[:, 0:804].rearrange("p (a b) -> p a b", b=67)
    gacc = B1[:, 0:768].rearrange("p (a b) -> p a b", b=64)
    gtm = B2[:, 0:768].rearrange("p (a b) -> p a b", b=64)
    qkv = B3[:, 0:768].rearrange("p (a b) -> p a b", b=64)
    gbT = sb([4, 2, 64], name="gbT")
    gacol = sb([4, 2], name="gacol")
    gbTok = sb([64, 8], name="gbTok")
    gst = sb([64, 32], name="gst")
    gtotb = sb([128, 4], name="gtotb")
    Rg = g4(0, 64); decT = g4(1, 64); dec2 = g4(2, 64); qkT = g4(3, 64)
    kTok = g8(4, 64, 128); vTok = g8(5, 64, 128); rvk = g8(6, 64, 128); rkk = g8(7, 64, 128); kdk = g8(8, 64, 128)
    usb = g8(9, 64, 128); vnew = g8(11, 64, 128); osb = g8(12, 64, 128); o2s = g8(16, 64, 128)
    wcT = g4(17)
    Sg = g8(18, 128, 128)
    gsq = g8(19); grs = g8(20)

    def gdn_chunk(l, t0, c, first, nxt=None):
        gdn_pre(l, t0, c)
        gdn_chunk2(l, t0, c, first, nxt)

    def gdn_pre(l, t0, c, part=None):
        part = part or ("z" if pf["gdn"] else "all")
        if part in ("all", "in"):
            S.dma("sp", cbg[:, :, 3:3 + c], projT[C_GDN:C_GDN + 1536, t0:t0 + c].rearrange("(k p) t -> p k t", p=128))
            S.dma("sp", garaw[:, 0:c], projT[C_GDN + 2048:C_GDN + 2052, t0:t0 + c])
            S.dma("sp", gbraw[:, 0:c], projT[C_GDN + 2052:C_GDN + 2056, t0:t0 + c])
        if part in ("all", "z"):
            S.dma("sp", zT[:, :, 0:c], projT[C_GDN + 1536:C_GDN + 2048, t0:t0 + c].rearrange("(k p) t -> p k t", p=128))
            act(zT[:, :, 0:c], zT[:, :, 0:c], AF.Silu)
            pf["gdn"] = False
        if part == "in":
            pf["gdn"] = True
        g_ps[0] = garaw
        g_ps[1] = gbraw

    g_ps = [None, None]

    def gdn_gb(c):
        act(gbT[:, 1, 0:c], g_ps[1][0:4, 0:c], AF.Sigmoid)
        p = g_ps[0]
        act(gbT[:, 0, 0:c], p[0:4, 0:c], AF.Exp, bias=pc("gdt_bias", 0, 4))
        act(gbT[:, 0, 0:c], gbT[:, 0, 0:c], AF.Ln, bias=epsc[0:4, 1:2])
        ts(gbT[:, 0, 0:c], gbT[:, 0, 0:c], gacol[:, 1:2], ALU.mult)

    def gdn_chunk2(l, t0, c, first, nxt=None):
        conv_silu(cbg, "gconv_w", 12, c, gacc[:, :, 0:c], gtm[:, :, 0:c], qkv[:, :, 0:c], None)
        gdn_gb(c)
        cp(gtm[:, :, 0:3], cbg[:, :, c:c + 3], eng="pool")
        cp(cbg[:, :, 0:3], gtm[:, :, 0:3], eng="pool")
        if nxt is not None:
            gdn_pre(l, nxt[0], nxt[1], "in")
        gdn_norm(c)
        gdn_core(c, 0, first)

    def gdn_norm(c):
        tt(gsq[:, :, 0:c], qkv[:, 0:8, 0:c], qkv[:, 0:8, 0:c], ALU.mult)
        p = P()
        pv = p.rearrange("p (a b) -> p a b", b=64)
        for j in GRP(range(8)):
            mm(pv[:, j, 0:c], ones[:, :], gsq[:, j, 0:c])
        act(grs[:, :, 0:c], pv[:, :, 0:c], AF.Ln, bias=epsc[:, 0:1])
        act(grs[:, :, 0:c], grs[:, :, 0:c], AF.Exp, scale=-0.5)
        tt(qkv[:, 0:8, 0:c], qkv[:, 0:8, 0:c], grs[:, :, 0:c], ALU.mult)
        ts(qkv[:, 0:4, 0:c], qkv[:, 0:4, 0:c], float(128 ** -0.5), ALU.mult)

    def gdn_core(c, o, first):
        p = P()
        pv = p.rearrange("p (a b) -> p a b", b=128)
        for h in GRP(range(4)):
            tr(pv[0:c, h, :], qkv[:, 4 + h, o:o + c], 128)
        cp(kTok[0:c], pv[0:c], eng="act")
        p = P()
        pv = p.rearrange("p (a b) -> p a b", b=128)
        for h in GRP(range(4)):
            tr(pv[0:c, h, :], qkv[:, 8 + h, o:o + c], 128)
        cp(vTok[0:c], pv[0:c], eng="act")
        p = P()
        tr(p[0:c, 0:4], gbT[:, 0, o:o + c], 4)
        tr(p[0:c, 4:8], gbT[:, 1, o:o + c], 4)
        cp(gbTok[0:c], p[0:c, 0:8])
        p = P()
        mm(p[0:c, 0:4], triu[0:c, 0:c], gbTok[0:c, 0:4])
        cp(gst[0:c, 0:4], p[0:c, 0:4])
        ts(gst[0:c, 4:8], p[0:c, 0:4], -1.0, ALU.mult)
        act(gst[0:c, 8:12], p[0:c, 0:4], AF.Exp)
        tt(gst[0:c, 16:20], gst[0:c, 8:12], gbTok[0:c, 4:8], ALU.mult)
        ts(gst[0:c, 20:24], gbTok[0:c, 4:8], -1.0, ALU.mult)
        tt(Rg[0:c, :, 0:c], bc(gbTok[0:c, 0:4].unsqueeze(2), [c, 4, c]), bc(triu[0:c, 0:c].unsqueeze(1), [c, 4, c]), ALU.mult)
        pa = cs_bc(Rg, 4, c, negU8[:, 0:4, :], negU)
        tt(decT[0:c, :, 0:c], pa[0:c, 0:4, 0:c], bc(gst[0:c, 4:8].unsqueeze(2), [c, 4, c]), ALU.add)
        act(decT[0:c, :, 0:c], decT[0:c, :, 0:c], AF.Exp)
        act(gtotb[:, :], pa[:, 0:4, c - 1], AF.Exp)
        tt(gst[0:c, 12:16], pa[0:c, 0:4, c - 1], gst[0:c, 4:8], ALU.add)
        act(gst[0:c, 12:16], gst[0:c, 12:16], AF.Exp)
        pb = cs_bc(Rg, 4, c, posL4, posLs)
        tt(dec2[0:c, :, 0:c], pb[0:c, 0:4, 0:c], bc(gst[0:c, 4:8].unsqueeze(2), [c, 4, c]), ALU.add)
        act(dec2[0:c, :, 0:c], dec2[0:c, :, 0:c], AF.Exp, scale=-1.0)
        pG = P()
        pGv = pG.rearrange("p (a b) -> p a b", b=64)
        pQ = P()
        pQv = pQ.rearrange("p (a b) -> p a b", b=64)
        for h in range(4):
            mm(pGv[0:c, h, 0:c], qkv[:, 4 + h, o:o + c], qkv[:, 4 + h, o:o + c])
            mm(pQv[0:c, h, 0:c], qkv[:, 4 + h, o:o + c], qkv[:, h, o:o + c])
        tt(qkT[0:c, :, 0:c], pQv[0:c, 0:4, 0:c], decT[0:c, :, 0:c], ALU.mult)
        tt(XA[0:c, 0:4, 0:c], pGv[0:c, 0:4, 0:c], dec2[0:c, :, 0:c], ALU.mult)
        tt(XA[0:c, 0:4, 0:c], XA[0:c, 0:4, 0:c], bc(gst[0:c, 20:24].unsqueeze(2), [c, 4, c]), ALU.mult)
        p = P()
        pv = p.rearrange("p (a b) -> p a b", b=64)
        for h in GRP(range(4)):
            tr(pv[0:c, h, 0:c], XA[0:c, h, 0:c], c)
        cp(XTA[0:c, 0:4, 0:c], pv[0:c, 0:4, 0:c])
        solve(XA[:, 0:4], XTA[:, 0:4], XB[:, 0:4], XTB[:, 0:4], SM[:, 0:4], 4, c)
        tt(rvk[0:c], vTok[0:c], bc(gbTok[0:c, 4:8].unsqueeze(2), [c, 4, 128]), ALU.mult)
        tt(rkk[0:c], kTok[0:c], bc(gst[0:c, 16:20].unsqueeze(2), [c, 4, 128]), ALU.mult, eng="pool")
        tt(kdk[0:c], kTok[0:c], bc(gst[0:c, 12:16].unsqueeze(2), [c, 4, 128]), ALU.mult, eng="pool")
        pU = P()
        pUv = pU.rearrange("p (a b) -> p a b", b=128)
        for h in GRP(range(4)):
            mm(pUv[0:c, h, :], SM[0:c, h, 0:c], rvk[0:c, h, :])
        cp(usb[0:c], pUv[0:c], eng="act")
        pW = P()
        pWv = pW.rearrange("p (a b) -> p a b", b=64)
        for h in GRP(range(4)):
            mm(pWv[:, h, 0:c], rkk[0:c, h, :], SM[0:c, h, 0:c])
        cp(wcT[:, :, 0:c], pWv[:, 0:4, 0:c], eng="act")
        if not first:
            pws = P()
            pwv = pws.rearrange("p (a b) -> p a b", b=128)
            for h in GRP(range(4)):
                mm(pwv[0:c, h, :], wcT[:, h, 0:c], Sg[:, h, :])
            tt(vnew[0:c], usb[0:c], pwv[0:c], ALU.subtract)
            vn = vnew
        else:
            vn = usb
        pO2 = P()
        pO2v = pO2.rearrange("p (a b) -> p a b", b=128)
        for h in GRP(range(4)):
            mm(pO2v[0:c, h, :], qkT[0:c, h, 0:c], vn[0:c, h, :])
        if not first:
            cp(o2s[0:c], pO2v[0:c], eng="act")
            pO1 = P()
            pO1v = pO1.rearrange("p (a b) -> p a b", b=128)
            for h in GRP(range(4)):
                mm(pO1v[0:c, h, :], qkv[:, h, o:o + c], Sg[:, h, :])
            tt(osb[0:c], pO1v[0:c], bc(gst[0:c, 8:12].unsqueeze(2), [c, 4, 128]), ALU.mult)
            tt(osb[0:c], osb[0:c], o2s[0:c], ALU.add)
        else:
            cp(osb[0:c], pO2v[0:c], eng="act")
        pS = P()
        pSv = pS.rearrange("p (a b) -> p a b", b=128)
        for h in GRP(range(4)):
            mm(pSv[:, h, :], kdk[0:c, h, :], vn[0:c, h, :])
        if first:
            cp(Sg, pSv)
        else:
            tt(Sg, Sg, bc(gtotb.unsqueeze(2), [128, 4, 128]), ALU.mult)
            tt(Sg, Sg, pSv, ALU.add)
        p4 = P()
        p4v = p4.rearrange("p (a b) -> p a b", b=64)
        for h in GRP(range(4)):
            tr(p4v[:, h, 0:c], osb[0:c, h, :], c)
        cp(yT[:, :, o:o + c], p4v[:, 0:4, 0:c], eng="act")

    def gdn_post(l, t0, c, bi):
        tt(sq[:, :, 0:c], yT[:, :, 0:c], yT[:, :, 0:c], ALU.mult)
        p = P()
        pv = p.rearrange("p (a b) -> p a b", b=64)
        for h in GRP(range(4)):
            mm(pv[:, h, 0:c], ones[:, :], sq[:, h, 0:c])
        act(sq[:, :, 0:c], pv[:, 0:4, 0:c], AF.Ln, bias=epsc[:, 0:1], scale=1.0 / 128)
        act(sq[:, :, 0:c], sq[:, :, 0:c], AF.Exp, scale=-0.5)
        tt(yT[:, :, 0:c], yT[:, :, 0:c], sq[:, :, 0:c], ALU.mult)
        tt(yT[:, :, 0:c], yT[:, :, 0:c], zT[:, :, 0:c], ALU.mult)
        ts(yT[:, :, 0:c], yT[:, :, 0:c], pc("gnorm_w"), ALU.mult)
        S.dma("sp", yb[bi][:, t0:t0 + c].rearrange("(k p) t -> p k t", p=128), yT[:, :, 0:c])

    def gdn_branch(l, bi):
        nonlocal Sg
        act(gacol[:, 0:1], pc("ga_log", 0, 4), AF.Exp)
        ts(gacol[:, 1:2], gacol[:, 0:1], -1.0, ALU.mult)
        mset(cbg[:, :, 0:3], 0.0)
        for ci, (t0, c) in enumerate(chunks):
            gdn_chunk(l, t0, c, ci == 0, nxt_of(ci))
            gdn_post(l, t0, c, bi)
        gdn_state_out(p_gdn[l], p_gconv[l])
        if with_samples:
            c = NS
            gdn_pre(l, TP, c)
            S.dma("sp", B1[0:48, 0:1536], st_gconv[l].rearrange("b i c -> (b i) c"))
            sbT = B2[:, 0:576].rearrange("p (k b i) -> p k b i", k=12, i=3)
            for half in range(2):
                p = P()
                for k in range(6):
                    kk_ = half * 6 + k
                    tr(p[:, k * 48:(k + 1) * 48], B1[0:48, kk_ * 128:(kk_ + 1) * 128], 48)
                cp(B2[:, half * 288:(half + 1) * 288], p[:, 0:288])
            W4 = pcs("gconv_w", 48)
            g16 = G[26][:, 0:192].rearrange("p (a b) -> p a b", b=16)
            t16 = G[27][:, 0:192].rearrange("p (a b) -> p a b", b=16)
            for i in range(4):
                src = sbT[:, :, :, i] if i < 3 else cbg[:, :, 3:3 + c]
                wv = bc(W4[:, i * 12:(i + 1) * 12].unsqueeze(2), [128, 12, c])
                if i == 0:
                    tt(g16, src, wv, ALU.mult)
                else:
                    tt(t16, src, wv, ALU.mult, eng="pool")
                    tt(g16, g16, t16, ALU.add)
            act(qkv[:, :, 0:c], g16, AF.Silu)
            gdn_gb(c)
            S.dma("sp", s_gconv[l, :, 0:2, :], st_gconv[l, :, 1:3, :])
            for g3 in range(3):
                p = P()
                for j in GRP(range(4)):
                    tr(p[0:c, j * 128:(j + 1) * 128], cbg[:, g3 * 4 + j, 3:3 + c], 128)
                cp(B1[0:c, g3 * 512:(g3 + 1) * 512], p[0:c, :], eng="act")
            S.dma("sp", s_gconv[l, :, 2, :], B1[0:c, 0:1536])
            gdn_norm(c)
            SGS = [g8(18, 128, 128), g8(28, 128, 128)]
            S.dma("sp", SGS[0], st_gdn[l, 0].rearrange("h k v -> k h v"))
            for b in range(NS):
                if b + 1 < NS:
                    S.dma("sp", SGS[(b + 1) % 2], st_gdn[l, b + 1].rearrange("h k v -> k h v"))
                Sg = SGS[b % 2]
                gdn_core(1, b, False)
                gdn_state_out(s_gdn[l, b], None)
            Sg = SGS[0]
            gdn_post(l, TP, c, bi)

    def gdn_state_out(dst_st, dst_conv):
        for h in range(4):
            S.dma("sp", dst_st[h], Sg[:, h, :])
        if dst_conv is None:
            return
        for k in range(12):
            p = P()
            tr(p[0:3, 0:128], cbg[:, k, 0:3], 128)
            cp(y1s[0:3, 0:128], p[0:3, 0:128])
            S.dma("sp", dst_conv[:, k * 128:(k + 1) * 128], y1s[0:3, 0:128])

    prw = B0[0:64, 0:1690].rearrange("p (a b) -> p a b", b=65)
    uT = B1[0:64, 0:1664].rearrange("p (a b) -> p a b", b=64)
    dsh = B2[0:64, 0:1664].rearrange("p (a b) -> p a b", b=64)
    zr = g8(0, 64)
    w2sb = G[30][0:64, :]
    a2sb = G[31][0:64, :]
    thw = sb([64, 64], name="thw")
    ldT = g8(1, 64); aT = g8(2, 64); kkT = g8(3, 64); kT_ = g8(4, 64); t8a = g8(5, 64); t8b = g8(6, 64)
    bon = g8(7, 64); lw = g8(8, 64)
    rmask = sb([64, 8, 64], name="rmask")
    mset(rmask, 1.0)
    mset(rmask[:, :, 0:1], 0.0)
    Wt = g8(9, 64); rt = g8(11, 64); at = g8(12, 64); bt = g8(13, 64); kt = g8(14, 64); bh = g8(15, 64); kh = g8(16, 64)
    BhTok = g8(17, 64); KhTok = g8(18, 64); VTok = g8(19, 64); AtTok = g8(20, 64)
    LakT = g8(26, 64); MrbT = g8(27, 64); MrkT = g8(28, 64); STs = g8(29, 64)
    Zsb = g8(5, 64); U1 = g8(6, 64); Usb = g8(1, 64); PTs = g8(2, 64); ytk = g8(3, 64); yrT = g8(4, 64)
    ybf8 = sb([64, 8, 64], BF16, name="ybf8")

    def mm8(pv, lhs, rhs, c, msk, out, o=0):
        for h in GRP(range(8)):
            mm(pv[0:c, h, 0:c], lhs[:, h, o:o + c], rhs[:, h, o:o + c])
        tt(out[0:c, :, 0:c], pv[0:c, :, 0:c], bc(msk[0:c, 0:c].unsqueeze(1), [c, 8, c]), ALU.mult)

    def to_tok(dst, src, c, o=0):
        p = P()
        pv = p.rearrange("p (a b) -> p a b", b=64)
        for h in GRP(range(8)):
            tr(pv[0:c, h, :], src[:, h, o:o + c], 64)
        cp(dst[0:c], pv[0:c], eng="act")

    def rw_chunk(l, t0, c, first, nxt=None):
        rw_pre(l, t0, c)
        rw_chunk2(l, t0, c, first, nxt)

    def rw_pre(l, t0, c, part=None):
        part = part or ("z" if pf["rw"] else "all")
        if part in ("all", "in"):
            S.dma("sp", prw[:, :, 1:1 + c], projT[0:1664, t0:t0 + c].rearrange("(j p) t -> p j t", p=64))
        if part in ("all", "z"):
            S.dma("sp", zr[:, :, 0:c], projT[1664:2176, t0:t0 + c].rearrange("(j p) t -> p j t", p=64))
            act(zr[:, :, 0:c], zr[:, :, 0:c], AF.Silu)
            pf["rw"] = False
        if part == "in":
            pf["rw"] = True

    def rw_chunk2(l, t0, c, first, nxt=None):
        rw_shift(c, None)
        if nxt is not None:
            rw_pre(l, nxt[0], nxt[1], "in")
        rw_elem(c, False)
        rw_core(c, 0, first)

    def rw_shift(c, shT):
        tt(dsh[:, :, 0:c], (prw[:, :, 0:c] if shT is None else shT), prw[:, :, 1:1 + c], ALU.subtract)
        tt(dsh[:, :, 0:c], dsh[:, :, 0:c], bc(pcs("mu", 26, 64).unsqueeze(2), [64, 26, c]), ALU.mult)
        tt(uT[:, :, 0:c], dsh[:, :, 0:c], prw[:, :, 1:1 + c], ALU.add)
        cp(dsh[:, :, 0:1], prw[:, :, c:c + 1], eng="pool")
        cp(prw[:, :, 0:1], dsh[:, :, 0:1], eng="pool")

    def rw_elem(c, sample):
        r_, k0, v_ = uT[:, 0:8], uT[:, 8:16], uT[:, 16:24]
        act(thw[:, 0:c], uT[:, 24, 0:c], AF.Tanh)
        p = P()
        pv = p.rearrange("p (a b) -> p a b", b=64)
        for h in GRP(range(8)):
            mm(pv[0:64, h, 0:c], w2sb[:, h * 64:(h + 1) * 64], thw[:, 0:c])
        tt(ldT[:, :, 0:c], pv[0:64, :, 0:c], bc(pcs("w0", 8, 64).unsqueeze(2), [64, 8, c]), ALU.add)
        act(ldT[:, :, 0:c], ldT[:, :, 0:c], AF.Sigmoid)
        ts(ldT[:, :, 0:c], ldT[:, :, 0:c], -EXPM05, ALU.mult)
        p = P()
        pv = p.rearrange("p (a b) -> p a b", b=64)
        for h in GRP(range(8)):
            mm(pv[0:64, h, 0:c], a2sb[:, h * 64:(h + 1) * 64], uT[:, 25, 0:c])
        tt(aT[:, :, 0:c], pv[0:64, :, 0:c], bc(pcs("a0", 8, 64).unsqueeze(2), [64, 8, c]), ALU.add)
        act(aT[:, :, 0:c], aT[:, :, 0:c], AF.Sigmoid)
        tt(kkT[:, :, 0:c], k0[:, :, 0:c], bc(pcs("k_k", 8, 64).unsqueeze(2), [64, 8, c]), ALU.mult)
        tt(t8a[:, :, 0:c], kkT[:, :, 0:c], kkT[:, :, 0:c], ALU.mult)
        p = P()
        pv = p.rearrange("p (a b) -> p a b", b=64)
        for h in GRP(range(8)):
            mm(pv[0:64, h, 0:c], ones[0:64, 0:64], t8a[:, h, 0:c])
        act(t8a[:, :, 0:c], pv[0:64, :, 0:c], AF.Ln, bias=epsc[0:64, 0:1])
        act(t8a[:, :, 0:c], t8a[:, :, 0:c], AF.Exp, scale=-0.5)
        tt(kkT[:, :, 0:c], kkT[:, :, 0:c], t8a[:, :, 0:c], ALU.mult)
        ts(t8a[:, :, 0:c], aT[:, :, 0:c], -1.0, ALU.add)
        tt(t8a[:, :, 0:c], t8a[:, :, 0:c], bc(pcs("k_a", 8, 64).unsqueeze(2), [64, 8, c]), ALU.mult)
        stt(kT_[:, :, 0:c], t8a[:, :, 0:c], 1.0, k0[:, :, 0:c], ALU.add, ALU.mult)
        tt(t8a[:, :, 0:c], r_[:, :, 0:c], kT_[:, :, 0:c], ALU.mult)
        tt(t8a[:, :, 0:c], t8a[:, :, 0:c], bc(pcs("r_k", 8, 64).unsqueeze(2), [64, 8, c]), ALU.mult)
        p = P()
        pv = p.rearrange("p (a b) -> p a b", b=64)
        for h in GRP(range(8)):
            mm(pv[0:64, h, 0:c], ones[0:64, 0:64], t8a[:, h, 0:c])
        tt(bon[:, :, 0:c], pv[0:64, :, 0:c], v_[:, :, 0:c], ALU.mult)
        if sample:
            cp(lw[:, :, 0:c], ldT[:, :, 0:c])
            act(Wt[:, :, 0:c], lw[:, :, 0:c], AF.Exp)
            tt(rt[:, :, 0:c], r_[:, :, 0:c], Wt[:, :, 0:c], ALU.mult)
            act(t8a[:, :, 0:c], lw[:, :, 0:c], AF.Exp, scale=-1.0)
            tt(t8b[:, :, 0:c], kkT[:, :, 0:c], aT[:, :, 0:c], ALU.mult)
            tt(bt[:, :, 0:c], t8b[:, :, 0:c], t8a[:, :, 0:c], ALU.mult)
            tt(kt[:, :, 0:c], kT_[:, :, 0:c], t8a[:, :, 0:c], ALU.mult)
            ts(at[:, :, 0:c], kkT[:, :, 0:c], -1.0, ALU.mult)
            cp(bh[:, :, 0:c], t8b[:, :, 0:c])
            cp(kh[:, :, 0:c], kT_[:, :, 0:c])
            return
        if c == 64:
            S.op("dve", lambda e: e.tensor_tensor_scan(out=lw.rearrange("p a b -> p (a b)"), data0=rmask.rearrange("p a b -> p (a b)"),
                                                       data1=ldT.rearrange("p a b -> p (a b)"), initial=0.0,
                                                       op0=ALU.mult, op1=ALU.add), reads=[rmask, ldT], writes=[lw])
        else:
            for h in range(8):
                S.op("dve", lambda e, h=h: e.tensor_tensor_scan(out=lw[:, h, 0:c], data0=rmask[:, h, 0:c], data1=ldT[:, h, 0:c],
                                                                initial=0.0, op0=ALU.mult, op1=ALU.add), reads=[rmask, ldT], writes=[lw])
        act(Wt[:, :, 0:c], lw[:, :, 0:c], AF.Exp)
        tt(rt[:, :, 0:c], r_[:, :, 0:c], Wt[:, :, 0:c], ALU.mult)
        act(t8a[:, :, 0:c], lw[:, :, 0:c], AF.Exp, scale=-1.0)
        tt(t8b[:, :, 0:c], kkT[:, :, 0:c], aT[:, :, 0:c], ALU.mult)
        tt(bt[:, :, 0:c], t8b[:, :, 0:c], t8a[:, :, 0:c], ALU.mult)
        tt(kt[:, :, 0:c], kT_[:, :, 0:c], t8a[:, :, 0:c], ALU.mult)
        tt(t8a[:, :, 0:c], lw[:, :, 0:c], ldT[:, :, 0:c], ALU.subtract)
        act(t8a[:, :, 0:c], t8a[:, :, 0:c], AF.Exp)
        stt(at[:, :, 0:c], kkT[:, :, 0:c], -1.0, t8a[:, :, 0:c], ALU.mult, ALU.mult)
        tt(t8a[:, :, 0:c], bc(lw[:, :, c - 1:c], [64, 8, c]), lw[:, :, 0:c], ALU.subtract)
        act(t8a[:, :, 0:c], t8a[:, :, 0:c], AF.Exp)
        tt(bh[:, :, 0:c], t8b[:, :, 0:c], t8a[:, :, 0:c], ALU.mult)
        tt(kh[:, :, 0:c], kT_[:, :, 0:c], t8a[:, :, 0:c], ALU.mult)

    def rw_core(c, o, first):
        r_ = uT[:, 0:8]
        v_ = uT[:, 16:24]
        to_tok(BhTok, bh, c, o)
        to_tok(KhTok, kh, c, o)
        to_tok(VTok, v_, c, o)
        to_tok(AtTok, at, c, o)
        pv = P().rearrange("p (a b) -> p a b", b=64)
        mm8(pv, at, bt, c, trils, XA, o)
        pv = P().rearrange("p (a b) -> p a b", b=64)
        mm8(pv, bt, at, c, trius, XTA, o)
        pv = P().rearrange("p (a b) -> p a b", b=64)
        mm8(pv, kt, at, c, trius, LakT, o)
        pv = P().rearrange("p (a b) -> p a b", b=64)
        mm8(pv, bt, rt, c, triu, MrbT, o)
        pv = P().rearrange("p (a b) -> p a b", b=64)
        mm8(pv, kt, rt, c, triu, MrkT, o)
        solve(XA, XTA, XB, XTB, SM, 8, c)
        pv = P().rearrange("p (a b) -> p a b", b=64)
        for h in GRP(range(8)):
            mm(pv[0:c, h, :], LakT[0:c, h, 0:c], VTok[0:c, h, :])
        cp(Zsb[0:c], pv[0:c], eng="act")
        pv = P().rearrange("p (a b) -> p a b", b=64)
        for h in GRP(range(8)):
            mm(pv[0:c, h, :], SM[0:c, h, 0:c], Zsb[0:c, h, :])
        cp(U1[0:c], pv[0:c], eng="act")
        pv = P().rearrange("p (a b) -> p a b", b=64)
        for h in GRP(range(8)):
            mm(pv[0:64, h, 0:c], AtTok[0:c, h, :], SM[0:c, h, 0:c])
        cp(PTs[:, :, 0:c], pv[0:64, :, 0:c], eng="act")
        if not first:
            pv = P().rearrange("p (a b) -> p a b", b=64)
            for h in GRP(range(8)):
                mm(pv[0:c, h, :], PTs[:, h, 0:c], STs[:, h, :])
            tt(Usb[0:c], U1[0:c], pv[0:c], ALU.add)
            Uc = Usb
        else:
            Uc = U1
        pY = P().rearrange("p (a b) -> p a b", b=64)
        for h in range(8):
            if not first:
                mm(pY[0:c, h, :], rt[:, h, o:o + c], STs[:, h, :], start=True, stop=False)
            mm(pY[0:c, h, :], MrbT[0:c, h, 0:c], Uc[0:c, h, :], start=first, stop=False)
            mm(pY[0:c, h, :], MrkT[0:c, h, 0:c], VTok[0:c, h, :], start=False, stop=True)
        cp(ytk[0:c], pY[0:c], eng="act")
        pS = P().rearrange("p (a b) -> p a b", b=64)
        for h in range(8):
            mm(pS[0:64, h, :], BhTok[0:c, h, :], Uc[0:c, h, :], start=True, stop=False)
            mm(pS[0:64, h, :], KhTok[0:c, h, :], VTok[0:c, h, :], start=False, stop=True)
        if first:
            cp(STs, pS[0:64])
        else:
            tt(STs, STs, bc(Wt[:, :, o + c - 1:o + c], [64, 8, 64]), ALU.mult)
            tt(STs, STs, pS[0:64], ALU.add)
        pv = P().rearrange("p (a b) -> p a b", b=64)
        for h in GRP(range(8)):
            tr(pv[0:64, h, 0:c], ytk[0:c, h, :], c)
        cp(yrT[:, :, o:o + c], pv[0:64, :, 0:c], eng="act")

    def rw_post(l, t0, c, bi):
        pv = P().rearrange("p (a b) -> p a b", b=64)
        for h in GRP(range(8)):
            mm(pv[0:64, h, 0:c], ones[0:64, 0:64], yrT[:, h, 0:c])
        stt(yrT[:, :, 0:c], pv[0:64, :, 0:c], -1.0 / 64, yrT[:, :, 0:c], ALU.mult, ALU.add)
        tt(t8a[:, :, 0:c], yrT[:, :, 0:c], yrT[:, :, 0:c], ALU.mult)
        pv = P().rearrange("p (a b) -> p a b", b=64)
        for h in GRP(range(8)):
            mm(pv[0:64, h, 0:c], ones[0:64, 0:64], t8a[:, h, 0:c])
        act(t8a[:, :, 0:c], pv[0:64, :, 0:c], AF.Ln, bias=epsc[0:64, 3:4], scale=1.0 / 64)
        act(t8a[:, :, 0:c], t8a[:, :, 0:c], AF.Exp, scale=-0.5)
        tt(yrT[:, :, 0:c], yrT[:, :, 0:c], t8a[:, :, 0:c], ALU.mult)
        tt(yrT[:, :, 0:c], yrT[:, :, 0:c], bc(pcs("gn_w", 8, 64).unsqueeze(2), [64, 8, c]), ALU.mult)
        tt(yrT[:, :, 0:c], yrT[:, :, 0:c], bc(pcs("gn_b", 8, 64).unsqueeze(2), [64, 8, c]), ALU.add)
        tt(yrT[:, :, 0:c], yrT[:, :, 0:c], bon[:, :, 0:c], ALU.add)
        tt(yrT[:, :, 0:c], yrT[:, :, 0:c], zr[:, :, 0:c], ALU.mult)
        S.dma("sp", yb[bi][:, t0:t0 + c].rearrange("(h p) t -> p h t", p=64), yrT[:, :, 0:c])

    def rw_branch(l, bi):
        S.dma("sp", w2sb, w2[l])
        S.dma("sp", a2sb, a2[l])
        mset(prw[:, :, 0:1], 0.0)
        for ci, (t0, c) in enumerate(chunks):
            rw_chunk(l, t0, c, ci == 0, nxt_of(ci))
            rw_post(l, t0, c, bi)
        rw_state_out(p_wkv[l], p_shift[l])
        if with_samples:
            c = NS
            rw_pre(l, TP, c)
            S.dma("sp", B2[64:64 + c, 0:1664], st_shift[l]) if False else None
            stt_ = G[10][0:c, :]
            shT = g8(29, 64)[:, :, 0:32].rearrange("p a b -> p (a b)")[:, 0:0] if False else None
            shT = B2[64:128, 0:416].rearrange("p (a b) -> p a b", b=16) if False else None
            shT = G[25][0:64, 0:416].rearrange("p (a b) -> p a b", b=16)
            for q in range(4):
                nj = 8 if q < 3 else 2
                S.dma("sp", stt_[:, 0:nj * 64], st_shift[l, :, q * 512:q * 512 + nj * 64])
                p = P()
                for j in GRP(range(nj)):
                    tr(p[0:64, j * 16:(j + 1) * 16], stt_[:, j * 64:(j + 1) * 64], c)
                cp(shT[:, q * 8:q * 8 + nj, :], p[0:64, 0:nj * 16].rearrange("p (a b) -> p a b", b=16))
            rw_shift(c, shT)
            for q in range(4):
                nj = 8 if q < 3 else 2
                p = P()
                for j in GRP(range(nj)):
                    tr(p[0:c, j * 64:(j + 1) * 64], prw[:, q * 8 + j, 1:1 + c], 64)
                cp(stt_[:, 0:nj * 64], p[0:c, 0:nj * 64], eng="act")
                S.dma("sp", s_shift[l, :, q * 512:q * 512 + nj * 64], stt_[:, 0:nj * 64])
            rw_elem(c, True)
            stg = B2[0:64, 0:512].rearrange("p (a b) -> p a b", b=64)
            S.dma("sp", stg, st_wkv[l, 0].rearrange("h v k -> v h k"))
            for b in range(NS):
                pv = P().rearrange("p (a b) -> p a b", b=64)
                for h in GRP(range(8)):
                    tr(pv[0:64, h, :], stg[:, h, :], 64)
                cp(STs, pv[0:64])
                if b + 1 < NS:
                    S.dma("sp", stg, st_wkv[l, b + 1].rearrange("h v k -> v h k"))
                rw_core(1, b, False)
                rw_state_out(s_wkv[l, b], None)
            rw_post(l, TP, c, bi)

    def rw_state_out(dst_st, dst_shift):
        pv = P().rearrange("p (a b) -> p a b", b=64)
        for h in GRP(range(8)):
            tr(pv[0:64, h, :], STs[:, h, :], 64)
        cp(ytk, pv[0:64])
        S.dma("sp", dst_st.rearrange("h v k -> v h k"), ytk)
        if dst_shift is None:
            return
        p = P()
        tr(p[0:26, 0:64], prw[:, :, 0:1].rearrange("p a b -> p (a b)"), 64)
        cp(y1s[0:26, 0:64], p[0:26, 0:64])
        S.dma("sp", dst_shift.rearrange("(j p) -> j p", p=64), y1s[0:26, 0:64])

    for l in range(depth):
        S.dma("sp", pt, ptab[l])
        cur_src[0] = xin if l == 0 else xs[l % 2]
        phase1(l)
        bi = 0
        for b in branches:
            if b == "ssm":
                ssd_branch(l, bi)
                bi += 1
            if b == "gdn":
                gdn_branch(l, bi)
                bi += 1
            if b == "rw":
                rw_branch(l, bi)
                bi += 1
        phase3(l)

    S.finish()
    S.emit()
    return nc, S


_CACHE = {}


def kernel(**inp):
    inp = {k: np.asarray(v) for k, v in inp.items()}
    if "nc" not in _CACHE:
        _CACHE["nc"] = build()[0]
    nc = _CACHE["nc"]
    return _run(nc, inp, 8)


def _in_maps(inp, ncores):
    ptab = np.stack([_build_pt(inp, l) for l in range(DEPTH)])
    maps = []
    for c in range(ncores):
        xin = np.concatenate([inp["meta_tokens"], inp["x_prompt"][c], inp["x_sample"][16 * c:16 * c + 16, 0]], axis=0)
        sl = slice(16 * c, 16 * c + 16)
        maps.append({
            "xin": np.ascontiguousarray(xin, dtype=np.float32),
            "w_in": inp["w_in"], "w_rw_out": inp["w_rw_out"], "w_ssm_out": inp["w_ssm_out"],
            "w_gdn_out": inp["w_gdn_out"], "w_out": inp["w_out"], "rw_w2": inp["rw_w2"], "rw_a2": inp["rw_a2"],
            "ptab": ptab, "final_norm_w": inp["final_norm_w"].reshape(1, D),
            "st_wkv": np.ascontiguousarray(inp["state_rwkv_wkv"][:, sl]),
            "st_shift": np.ascontiguousarray(inp["state_rwkv_shift"][:, sl]),
            "st_ssm": np.ascontiguousarray(inp["state_ssm"][:, sl]),
            "st_sconv": np.ascontiguousarray(inp["state_ssm_conv"][:, sl]),
            "st_gdn": np.ascontiguousarray(inp["state_gdn"][:, sl]),
            "st_gconv": np.ascontiguousarray(inp["state_gdn_conv"][:, sl]),
        })
    return maps


def _run(nc, inp, ncores):
    maps = _in_maps(inp, ncores)
    res = run_bass_kernel_spmd(nc, maps, core_ids=list(range(ncores)))
    R = res.results
    y_prompt = np.stack([R[c]["y_out"][NMETA:TP] for c in range(ncores)])
    y_sample = np.concatenate([R[c]["y_out"][TP:TT] for c in range(ncores)])[:, None, :]

    def pst(name):
        return np.stack([R[c][name] for c in range(ncores)], axis=1)

    def sst(name):
        return np.concatenate([R[c][name] for c in range(ncores)], axis=1)
    return (y_prompt, y_sample, pst("p_wkv"), pst("p_shift"), pst("p_ssm"), pst("p_sconv"), pst("p_gdn"),
            pst("p_gconv"), sst("s_wkv"), sst("s_shift"), sst("s_ssm"), sst("s_sconv"), sst("s_gdn"), sst("s_gconv"))
```

```python
import numpy as np
import concourse.bass as bass
import concourse.mybir as mybir
from concourse.bass_utils import run_bass_kernel_spmd

F32 = mybir.dt.float32
BF16 = mybir.dt.bfloat16
ALU = mybir.AluOpType
AF = mybir.ActivationFunctionType
AX = mybir.AxisListType

DEPTH = 4
D = 1024
NMETA = 16
SEQ = 2048
TP = NMETA + SEQ
NS = 16
TT = TP + NS
DIN = 8848
C_RW, C_SSM, C_GDN, C_GATE = 0, 2176, 3720, 5776
EXPM05 = float(np.exp(-0.5))

SEM_EPOCH = 20000
N_DMA_SLOTS = 6


class _Sem:
    __slots__ = ("h", "val")

    def __init__(self, h):
        self.h = h
        self.val = 0


class _Buf:
    __slots__ = ("w", "r")

    def __init__(self):
        self.w = None
        self.r = {}


class _Eng:
    def __init__(self, name):
        self.name = name
        self.sem = None
        self.prog = []
        self.known = {}
        self.slots = []
        self.slot_i = 0


class Sched:
    def __init__(self, nc):
        self.nc = nc
        self.bufs = {}
        self.E = {n: _Eng(n) for n in ("pe", "dve", "act", "pool", "sp")}
        self.nsem = 0
        for e in self.E.values():
            e.sem = self._new_sem()
        for qn in ("sp", "act", "pool"):
            self.E[qn].slots = [self._new_sem() for _ in range(N_DMA_SLOTS)]

    def _new_sem(self):
        h = self.nc.semaphore(f"s{self.nsem}").__enter__()
        self.nsem += 1
        return _Sem(h)

    def _buf(self, ap):
        key = ap.tensor.name
        b = self.bufs.get(key)
        if b is None:
            b = self.bufs[key] = _Buf()
        return b

    def _need(self, eng, deps, sem, val, same_ok):
        if sem is eng.sem and same_ok:
            return
        if eng.known.get(sem, 0) >= val:
            return
        deps[sem] = max(deps.get(sem, 0), val)

    def _deps(self, eng, reads, writes, same_ok=True):
        deps = {}
        for ap in reads:
            b = self._buf(ap)
            if b.w is not None:
                self._need(eng, deps, b.w[0], b.w[1], False)
        for ap in writes:
            b = self._buf(ap)
            if b.w is not None:
                self._need(eng, deps, b.w[0], b.w[1], same_ok)
            for s, v in b.r.items():
                self._need(eng, deps, s, v, same_ok)
        for s, v in deps.items():
            eng.prog.append(("wait", s, v))
            eng.known[s] = v

    def op(self, en, fn, reads=(), writes=()):
        eng = self.E[en]
        self._deps(eng, reads, writes)
        if eng.sem.val >= SEM_EPOCH:
            eng.sem = self._new_sem()
        s = eng.sem
        s.val += 1
        eng.prog.append(("inst", fn, s, 1))
        for ap in reads:
            self._buf(ap).r[s] = s.val
        for ap in writes:
            b = self._buf(ap)
            b.w = (s, s.val)
            b.r = {}

    def dma(self, qn, out, in_, **kw):
        eng = self.E[qn]
        slot = eng.slots[eng.slot_i % N_DMA_SLOTS]
        eng.slot_i += 1
        if slot.val > 0 and eng.known.get(slot, 0) < slot.val:
            eng.prog.append(("wait", slot, slot.val))
            eng.known[slot] = slot.val
        self._deps(eng, [in_], [out], same_ok=False)
        slot.val += 16
        v = slot.val

        def fn(e, out=out, in_=in_, kw=kw):
            return e.dma_start(out=out, in_=in_, **kw)
        eng.prog.append(("inst", fn, slot, 16))
        self._buf(in_).r[slot] = v
        b = self._buf(out)
        b.w = (slot, v)
        b.r = {}

    def finish(self):
        sp = self.E["sp"]
        for e in self.E.values():
            for s in e.slots:
                if s.val > 0 and sp.known.get(s, 0) < s.val:
                    sp.prog.append(("wait", s, s.val))
                    sp.known[s] = s.val
        for e in self.E.values():
            if e is not sp and e.sem.val > 0:
                sp.prog.append(("wait", e.sem, e.sem.val))

    def emit(self):
        E = self.E

        def replay(eng, h):
            for it in eng.prog:
                if it[0] == "wait":
                    h.wait_ge(it[1].h, it[2])
                else:
                    it[1](h).then_inc(it[2].h, it[3])

        with self.nc.Block() as block:
            @block.tensor
            def _(h):
                replay(E["pe"], h)

            @block.vector
            def _(h):
                replay(E["dve"], h)

            @block.scalar
            def _(h):
                replay(E["act"], h)

            @block.gpsimd
            def _(h):
                replay(E["pool"], h)

            @block.sync
            def _(h):
                replay(E["sp"], h)

    def stats(self):
        return ({n: sum(1 for i in e.prog if i[0] == "inst") for n, e in self.E.items()},
                {n: sum(1 for i in e.prog if i[0] == "wait") for n, e in self.E.items()}, self.nsem)


def _pt_layout():
    cols = {}
    n = 0

    def add(name, k):
        nonlocal n
        cols[name] = n
        n += k
    add("norm_w", 8)
    add("mu", 26); add("w0", 8); add("a0", 8); add("k_k", 8); add("k_a", 8); add("r_k", 8)
    add("gn_w", 8); add("gn_b", 8)
    add("sconv_w", 32); add("sconv_b", 8); add("snorm_w", 4); add("dt_bias", 1); add("a_log", 1)
    add("ssm_d", 4)
    add("gconv_w", 48); add("gdt_bias", 1); add("ga_log", 1); add("gnorm_w", 1)
    return cols, n


PTC, NPC = _pt_layout()


def _build_pt(inp, l):
    pt = np.zeros((128, NPC), np.float32)

    def put128(name, v):
        m = v.size // 128
        pt[:, PTC[name]:PTC[name] + m] = v.reshape(m, 128).T

    def put64(name, v):
        m = v.size // 64
        pt[0:64, PTC[name]:PTC[name] + m] = v.reshape(m, 64).T

    put128("norm_w", inp["norm_w"][l])
    put64("mu", inp["rw_mu"][l]); put64("w0", inp["rw_w0"][l]); put64("a0", inp["rw_a0"][l])
    put64("k_k", inp["rw_k_k"][l]); put64("k_a", inp["rw_k_a"][l]); put64("r_k", inp["rw_r_k"][l].reshape(-1))
    put64("gn_w", inp["rw_gn_w"][l]); put64("gn_b", inp["rw_gn_b"][l])
    put128("sconv_w", inp["ssm_conv_w"][l].reshape(-1))
    put128("sconv_b", inp["ssm_conv_b"][l]); put128("snorm_w", inp["ssm_norm_w"][l])
    pt[0:8, PTC["dt_bias"]] = inp["ssm_dt_bias"][l]
    pt[0:8, PTC["a_log"]] = inp["ssm_a_log"][l]
    put128("ssm_d", np.repeat(inp["ssm_d"][l], 64))
    put128("gconv_w", inp["gdn_conv_w"][l].reshape(-1))
    pt[0:4, PTC["gdt_bias"]] = inp["gdn_dt_bias"][l]
    pt[0:4, PTC["ga_log"]] = inp["gdn_a_log"][l]
    pt[:, PTC["gnorm_w"]] = inp["gdn_norm_w"][l]
    return pt


def build(depth=DEPTH, branches=("rw", "ssm", "gdn"), debug=False, with_samples=True):
    nc = bass.Bass("TRN2", target_bir_lowering=False)
    S = Sched(nc)

    def din(name, shape):
        return nc.dram_tensor(name, list(shape), F32, kind="ExternalInput").ap()

    def dout(name, shape):
        return nc.dram_tensor(name, list(shape), F32, kind="ExternalOutput").ap()

    def dscr(name, shape):
        return nc.dram_tensor(name, list(shape), F32).ap()

    xin = din("xin", [TT, D])
    w_in = din("w_in", [DEPTH, D, DIN])
    w_rwo = din("w_rw_out", [DEPTH, 512, D])
    w_sso = din("w_ssm_out", [DEPTH, 512, D])
    w_gdo = din("w_gdn_out", [DEPTH, 512, D])
    w_out = din("w_out", [DEPTH, D, D])
    w2 = din("rw_w2", [DEPTH, 64, 512])
    a2 = din("rw_a2", [DEPTH, 64, 512])
    ptab = din("ptab", [DEPTH, 128, NPC])
    fnw = din("final_norm_w", [1, D])
    st_wkv = din("st_wkv", [DEPTH, NS, 8, 64, 64])
    st_shift = din("st_shift", [DEPTH, NS, 1664])
    st_ssm = din("st_ssm", [DEPTH, NS, 8, 64, 128])
    st_sconv = din("st_sconv", [DEPTH, NS, 3, 1024])
    st_gdn = din("st_gdn", [DEPTH, NS, 4, 128, 128])
    st_gconv = din("st_gconv", [DEPTH, NS, 3, 1536])

    y_out = dout("y_out", [TT, D])
    p_wkv = dout("p_wkv", [DEPTH, 8, 64, 64])
    p_shift = dout("p_shift", [DEPTH, 1664])
    p_ssm = dout("p_ssm", [DEPTH, 8, 64, 128])
    p_sconv = dout("p_sconv", [DEPTH, 3, 1024])
    p_gdn = dout("p_gdn", [DEPTH, 4, 128, 128])
    p_gconv = dout("p_gconv", [DEPTH, 3, 1536])
    s_wkv = dout("s_wkv", [DEPTH, NS, 8, 64, 64])
    s_shift = dout("s_shift", [DEPTH, NS, 1664])
    s_ssm = dout("s_ssm", [DEPTH, NS, 8, 64, 128])
    s_sconv = dout("s_sconv", [DEPTH, NS, 3, 1024])
    s_gdn = dout("s_gdn", [DEPTH, NS, 4, 128, 128])
    s_gconv = dout("s_gconv", [DEPTH, NS, 3, 1536])
    dbg = dout("dbg", [3, 512, TT]) if debug else None

    xs = [dscr("xs0", [TT, D]), dscr("xs1", [TT, D])]
    projT = dscr("projT", [DIN, TT])
    yb = [dscr(f"yb{i}", [512, TT]) for i in range(3)]

    _n = [0]

    def sb(shape, dt=F32, name=None):
        _n[0] += 1
        return nc.alloc_sbuf_tensor(name or f"t{_n[0]}", list(shape), dt).ap()

    PS = [nc.alloc_psum_tensor(f"ps{i}", [128, 512], F32).ap() for i in range(8)]
    _pi = [0]

    def P():
        _pi[0] += 1
        return PS[_pi[0] % 8]

    def mm(out, lhsT, rhs, start=True, stop=True):
        S.op("pe", lambda e: e.matmul(out, lhsT=lhsT, rhs=rhs, start=start, stop=stop),
             reads=[lhsT, rhs], writes=[out])

    def tr(out, in_, n):
        S.op("pe", lambda e: e.transpose(out, in_, ident[0:n, 0:n]), reads=[in_, ident], writes=[out])

    def act(out, in_, func, bias=None, scale=1.0, accum=None, eng="act"):
        rd = [in_]
        kw = {}
        if bias is not None:
            kw["bias"] = bias
            rd.append(bias)
        if not isinstance(scale, float):
            rd.append(scale)
        if accum is not None:
            kw["accum_out"] = accum
        wr = [out] + ([accum] if accum is not None else [])
        S.op("act", lambda e: e.activation(out=out, in_=in_, func=func, scale=scale, **kw), reads=rd, writes=wr)

    def tt(out, a, b, op, eng="dve"):
        S.op(eng, lambda e: e.tensor_tensor(out=out, in0=a, in1=b, op=op), reads=[a, b], writes=[out])

    def ts(out, a, s1, op0, s2=None, op1=None, eng="dve"):
        rd = [a] + [s for s in (s1, s2) if s is not None and not isinstance(s, float)]
        if op1 is None:
            S.op(eng, lambda e: e.tensor_scalar(out=out, in0=a, scalar1=s1, scalar2=None, op0=op0), reads=rd, writes=[out])
        else:
            S.op(eng, lambda e: e.tensor_scalar(out=out, in0=a, scalar1=s1, scalar2=s2, op0=op0, op1=op1), reads=rd, writes=[out])

    def stt(out, a, s, b, op0, op1, eng="dve"):
        rd = [a, b] + ([] if isinstance(s, float) else [s])
        S.op(eng, lambda e: e.scalar_tensor_tensor(out=out, in0=a, scalar=s, in1=b, op0=op0, op1=op1), reads=rd, writes=[out])

    def cp(out, in_, eng="dve"):
        if eng == "act":
            S.op("act", lambda e: e.copy(out=out, in_=in_), reads=[in_], writes=[out])
        else:
            S.op(eng, lambda e: e.tensor_copy(out=out, in_=in_), reads=[in_], writes=[out])

    def mset(t, v, eng="pool"):
        S.op(eng, lambda e: e.memset(t, v), writes=[t])

    def asel(t, pattern, cmp, fill, base, cm):
        S.op("pool", lambda e: e.affine_select(out=t, in_=t, pattern=pattern, compare_op=cmp, fill=fill,
                                               base=base, channel_multiplier=cm), reads=[t], writes=[t])

    def bc(ap, shape):
        return ap.to_broadcast(list(shape))

    ident = sb([128, 128], name="ident")
    mset(ident, 0.0)
    asel(ident, [[-1, 128]], ALU.not_equal, 1.0, 0, 1)
    ones = sb([128, 128], name="ones")
    mset(ones, 1.0)
    triu = sb([64, 64], name="triu")
    mset(triu, 1.0)
    asel(triu, [[1, 64]], ALU.is_ge, 0.0, 0, -1)
    trius = sb([64, 64], name="trius")
    mset(trius, 1.0)
    asel(trius, [[1, 64]], ALU.is_gt, 0.0, 0, -1)
    trils = sb([64, 64], name="trils")
    mset(trils, 1.0)
    asel(trils, [[-1, 64]], ALU.is_gt, 0.0, 0, 1)
    negU = sb([64, 64], name="negU")
    mset(negU, 0.0)
    asel(negU, [[1, 64]], ALU.is_ge, -30000.0, 0, -1)
    posLs = sb([64, 64], name="posLs")
    mset(posLs, 0.0)
    asel(posLs, [[-1, 64]], ALU.is_gt, 30000.0, 0, 1)
    epsc = sb([128, 4], name="epsc")
    mset(epsc[:, 0:1], 1e-6)
    mset(epsc[:, 1:2], 1.0)
    mset(epsc[:, 2:3], 1e-5)
    mset(epsc[:, 3:4], 64e-5)
    negU8 = sb([64, 8, 64], name="negU8")
    cp(negU8, bc(negU.unsqueeze(1), [64, 8, 64]), eng="pool")
    posL4 = sb([64, 4, 64], name="posL4")
    cp(posL4, bc(posLs.unsqueeze(1), [64, 4, 64]), eng="pool")

    hT_all = sb([128, 8, TT], BF16, name="hT_all")
    WA = sb([128, 8, 1024], BF16, name="WA")
    WB = sb([128, 8, 1024], BF16, name="WB")
    Wo = hT_all[:, :, 0:1024]
    ysb = [sb([128, 4, 512], BF16, name=f"ysb{i}") for i in range(3)]
    mT = sb([128, 8, 512], BF16, name="mT")
    dtraw = sb([8, 64], name="dtraw")
    garaw = sb([4, 64], name="garaw")
    gbraw = sb([4, 64], name="gbraw")
    pt = sb([128, NPC], name="pt")
    xt = sb([128, D], name="xt")
    xr = sb([128, D], name="xr")
    junk = xr
    stat = sb([128, 4], name="stat")
    NG = 32
    G = [sb([128, 512], name=f"g{i}") for i in range(NG)]
    B0 = sb([128, 1690], name="B0"); B1 = sb([128, 1664], name="B1"); B2 = sb([128, 1664], name="B2"); B3 = sb([128, 768], name="B3")

    fnw_bc = B1[:, 0:D]

    def g8(i, rows=128, b=64):
        return G[i][0:rows, :].rearrange("p (a b) -> p a b", b=b)

    def g4(i, rows=128, b=64):
        return G[i][0:rows, 0:4 * b].rearrange("p (a b) -> p a b", b=b)

    def pc(name, k=0, rows=128):
        c0 = PTC[name] + k
        return pt[0:rows, c0:c0 + 1]

    def pcs(name, k, rows=128):
        c0 = PTC[name]
        return pt[0:rows, c0:c0 + k]

    ttiles = [(i * 128, 128) for i in range(16)] + [(2048, 32)]
    chunks = [(i * 64, 64) for i in range(32)] + [(2048, 16)]

    def norm_to_hT(t0, n, l):
        act(junk[0:n], xt[0:n], AF.Square, accum=stat[0:n, 0:1])
        act(stat[0:n, 1:2], stat[0:n, 0:1], AF.Ln, bias=epsc[0:n, 0:1], scale=1.0 / D)
        act(stat[0:n, 2:3], stat[0:n, 1:2], AF.Exp, scale=-0.5)
        ts(xr[0:n], xt[0:n], stat[0:n, 2:3], ALU.mult)
        for half in range(2):
            p = P()
            pv = p.rearrange("p (a b) -> p a b", b=128)
            for j in range(4):
                k = half * 4 + j
                tr(pv[:, j, 0:n], xr[0:n, k * 128:(k + 1) * 128], n)
            tt(hT_all[:, half * 4:half * 4 + 4, t0:t0 + n], pv[:, :, 0:n],
               bc(pcs("norm_w", 8)[:, half * 4:half * 4 + 4].unsqueeze(2), [128, 4, n]), ALU.mult)

    def final_norm(t0, n):
        act(junk[0:n], xt[0:n], AF.Square, accum=stat[0:n, 0:1])
        act(stat[0:n, 1:2], stat[0:n, 0:1], AF.Ln, bias=epsc[0:n, 0:1], scale=1.0 / D)
        act(stat[0:n, 2:3], stat[0:n, 1:2], AF.Exp, scale=-0.5)
        stt(xr[0:n], xt[0:n], stat[0:n, 2:3], fnw_bc[0:n], ALU.mult, ALU.mult)
        S.dma("sp", y_out[t0:t0 + n, :], xr[0:n])

    cbs = B0[:, 0:536].rearrange("p (a b) -> p a b", b=67)
    cacc = g8(0); ctmp = g8(1); xsT = g8(2)
    dtT = sb([8, 2, 64], name="dtT")
    acol = sb([8, 2], name="acol")
    tok8 = sb([64, 16], name="tok8")
    acs = sb([64, 40], name="acs")
    totb = sb([128, 8], name="totb")
    Rt = g8(3, 64); segT = g8(4, 64); xTok = g8(5, 64); xdt = g8(6, 64); xdtd = g8(7, 64)
    BTok = G[8][0:64, 0:256]
    MT = g8(9, 64)
    y1s = G[10][0:64, :]
    ytok = g8(11, 64)
    hst = g8(12)
    zT = g4(13); yT = g4(14); sq = g4(15)
    rs = G[16][:, 0:128].rearrange("p (a b) -> p a b", b=64)
    ybf = sb([128, 4, 64], BF16, name="ybf")
    cur_src = [xin]

    supers = [(0, 512), (512, 512), (1024, 512), (1536, 512), (2048, 32)]
    _stg = [0]

    def phase1(l):
        src = cur_src[0]
        for (t0, n) in ttiles:
            S.dma("sp", xt[0:n], src[t0:t0 + n, :])
            norm_to_hT(t0, n, l)
        gi = 0
        for c0 in range(0, DIN, 1024):
            ncol = min(1024, DIN - c0)
            W = WA if gi % 2 == 0 else WB
            gi += 1
            S.dma("pool", W[:, :, 0:ncol], w_in[l, :, c0:c0 + ncol].rearrange("(k p) c -> p k c", p=128))
            for (s0, sn) in supers:
                for j0 in range(0, ncol, 128):
                    w = min(128, ncol - j0)
                    p = P()
                    for k in range(8):
                        mm(p[0:w, 0:sn], W[:, k, j0:j0 + w], hT_all[:, k, s0:s0 + sn], start=(k == 0), stop=(k == 7))
                    _stg[0] += 1
                    stg = G[_stg[0] % 8]
                    cp(stg[0:w, 0:sn], p[0:w, 0:sn], eng=("act" if _stg[0] % 2 else "dve"))
                    S.dma("sp", projT[c0 + j0:c0 + j0 + w, s0:s0 + sn], stg[0:w, 0:sn])

    def phase3(l):
        wsrc = [w_rwo, w_sso, w_gdo]
        Wb = [WA[:, 0:4, :], WA[:, 4:8, :], WB[:, 0:4, :]]
        for b in range(3):
            S.dma("pool", Wb[b], wsrc[b][l].rearrange("(k p) c -> p k c", p=128))
        S.dma("pool", Wo, w_out[l].rearrange("(k p) c -> p k c", p=128))
        if l == depth - 1:
            S.dma("sp", fnw_bc, fnw.partition_broadcast(128))
        src = cur_src[0]
        dst = xs[(l + 1) % 2]
        gq = 0
        for (s0, sn) in supers:
            for b in range(3):
                S.dma("pool", ysb[b][:, :, 0:sn], yb[b][:, s0:s0 + sn].rearrange("(k p) t -> p k t", p=128))
            for cc in range(8):
                macc = G[8 + (cc % 2)]
                for b in range(3):
                    gq += 1
                    gt = G[10 + (gq % 4)]
                    r0 = C_GATE + b * 1024 + cc * 128
                    S.dma("sp", gt[:, 0:sn], projT[r0:r0 + 128, s0:s0 + sn])
                    act(gt[:, 0:sn], gt[:, 0:sn], AF.Sigmoid)
                    po = P()
                    for k in range(4):
                        mm(po[:, 0:sn], Wb[b][:, k, cc * 128:(cc + 1) * 128], ysb[b][:, k, 0:sn], start=(k == 0), stop=(k == 3))
                    if b == 0:
                        tt(macc[:, 0:sn], po[:, 0:sn], gt[:, 0:sn], ALU.mult)
                    else:
                        tt(gt[:, 0:sn], po[:, 0:sn], gt[:, 0:sn], ALU.mult)
                        if b == 1:
                            tt(macc[:, 0:sn], macc[:, 0:sn], gt[:, 0:sn], ALU.add)
                        else:
                            tt(mT[:, cc, 0:sn], macc[:, 0:sn], gt[:, 0:sn], ALU.add)
            for (t0, n) in ttiles:
                if not (s0 <= t0 < s0 + sn):
                    continue
                o_ = t0 - s0
                S.dma("sp", xt[0:n], src[t0:t0 + n, :])
                for half in range(2):
                    p = P()
                    for k in range(8):
                        mm(p[0:n, :], mT[:, k, o_:o_ + n], Wo[:, k, half * 512:(half + 1) * 512], start=(k == 0), stop=(k == 7))
                    tt(xt[0:n, half * 512:(half + 1) * 512], xt[0:n, half * 512:(half + 1) * 512], p[0:n, :], ALU.add)
                if l == depth - 1:
                    final_norm(t0, n)
                else:
                    S.dma("sp", dst[t0:t0 + n, :], xt[0:n])

    pf = {"ssd": False, "gdn": False, "rw": False}

    def nxt_of(ci):
        if ci + 1 < len(chunks):
            return chunks[ci + 1]
        return (TP, NS) if with_samples else None

    def ssd_pre(l, t0, c, part=None):
        part = part or ("z" if pf["ssd"] else "all")
        if part in ("all", "in"):
            S.dma("sp", cbs[:, :, 3:3 + c], projT[C_SSM + 512:C_SSM + 1536, t0:t0 + c].rearrange("(k p) t -> p k t", p=128))
            S.dma("sp", dtraw[:, 0:c], projT[C_SSM + 1536:C_SSM + 1544, t0:t0 + c])
        if part in ("all", "z"):
            S.dma("sp", zT[:, :, 0:c], projT[C_SSM:C_SSM + 512, t0:t0 + c].rearrange("(k p) t -> p k t", p=128))
            act(zT[:, :, 0:c], zT[:, :, 0:c], AF.Silu)
            pf["ssd"] = False
        if part == "in":
            pf["ssd"] = True
        dt_ps[0] = dtraw

    dt_ps = [None]

    def ssd_dt(c):
        p = dt_ps[0]
        act(dtT[:, 0, 0:c], p[0:8, 0:c], AF.Exp, bias=pc("dt_bias", 0, 8))
        act(dtT[:, 0, 0:c], dtT[:, 0, 0:c], AF.Ln, bias=epsc[0:8, 1:2])
        ts(dtT[:, 1, 0:c], dtT[:, 0, 0:c], acol[:, 1:2], ALU.mult)

    def conv_silu(cb, w_name, nk, c, accv, tmpv, outv, bias_name=None):
        W4 = pcs(w_name, 4 * nk)
        for i in range(4):
            wv = bc(W4[:, i * nk:(i + 1) * nk].unsqueeze(2), [128, nk, c])
            if i == 0:
                tt(accv, cb[:, :, 0:c], wv, ALU.mult)
            else:
                tt(tmpv, cb[:, :, i:i + c], wv, ALU.mult, eng="pool")
                tt(accv, accv, tmpv, ALU.add)
        if bias_name is not None:
            tt(accv, accv, bc(pcs(bias_name, nk).unsqueeze(2), [128, nk, c]), ALU.add)
        act(outv, accv, AF.Silu)

    def ssd_chunk(l, t0, c, first, nxt=None):
        ssd_pre(l, t0, c)
        conv_silu(cbs, "sconv_w", 8, c, cacc[:, :, 0:c], ctmp[:, :, 0:c], xsT[:, :, 0:c], "sconv_b")
        ssd_dt(c)
        cp(ctmp[:, :, 0:3], cbs[:, :, c:c + 3], eng="pool")
        cp(cbs[:, :, 0:3], ctmp[:, :, 0:3], eng="pool")
        if nxt is not None:
            ssd_pre(l, nxt[0], nxt[1], "in")
        ssd_core(c, 0, first)

    def ssd_core(c, o, first):
        p = P()
        tr(p[0:c, 0:8], dtT[:, 0, o:o + c], 8)
        tr(p[0:c, 8:16], dtT[:, 1, o:o + c], 8)
        cp(tok8[0:c], p[0:c, 0:16])
        p = P()
        pv = p.rearrange("p (a b) -> p a b", b=128)
        for j in range(4):
            tr(pv[0:c, j, :], xsT[:, j, o:o + c], 128)
        cp(xTok[0:c].rearrange("p a b -> p (a b)"), p[0:c, :], eng="act")
        p = P()
        for j in range(2):
            tr(p[0:c, j * 128:(j + 1) * 128], xsT[:, 4 + j, o:o + c], 128)
        cp(BTok[0:c], p[0:c, 0:256], eng="act")
        p = P()
        mm(p[0:c, 0:8], triu[0:c, 0:c], tok8[0:c, 8:16])
        cp(acs[0:c, 0:8], p[0:c, 0:8])
        ts(acs[0:c, 8:16], p[0:c, 0:8], -1.0, ALU.mult)
        act(acs[0:c, 16:24], p[0:c, 0:8], AF.Exp)
        tt(Rt[0:c, :, 0:c], bc(tok8[0:c, 8:16].unsqueeze(2), [c, 8, c]),
           bc(triu[0:c, 0:c].unsqueeze(1), [c, 8, c]), ALU.mult)
        pa = P()
        pav = pa.rearrange("p (a b) -> p a b", b=64)
        if c == 64:
            mm(pa[:, :], ones[0:c, :], Rt[0:c].rearrange("p a b -> p (a b)"), start=True, stop=False)
            mm(pa[0:64, :], ident[0:64, 0:64], negU8.rearrange("p a b -> p (a b)"), start=False, stop=True)
        else:
            for h in range(8):
                mm(pav[:, h, 0:c], ones[0:c, :], Rt[0:c, h, 0:c], start=True, stop=False)
                mm(pav[0:c, h, 0:c], ident[0:c, 0:c], negU[0:c, 0:c], start=False, stop=True)
        tt(segT[0:c, :, 0:c], pav[0:c, :, 0:c], bc(acs[0:c, 8:16].unsqueeze(2), [c, 8, c]), ALU.add)
        act(segT[0:c, :, 0:c], segT[0:c, :, 0:c], AF.Exp)
        act(totb[:, :], pav[:, :, c - 1], AF.Exp)
        tt(acs[0:c, 24:32], pav[0:c, :, c - 1], acs[0:c, 8:16], ALU.add)
        act(acs[0:c, 24:32], acs[0:c, 24:32], AF.Exp)
        tt(xdt[0:c], xTok[0:c], bc(tok8[0:c, 0:8].unsqueeze(2), [c, 8, 64]), ALU.mult)
        tt(xdtd[0:c], xdt[0:c], bc(acs[0:c, 24:32].unsqueeze(2), [c, 8, 64]), ALU.mult, eng="pool")
        pc_ = P()
        pcv = pc_.rearrange("p (a b) -> p a b", b=64)
        for g in range(2):
            mm(pcv[0:c, g, 0:c], xsT[:, 4 + g, o:o + c], xsT[:, 6 + g, o:o + c])
        for g in range(2):
            tt(MT[0:c, g * 4:(g + 1) * 4, 0:c], segT[0:c, g * 4:(g + 1) * 4, 0:c],
               bc(pcv[0:c, g:g + 1, 0:c], [c, 4, c]), ALU.mult)
        p1 = P()
        p1v = p1.rearrange("p (a b) -> p a b", b=64)
        for h in range(8):
            mm(p1v[0:c, h, :], MT[0:c, h, 0:c], xdt[0:c, h, :])
        cp(y1s[0:c], p1[0:c, :], eng="act")
        p2 = P()
        if not first:
            for g in range(2):
                mm(p2[0:c, g * 256:(g + 1) * 256], xsT[:, 6 + g, o:o + c],
                   hst[:, g * 4:(g + 1) * 4, :].rearrange("p a b -> p (a b)"))
            tt(ytok[0:c], p2[0:c, :].rearrange("p (a b) -> p a b", b=64),
               bc(acs[0:c, 16:24].unsqueeze(2), [c, 8, 64]), ALU.mult)
            tt(ytok[0:c], ytok[0:c], y1s[0:c].rearrange("p (a b) -> p a b", b=64), ALU.add)
            ysrc = ytok
        else:
            ysrc = y1s.rearrange("p (a b) -> p a b", b=64)
        p3 = P()
        for g in range(2):
            mm(p3[:, g * 256:(g + 1) * 256], BTok[0:c, g * 128:(g + 1) * 128],
               xdtd[0:c, g * 4:(g + 1) * 4, :].rearrange("p a b -> p (a b)"))
        if first:
            cp(hst.rearrange("p a b -> p (a b)"), p3[:, :])
        else:
            tt(hst, hst, bc(totb.unsqueeze(2), [128, 8, 64]), ALU.mult)
            tt(hst.rearrange("p a b -> p (a b)"), hst.rearrange("p a b -> p (a b)"), p3[:, :], ALU.add)
        p4 = P()
        p4v = p4.rearrange("p (a b) -> p a b", b=64)
        for j in range(4):
            tr(p4v[:, j, 0:c], ysrc[0:c, 2 * j:2 * j + 2, :].rearrange("p a b -> p (a b)"), c)
        cp(yT[:, :, o:o + c], p4v[:, 0:4, 0:c], eng="act")

    def ssd_post(l, t0, c, bi):
        tt(sq[:, :, 0:c], xsT[:, 0:4, 0:c], bc(pcs("ssm_d", 4).unsqueeze(2), [128, 4, c]), ALU.mult)
        tt(yT[:, :, 0:c], yT[:, :, 0:c], sq[:, :, 0:c], ALU.add)
        tt(yT[:, :, 0:c], yT[:, :, 0:c], zT[:, :, 0:c], ALU.mult)
        tt(sq[:, :, 0:c], yT[:, :, 0:c], yT[:, :, 0:c], ALU.mult)
        p = P()
        pv = p.rearrange("p (a b) -> p a b", b=64)
        for g in range(2):
            mm(pv[:, g, 0:c], ones[:, :], sq[:, 2 * g, 0:c], start=True, stop=False)
            mm(pv[:, g, 0:c], ones[:, :], sq[:, 2 * g + 1, 0:c], start=False, stop=True)
        act(rs[:, :, 0:c], pv[:, 0:2, 0:c], AF.Ln, bias=epsc[:, 2:3], scale=1.0 / 256)
        act(rs[:, :, 0:c], rs[:, :, 0:c], AF.Exp, scale=-0.5)
        for g in range(2):
            tt(yT[:, 2 * g:2 * g + 2, 0:c], yT[:, 2 * g:2 * g + 2, 0:c], bc(rs[:, g:g + 1, 0:c], [128, 2, c]), ALU.mult)
        tt(yT[:, :, 0:c], yT[:, :, 0:c], bc(pcs("snorm_w", 4).unsqueeze(2), [128, 4, c]), ALU.mult)
        S.dma("sp", yb[bi][:, t0:t0 + c].rearrange("(k p) t -> p k t", p=128), yT[:, :, 0:c])

    def ssd_branch(l, bi):
        act(acol[:, 0:1], pc("a_log", 0, 8), AF.Exp)
        ts(acol[:, 1:2], acol[:, 0:1], -1.0, ALU.mult)
        mset(cbs[:, :, 0:3], 0.0)
        for ci, (t0, c) in enumerate(chunks):
            ssd_chunk(l, t0, c, ci == 0, nxt_of(ci))
            ssd_post(l, t0, c, bi)
        ssd_state_out(p_ssm[l], p_sconv[l])
        if with_samples:
            c = NS
            ssd_pre(l, TP, c)
            S.dma("sp", B1[0:48, 0:1024], st_sconv[l].rearrange("b i c -> (b i) c"))
            p = P()
            for k in range(8):
                tr(p[:, k * 48:(k + 1) * 48], B1[0:48, k * 128:(k + 1) * 128], 48)
            sbT = B2[:, 0:384].rearrange("p (k b i) -> p k b i", k=8, i=3)
            cp(B2[:, 0:384], p[:, 0:384])
            W4 = pcs("sconv_w", 32)
            for i in range(4):
                src = sbT[:, :, :, i] if i < 3 else cbs[:, :, 3:3 + c]
                wv = bc(W4[:, i * 8:(i + 1) * 8].unsqueeze(2), [128, 8, c])
                if i == 0:
                    tt(cacc[:, :, 0:c], src, wv, ALU.mult)
                else:
                    tt(ctmp[:, :, 0:c], src, wv, ALU.mult, eng="pool")
                    tt(cacc[:, :, 0:c], cacc[:, :, 0:c], ctmp[:, :, 0:c], ALU.add)
            tt(cacc[:, :, 0:c], cacc[:, :, 0:c], bc(pcs("sconv_b", 8).unsqueeze(2), [128, 8, c]), ALU.add)
            act(xsT[:, :, 0:c], cacc[:, :, 0:c], AF.Silu)
            ssd_dt(c)
            S.dma("sp", s_sconv[l, :, 0:2, :], st_sconv[l, :, 1:3, :])
            for half in range(2):
                p = P()
                for j in range(4):
                    tr(p[0:c, j * 128:(j + 1) * 128], cbs[:, half * 4 + j, 3:3 + c], 128)
                cp(B1[0:c, half * 512:(half + 1) * 512], p[0:c, :], eng="act")
            S.dma("sp", s_sconv[l, :, 2, :], B1[0:c, 0:1024])
            stg = g8(17, 64, 128)
            stg2 = g8(18, 64, 128)

            def ld_state(b):
                S.dma("sp", stg[:, 0:4, :], st_ssm[l, b, 0:4].rearrange("h p n -> p h n"))
                S.dma("sp", stg2[:, 0:4, :], st_ssm[l, b, 4:8].rearrange("h p n -> p h n"))
            ld_state(0)
            for b in range(NS):
                p = P()
                pv = p.rearrange("p (a b) -> p a b", b=64)
                for h in range(4):
                    tr(pv[:, h, :], stg[:, h, :], 64)
                    tr(pv[:, 4 + h, :], stg2[:, h, :], 64)
                cp(hst, pv)
                if b + 1 < NS:
                    ld_state(b + 1)
                ssd_core(1, b, False)
                ssd_state_out(s_ssm[l, b], None, b)
            ssd_post(l, TP, c, bi)

    def ssd_state_out(dst_st, dst_conv, par=0):
        for half in range(2):
            p = P()
            for h in range(4):
                tr(p[0:64, h * 128:(h + 1) * 128], hst[:, half * 4 + h, :], 128)
            so = G[21 + 2 * (par % 2) + half][0:64, :]
            cp(so, p[0:64, :], eng=("act" if half else "dve"))
            S.dma("sp", dst_st[half * 4:(half + 1) * 4].rearrange("h p n -> p h n"), so.rearrange("p (h n) -> p h n", n=128))
        if dst_conv is None:
            return
        for k in range(8):
            p = P()
            tr(p[0:3, 0:128], cbs[:, k, 0:3], 128)
            cp(y1s[0:3, 0:128], p[0:3, 0:128])
            S.dma("sp", dst_conv[:, k * 128:(k + 1) * 128], y1s[0:3, 0:128])

    def nsteps_for(c):
        n = 0
        while (1 << (n + 1)) < c:
            n += 1
        return n

    def cs_bc(R, nh, c, mask_nh, mask2d):
        p = P()
        pv = p.rearrange("p (a b) -> p a b", b=64)
        if c == 64:
            mm(p[:, 0:nh * 64], ones[0:c, :], R[0:c].rearrange("p a b -> p (a b)"), start=True, stop=False)
            mm(p[0:64, 0:nh * 64], ident[0:64, 0:64], mask_nh.rearrange("p a b -> p (a b)"), start=False, stop=True)
        else:
            for h in range(nh):
                mm(pv[:, h, 0:c], ones[0:c, :], R[0:c, h, 0:c], start=True, stop=False)
                mm(pv[0:c, h, 0:c], ident[0:c, 0:c], mask2d[0:c, 0:c], start=False, stop=True)
        return pv

    def solve(X, XT, Xb, XTb, Sm, nh, c):
        tt(Sm[0:c, :, 0:c], XT[0:c, :, 0:c], bc(ident[0:c, 0:c].unsqueeze(1), [c, nh, c]), ALU.add)
        ns = nsteps_for(c)
        cur, curT, nxt, nxtT = X, XT, Xb, XTb
        for s_ in range(ns):
            last = s_ == ns - 1
            pX = P()
            pXv = pX.rearrange("p (a b) -> p a b", b=64)
            for h in range(nh):
                mm(pXv[0:c, h, 0:c], curT[0:c, h, 0:c], cur[0:c, h, 0:c])
            cp(nxt[0:c, :, 0:c], pXv[0:c, 0:nh, 0:c], eng="act")
            if not last:
                pT = P()
                pTv = pT.rearrange("p (a b) -> p a b", b=64)
                for h in range(nh):
                    mm(pTv[0:c, h, 0:c], cur[0:c, h, 0:c], curT[0:c, h, 0:c])
                cp(nxtT[0:c, :, 0:c], pTv[0:c, 0:nh, 0:c])
            pS = P()
            pSv = pS.rearrange("p (a b) -> p a b", b=64)
            for h in range(nh):
                mm(pSv[0:c, h, 0:c], nxt[0:c, h, 0:c], Sm[0:c, h, 0:c])
            tt(Sm[0:c, :, 0:c], Sm[0:c, :, 0:c], pSv[0:c, 0:nh, 0:c], ALU.add)
            cur, curT, nxt, nxtT = nxt, nxtT, cur, curT

    XA = g8(21, 64); XTA = g8(22, 64); XB = g8(23, 64); XTB = g8(24, 64); SM = g8(25, 64)

    cbg = B0[:, 0:804].rearrange("p (a b) -> p a b", b=67)
    gacc = B1[:, 0:768].rearrange("p (a b) -> p a b", b=64)
    gtm = B2[:, 0:768].rearrange("p (a b) -> p a b", b=64)
    qkv = B3[:, 0:768].rearrange("p (a b) -> p a b", b=64)
    gbT = sb([4, 2, 64], name="gbT")
    gacol = sb([4, 2], name="gacol")
    gbTok = sb([64, 8], name="gbTok")
    gst = sb([64, 32], name="gst")
    gtotb = sb([128, 4], name="gtotb")
    Rg = g4(0, 64); decT = g4(1, 64); dec2 = g4(2, 64); qkT = g4(3, 64)
    kTok = g8(4, 64, 128); vTok = g8(5, 64, 128); rvk = g8(6, 64, 128); rkk = g8(7, 64, 128); kdk = g8(8, 64, 128)
    usb = g8(9, 64, 128); vnew = g8(11, 64, 128); osb = g8(12, 64, 128); o2s = g8(16, 64, 128)
    wcT = g4(17)
    Sg = g8(18, 128, 128)
    gsq = g8(19); grs = g8(20)

    def gdn_chunk(l, t0, c, first, nxt=None):
        gdn_pre(l, t0, c)
        gdn_chunk2(l, t0, c, first, nxt)

    def gdn_pre(l, t0, c, part=None):
        part = part or ("z" if pf["gdn"] else "all")
        if part in ("all", "in"):
            S.dma("sp", cbg[:, :, 3:3 + c], projT[C_GDN:C_GDN + 1536, t0:t0 + c].rearrange("(k p) t -> p k t", p=128))
            S.dma("sp", garaw[:, 0:c], projT[C_GDN + 2048:C_GDN + 2052, t0:t0 + c])
            S.dma("sp", gbraw[:, 0:c], projT[C_GDN + 2052:C_GDN + 2056, t0:t0 + c])
        if part in ("all", "z"):
            S.dma("sp", zT[:, :, 0:c], projT[C_GDN + 1536:C_GDN + 2048, t0:t0 + c].rearrange("(k p) t -> p k t", p=128))
            act(zT[:, :, 0:c], zT[:, :, 0:c], AF.Silu)
            pf["gdn"] = False
        if part == "in":
            pf["gdn"] = True
        g_ps[0] = garaw
        g_ps[1] = gbraw

    g_ps = [None, None]

    def gdn_gb(c):
        act(gbT[:, 1, 0:c], g_ps[1][0:4, 0:c], AF.Sigmoid)
        p = g_ps[0]
        act(gbT[:, 0, 0:c], p[0:4, 0:c], AF.Exp, bias=pc("gdt_bias", 0, 4))
        act(gbT[:, 0, 0:c], gbT[:, 0, 0:c], AF.Ln, bias=epsc[0:4, 1:2])
        ts(gbT[:, 0, 0:c], gbT[:, 0, 0:c], gacol[:, 1:2], ALU.mult)

    def gdn_chunk2(l, t0, c, first, nxt=None):
        conv_silu(cbg, "gconv_w", 12, c, gacc[:, :, 0:c], gtm[:, :, 0:c], qkv[:, :, 0:c], None)
        gdn_gb(c)
        cp(gtm[:, :, 0:3], cbg[:, :, c:c + 3], eng="pool")
        cp(cbg[:, :, 0:3], gtm[:, :, 0:3], eng="pool")
        if nxt is not None:
            gdn_pre(l, nxt[0], nxt[1], "in")
        gdn_norm(c)
        gdn_core(c, 0, first)

    def gdn_norm(c):
        tt(gsq[:, :, 0:c], qkv[:, 0:8, 0:c], qkv[:, 0:8, 0:c], ALU.mult)
        p = P()
        pv = p.rearrange("p (a b) -> p a b", b=64)
        for j in range(8):
            mm(pv[:, j, 0:c], ones[:, :], gsq[:, j, 0:c])
        act(grs[:, :, 0:c], pv[:, :, 0:c], AF.Ln, bias=epsc[:, 0:1])
        act(grs[:, :, 0:c], grs[:, :, 0:c], AF.Exp, scale=-0.5)
        tt(qkv[:, 0:8, 0:c], qkv[:, 0:8, 0:c], grs[:, :, 0:c], ALU.mult)
        ts(qkv[:, 0:4, 0:c], qkv[:, 0:4, 0:c], float(128 ** -0.5), ALU.mult)

    def gdn_core(c, o, first):
        p = P()
        pv = p.rearrange("p (a b) -> p a b", b=128)
        for h in range(4):
            tr(pv[0:c, h, :], qkv[:, 4 + h, o:o + c], 128)
        cp(kTok[0:c], pv[0:c], eng="act")
        p = P()
        pv = p.rearrange("p (a b) -> p a b", b=128)
        for h in range(4):
            tr(pv[0:c, h, :], qkv[:, 8 + h, o:o + c], 128)
        cp(vTok[0:c], pv[0:c], eng="act")
        p = P()
        tr(p[0:c, 0:4], gbT[:, 0, o:o + c], 4)
        tr(p[0:c, 4:8], gbT[:, 1, o:o + c], 4)
        cp(gbTok[0:c], p[0:c, 0:8])
        p = P()
        mm(p[0:c, 0:4], triu[0:c, 0:c], gbTok[0:c, 0:4])
        cp(gst[0:c, 0:4], p[0:c, 0:4])
        ts(gst[0:c, 4:8], p[0:c, 0:4], -1.0, ALU.mult)
        act(gst[0:c, 8:12], p[0:c, 0:4], AF.Exp)
        tt(gst[0:c, 16:20], gst[0:c, 8:12], gbTok[0:c, 4:8], ALU.mult)
        ts(gst[0:c, 20:24], gbTok[0:c, 4:8], -1.0, ALU.mult)
        tt(Rg[0:c, :, 0:c], bc(gbTok[0:c, 0:4].unsqueeze(2), [c, 4, c]), bc(triu[0:c, 0:c].unsqueeze(1), [c, 4, c]), ALU.mult)
        pa = cs_bc(Rg, 4, c, negU8[:, 0:4, :], negU)
        tt(decT[0:c, :, 0:c], pa[0:c, 0:4, 0:c], bc(gst[0:c, 4:8].unsqueeze(2), [c, 4, c]), ALU.add)
        act(decT[0:c, :, 0:c], decT[0:c, :, 0:c], AF.Exp)
        act(gtotb[:, :], pa[:, 0:4, c - 1], AF.Exp)
        tt(gst[0:c, 12:16], pa[0:c, 0:4, c - 1], gst[0:c, 4:8], ALU.add)
        act(gst[0:c, 12:16], gst[0:c, 12:16], AF.Exp)
        pb = cs_bc(Rg, 4, c, posL4, posLs)
        tt(dec2[0:c, :, 0:c], pb[0:c, 0:4, 0:c], bc(gst[0:c, 4:8].unsqueeze(2), [c, 4, c]), ALU.add)
        act(dec2[0:c, :, 0:c], dec2[0:c, :, 0:c], AF.Exp, scale=-1.0)
        pG = P()
        pGv = pG.rearrange("p (a b) -> p a b", b=64)
        pQ = P()
        pQv = pQ.rearrange("p (a b) -> p a b", b=64)
        for h in range(4):
            mm(pGv[0:c, h, 0:c], qkv[:, 4 + h, o:o + c], qkv[:, 4 + h, o:o + c])
            mm(pQv[0:c, h, 0:c], qkv[:, 4 + h, o:o + c], qkv[:, h, o:o + c])
        tt(qkT[0:c, :, 0:c], pQv[0:c, 0:4, 0:c], decT[0:c, :, 0:c], ALU.mult)
        tt(XA[0:c, 0:4, 0:c], pGv[0:c, 0:4, 0:c], dec2[0:c, :, 0:c], ALU.mult)
        tt(XA[0:c, 0:4, 0:c], XA[0:c, 0:4, 0:c], bc(gst[0:c, 20:24].unsqueeze(2), [c, 4, c]), ALU.mult)
        p = P()
        pv = p.rearrange("p (a b) -> p a b", b=64)
        for h in range(4):
            tr(pv[0:c, h, 0:c], XA[0:c, h, 0:c], c)
        cp(XTA[0:c, 0:4, 0:c], pv[0:c, 0:4, 0:c])
        solve(XA[:, 0:4], XTA[:, 0:4], XB[:, 0:4], XTB[:, 0:4], SM[:, 0:4], 4, c)
        tt(rvk[0:c], vTok[0:c], bc(gbTok[0:c, 4:8].unsqueeze(2), [c, 4, 128]), ALU.mult)
        tt(rkk[0:c], kTok[0:c], bc(gst[0:c, 16:20].unsqueeze(2), [c, 4, 128]), ALU.mult, eng="pool")
        tt(kdk[0:c], kTok[0:c], bc(gst[0:c, 12:16].unsqueeze(2), [c, 4, 128]), ALU.mult, eng="pool")
        pU = P()
        pUv = pU.rearrange("p (a b) -> p a b", b=128)
        for h in range(4):
            mm(pUv[0:c, h, :], SM[0:c, h, 0:c], rvk[0:c, h, :])
        cp(usb[0:c], pUv[0:c], eng="act")
        pW = P()
        pWv = pW.rearrange("p (a b) -> p a b", b=64)
        for h in range(4):
            mm(pWv[:, h, 0:c], rkk[0:c, h, :], SM[0:c, h, 0:c])
        cp(wcT[:, :, 0:c], pWv[:, 0:4, 0:c], eng="act")
        if not first:
            pws = P()
            pwv = pws.rearrange("p (a b) -> p a b", b=128)
            for h in range(4):
                mm(pwv[0:c, h, :], wcT[:, h, 0:c], Sg[:, h, :])
            tt(vnew[0:c], usb[0:c], pwv[0:c], ALU.subtract)
            vn = vnew
        else:
            vn = usb
        pO2 = P()
        pO2v = pO2.rearrange("p (a b) -> p a b", b=128)
        for h in range(4):
            mm(pO2v[0:c, h, :], qkT[0:c, h, 0:c], vn[0:c, h, :])
        if not first:
            cp(o2s[0:c], pO2v[0:c], eng="act")
            pO1 = P()
            pO1v = pO1.rearrange("p (a b) -> p a b", b=128)
            for h in range(4):
                mm(pO1v[0:c, h, :], qkv[:, h, o:o + c], Sg[:, h, :])
            tt(osb[0:c], pO1v[0:c], bc(gst[0:c, 8:12].unsqueeze(2), [c, 4, 128]), ALU.mult)
            tt(osb[0:c], osb[0:c], o2s[0:c], ALU.add)
        else:
            cp(osb[0:c], pO2v[0:c], eng="act")
        pS = P()
        pSv = pS.rearrange("p (a b) -> p a b", b=128)
        for h in range(4):
            mm(pSv[:, h, :], kdk[0:c, h, :], vn[0:c, h, :])
        if first:
            cp(Sg, pSv)
        else:
            tt(Sg, Sg, bc(gtotb.unsqueeze(2), [128, 4, 128]), ALU.mult)
            tt(Sg, Sg, pSv, ALU.add)
        p4 = P()
        p4v = p4.rearrange("p (a b) -> p a b", b=64)
        for h in range(4):
            tr(p4v[:, h, 0:c], osb[0:c, h, :], c)
        cp(yT[:, :, o:o + c], p4v[:, 0:4, 0:c], eng="act")

    def gdn_post(l, t0, c, bi):
        tt(sq[:, :, 0:c], yT[:, :, 0:c], yT[:, :, 0:c], ALU.mult)
        p = P()
        pv = p.rearrange("p (a b) -> p a b", b=64)
        for h in range(4):
            mm(pv[:, h, 0:c], ones[:, :], sq[:, h, 0:c])
        act(sq[:, :, 0:c], pv[:, 0:4, 0:c], AF.Ln, bias=epsc[:, 0:1], scale=1.0 / 128)
        act(sq[:, :, 0:c], sq[:, :, 0:c], AF.Exp, scale=-0.5)
        tt(yT[:, :, 0:c], yT[:, :, 0:c], sq[:, :, 0:c], ALU.mult)
        tt(yT[:, :, 0:c], yT[:, :, 0:c], zT[:, :, 0:c], ALU.mult)
        ts(yT[:, :, 0:c], yT[:, :, 0:c], pc("gnorm_w"), ALU.mult)
        S.dma("sp", yb[bi][:, t0:t0 + c].rearrange("(k p) t -> p k t", p=128), yT[:, :, 0:c])

    def gdn_branch(l, bi):
        nonlocal Sg
        act(gacol[:, 0:1], pc("ga_log", 0, 4), AF.Exp)
        ts(gacol[:, 1:2], gacol[:, 0:1], -1.0, ALU.mult)
        mset(cbg[:, :, 0:3], 0.0)
        for ci, (t0, c) in enumerate(chunks):
            gdn_chunk(l, t0, c, ci == 0, nxt_of(ci))
            gdn_post(l, t0, c, bi)
        gdn_state_out(p_gdn[l], p_gconv[l])
        if with_samples:
            c = NS
            gdn_pre(l, TP, c)
            S.dma("sp", B1[0:48, 0:1536], st_gconv[l].rearrange("b i c -> (b i) c"))
            sbT = B2[:, 0:576].rearrange("p (k b i) -> p k b i", k=12, i=3)
            for half in range(2):
                p = P()
                for k in range(6):
                    kk_ = half * 6 + k
                    tr(p[:, k * 48:(k + 1) * 48], B1[0:48, kk_ * 128:(kk_ + 1) * 128], 48)
                cp(B2[:, half * 288:(half + 1) * 288], p[:, 0:288])
            W4 = pcs("gconv_w", 48)
            g16 = G[26][:, 0:192].rearrange("p (a b) -> p a b", b=16)
            t16 = G[27][:, 0:192].rearrange("p (a b) -> p a b", b=16)
            for i in range(4):
                src = sbT[:, :, :, i] if i < 3 else cbg[:, :, 3:3 + c]
                wv = bc(W4[:, i * 12:(i + 1) * 12].unsqueeze(2), [128, 12, c])
                if i == 0:
                    tt(g16, src, wv, ALU.mult)
                else:
                    tt(t16, src, wv, ALU.mult, eng="pool")
                    tt(g16, g16, t16, ALU.add)
            act(qkv[:, :, 0:c], g16, AF.Silu)
            gdn_gb(c)
            S.dma("sp", s_gconv[l, :, 0:2, :], st_gconv[l, :, 1:3, :])
            for g3 in range(3):
                p = P()
                for j in range(4):
                    tr(p[0:c, j * 128:(j + 1) * 128], cbg[:, g3 * 4 + j, 3:3 + c], 128)
                cp(B1[0:c, g3 * 512:(g3 + 1) * 512], p[0:c, :], eng="act")
            S.dma("sp", s_gconv[l, :, 2, :], B1[0:c, 0:1536])
            gdn_norm(c)
            SGS = [g8(18, 128, 128), g8(28, 128, 128)]
            S.dma("sp", SGS[0], st_gdn[l, 0].rearrange("h k v -> k h v"))
            for b in range(NS):
                if b + 1 < NS:
                    S.dma("sp", SGS[(b + 1) % 2], st_gdn[l, b + 1].rearrange("h k v -> k h v"))
                Sg = SGS[b % 2]
                gdn_core(1, b, False)
                gdn_state_out(s_gdn[l, b], None)
            Sg = SGS[0]
            gdn_post(l, TP, c, bi)

    def gdn_state_out(dst_st, dst_conv):
        for h in range(4):
            S.dma("sp", dst_st[h], Sg[:, h, :])
        if dst_conv is None:
            return
        for k in range(12):
            p = P()
            tr(p[0:3, 0:128], cbg[:, k, 0:3], 128)
            cp(y1s[0:3, 0:128], p[0:3, 0:128])
            S.dma("sp", dst_conv[:, k * 128:(k + 1) * 128], y1s[0:3, 0:128])

    prw = B0[0:64, 0:1690].rearrange("p (a b) -> p a b", b=65)
    uT = B1[0:64, 0:1664].rearrange("p (a b) -> p a b", b=64)
    dsh = B2[0:64, 0:1664].rearrange("p (a b) -> p a b", b=64)
    zr = g8(0, 64)
    w2sb = G[30][0:64, :]
    a2sb = G[31][0:64, :]
    thw = sb([64, 64], name="thw")
    ldT = g8(1, 64); aT = g8(2, 64); kkT = g8(3, 64); kT_ = g8(4, 64); t8a = g8(5, 64); t8b = g8(6, 64)
    bon = g8(7, 64); lw = g8(8, 64)
    rmask = sb([64, 8, 64], name="rmask")
    mset(rmask, 1.0)
    mset(rmask[:, :, 0:1], 0.0)
    Wt = g8(9, 64); rt = g8(11, 64); at = g8(12, 64); bt = g8(13, 64); kt = g8(14, 64); bh = g8(15, 64); kh = g8(16, 64)
    BhTok = g8(17, 64); KhTok = g8(18, 64); VTok = g8(19, 64); AtTok = g8(20, 64)
    LakT = g8(26, 64); MrbT = g8(27, 64); MrkT = g8(28, 64); STs = g8(29, 64)
    Zsb = g8(5, 64); U1 = g8(6, 64); Usb = g8(1, 64); PTs = g8(2, 64); ytk = g8(3, 64); yrT = g8(4, 64)
    ybf8 = sb([64, 8, 64], BF16, name="ybf8")

    def mm8(pv, lhs, rhs, c, msk, out, o=0):
        for h in range(8):
            mm(pv[0:c, h, 0:c], lhs[:, h, o:o + c], rhs[:, h, o:o + c])
        tt(out[0:c, :, 0:c], pv[0:c, :, 0:c], bc(msk[0:c, 0:c].unsqueeze(1), [c, 8, c]), ALU.mult)

    def to_tok(dst, src, c, o=0):
        p = P()
        pv = p.rearrange("p (a b) -> p a b", b=64)
        for h in range(8):
            tr(pv[0:c, h, :], src[:, h, o:o + c], 64)
        cp(dst[0:c], pv[0:c], eng="act")

    def rw_chunk(l, t0, c, first, nxt=None):
        rw_pre(l, t0, c)
        rw_chunk2(l, t0, c, first, nxt)

    def rw_pre(l, t0, c, part=None):
        part = part or ("z" if pf["rw"] else "all")
        if part in ("all", "in"):
            S.dma("sp", prw[:, :, 1:1 + c], projT[0:1664, t0:t0 + c].rearrange("(j p) t -> p j t", p=64))
        if part in ("all", "z"):
            S.dma("sp", zr[:, :, 0:c], projT[1664:2176, t0:t0 + c].rearrange("(j p) t -> p j t", p=64))
            act(zr[:, :, 0:c], zr[:, :, 0:c], AF.Silu)
            pf["rw"] = False
        if part == "in":
            pf["rw"] = True

    def rw_chunk2(l, t0, c, first, nxt=None):
        rw_shift(c, None)
        if nxt is not None:
            rw_pre(l, nxt[0], nxt[1], "in")
        rw_elem(c, False)
        rw_core(c, 0, first)

    def rw_shift(c, shT):
        tt(dsh[:, :, 0:c], (prw[:, :, 0:c] if shT is None else shT), prw[:, :, 1:1 + c], ALU.subtract)
        tt(dsh[:, :, 0:c], dsh[:, :, 0:c], bc(pcs("mu", 26, 64).unsqueeze(2), [64, 26, c]), ALU.mult)
        tt(uT[:, :, 0:c], dsh[:, :, 0:c], prw[:, :, 1:1 + c], ALU.add)
        cp(dsh[:, :, 0:1], prw[:, :, c:c + 1], eng="pool")
        cp(prw[:, :, 0:1], dsh[:, :, 0:1], eng="pool")

    def rw_elem(c, sample):
        r_, k0, v_ = uT[:, 0:8], uT[:, 8:16], uT[:, 16:24]
        act(thw[:, 0:c], uT[:, 24, 0:c], AF.Tanh)
        p = P()
        pv = p.rearrange("p (a b) -> p a b", b=64)
        for h in range(8):
            mm(pv[0:64, h, 0:c], w2sb[:, h * 64:(h + 1) * 64], thw[:, 0:c])
        tt(ldT[:, :, 0:c], pv[0:64, :, 0:c], bc(pcs("w0", 8, 64).unsqueeze(2), [64, 8, c]), ALU.add)
        act(ldT[:, :, 0:c], ldT[:, :, 0:c], AF.Sigmoid)
        ts(ldT[:, :, 0:c], ldT[:, :, 0:c], -EXPM05, ALU.mult)
        p = P()
        pv = p.rearrange("p (a b) -> p a b", b=64)
        for h in range(8):
            mm(pv[0:64, h, 0:c], a2sb[:, h * 64:(h + 1) * 64], uT[:, 25, 0:c])
        tt(aT[:, :, 0:c], pv[0:64, :, 0:c], bc(pcs("a0", 8, 64).unsqueeze(2), [64, 8, c]), ALU.add)
        act(aT[:, :, 0:c], aT[:, :, 0:c], AF.Sigmoid)
        tt(kkT[:, :, 0:c], k0[:, :, 0:c], bc(pcs("k_k", 8, 64).unsqueeze(2), [64, 8, c]), ALU.mult)
        tt(t8a[:, :, 0:c], kkT[:, :, 0:c], kkT[:, :, 0:c], ALU.mult)
        p = P()
        pv = p.rearrange("p (a b) -> p a b", b=64)
        for h in range(8):
            mm(pv[0:64, h, 0:c], ones[0:64, 0:64], t8a[:, h, 0:c])
        act(t8a[:, :, 0:c], pv[0:64, :, 0:c], AF.Ln, bias=epsc[0:64, 0:1])
        act(t8a[:, :, 0:c], t8a[:, :, 0:c], AF.Exp, scale=-0.5)
        tt(kkT[:, :, 0:c], kkT[:, :, 0:c], t8a[:, :, 0:c], ALU.mult)
        ts(t8a[:, :, 0:c], aT[:, :, 0:c], -1.0, ALU.add)
        tt(t8a[:, :, 0:c], t8a[:, :, 0:c], bc(pcs("k_a", 8, 64).unsqueeze(2), [64, 8, c]), ALU.mult)
        stt(kT_[:, :, 0:c], t8a[:, :, 0:c], 1.0, k0[:, :, 0:c], ALU.add, ALU.mult)
        tt(t8a[:, :, 0:c], r_[:, :, 0:c], kT_[:, :, 0:c], ALU.mult)
        tt(t8a[:, :, 0:c], t8a[:, :, 0:c], bc(pcs("r_k", 8, 64).unsqueeze(2), [64, 8, c]), ALU.mult)
        p = P()
        pv = p.rearrange("p (a b) -> p a b", b=64)
        for h in range(8):
            mm(pv[0:64, h, 0:c], ones[0:64, 0:64], t8a[:, h, 0:c])
        tt(bon[:, :, 0:c], pv[0:64, :, 0:c], v_[:, :, 0:c], ALU.mult)
        if sample:
            cp(lw[:, :, 0:c], ldT[:, :, 0:c])
            act(Wt[:, :, 0:c], lw[:, :, 0:c], AF.Exp)
            tt(rt[:, :, 0:c], r_[:, :, 0:c], Wt[:, :, 0:c], ALU.mult)
            act(t8a[:, :, 0:c], lw[:, :, 0:c], AF.Exp, scale=-1.0)
            tt(t8b[:, :, 0:c], kkT[:, :, 0:c], aT[:, :, 0:c], ALU.mult)
            tt(bt[:, :, 0:c], t8b[:, :, 0:c], t8a[:, :, 0:c], ALU.mult)
            tt(kt[:, :, 0:c], kT_[:, :, 0:c], t8a[:, :, 0:c], ALU.mult)
            ts(at[:, :, 0:c], kkT[:, :, 0:c], -1.0, ALU.mult)
            cp(bh[:, :, 0:c], t8b[:, :, 0:c])
            cp(kh[:, :, 0:c], kT_[:, :, 0:c])
            return
        if c == 64:
            S.op("dve", lambda e: e.tensor_tensor_scan(out=lw.rearrange("p a b -> p (a b)"), data0=rmask.rearrange("p a b -> p (a b)"),
                                                       data1=ldT.rearrange("p a b -> p (a b)"), initial=0.0,
                                                       op0=ALU.mult, op1=ALU.add), reads=[rmask, ldT], writes=[lw])
        else:
            for h in range(8):
                S.op("dve", lambda e, h=h: e.tensor_tensor_scan(out=lw[:, h, 0:c], data0=rmask[:, h, 0:c], data1=ldT[:, h, 0:c],
                                                                initial=0.0, op0=ALU.mult, op1=ALU.add), reads=[rmask, ldT], writes=[lw])
        act(Wt[:, :, 0:c], lw[:, :, 0:c], AF.Exp)
        tt(rt[:, :, 0:c], r_[:, :, 0:c], Wt[:, :, 0:c], ALU.mult)
        act(t8a[:, :, 0:c], lw[:, :, 0:c], AF.Exp, scale=-1.0)
        tt(t8b[:, :, 0:c], kkT[:, :, 0:c], aT[:, :, 0:c], ALU.mult)
        tt(bt[:, :, 0:c], t8b[:, :, 0:c], t8a[:, :, 0:c], ALU.mult)
        tt(kt[:, :, 0:c], kT_[:, :, 0:c], t8a[:, :, 0:c], ALU.mult)
        tt(t8a[:, :, 0:c], lw[:, :, 0:c], ldT[:, :, 0:c], ALU.subtract)
        act(t8a[:, :, 0:c], t8a[:, :, 0:c], AF.Exp)
        stt(at[:, :, 0:c], kkT[:, :, 0:c], -1.0, t8a[:, :, 0:c], ALU.mult, ALU.mult)
        tt(t8a[:, :, 0:c], bc(lw[:, :, c - 1:c], [64, 8, c]), lw[:, :, 0:c], ALU.subtract)
        act(t8a[:, :, 0:c], t8a[:, :, 0:c], AF.Exp)
        tt(bh[:, :, 0:c], t8b[:, :, 0:c], t8a[:, :, 0:c], ALU.mult)
        tt(kh[:, :, 0:c], kT_[:, :, 0:c], t8a[:, :, 0:c], ALU.mult)

    def rw_core(c, o, first):
        r_ = uT[:, 0:8]
        v_ = uT[:, 16:24]
        to_tok(BhTok, bh, c, o)
        to_tok(KhTok, kh, c, o)
        to_tok(VTok, v_, c, o)
        if c == 1 and not first:
            pv = P().rearrange("p (a b) -> p a b", b=64)
            mm8(pv, bt, rt, c, triu, MrbT, o)
            pv = P().rearrange("p (a b) -> p a b", b=64)
            mm8(pv, kt, rt, c, triu, MrkT, o)
            pv = P().rearrange("p (a b) -> p a b", b=64)
            for h in range(8):
                mm(pv[0:c, h, :], at[:, h, o:o + c], STs[:, h, :])
            cp(Usb[0:c], pv[0:c], eng="act")
            Uc = Usb
        else:
            to_tok(AtTok, at, c, o)
            pv = P().rearrange("p (a b) -> p a b", b=64)
            mm8(pv, at, bt, c, trils, XA, o)
            pv = P().rearrange("p (a b) -> p a b", b=64)
            mm8(pv, bt, at, c, trius, XTA, o)
            pv = P().rearrange("p (a b) -> p a b", b=64)
            mm8(pv, kt, at, c, trius, LakT, o)
            pv = P().rearrange("p (a b) -> p a b", b=64)
            mm8(pv, bt, rt, c, triu, MrbT, o)
            pv = P().rearrange("p (a b) -> p a b", b=64)
            mm8(pv, kt, rt, c, triu, MrkT, o)
            solve(XA, XTA, XB, XTB, SM, 8, c)
            pv = P().rearrange("p (a b) -> p a b", b=64)
            for h in range(8):
                mm(pv[0:c, h, :], LakT[0:c, h, 0:c], VTok[0:c, h, :])
            cp(Zsb[0:c], pv[0:c], eng="act")
            pv = P().rearrange("p (a b) -> p a b", b=64)
            for h in range(8):
                mm(pv[0:c, h, :], SM[0:c, h, 0:c], Zsb[0:c, h, :])
            cp(U1[0:c], pv[0:c], eng="act")
            pv = P().rearrange("p (a b) -> p a b", b=64)
            for h in range(8):
                mm(pv[0:64, h, 0:c], AtTok[0:c, h, :], SM[0:c, h, 0:c])
            cp(PTs[:, :, 0:c], pv[0:64, :, 0:c], eng="act")
            if not first:
                pv = P().rearrange("p (a b) -> p a b", b=64)
                for h in range(8):
                    mm(pv[0:c, h, :], PTs[:, h, 0:c], STs[:, h, :])
                tt(Usb[0:c], U1[0:c], pv[0:c], ALU.add)
                Uc = Usb
            else:
                Uc = U1
        pY = P().rearrange("p (a b) -> p a b", b=64)
        for h in range(8):
            if not first:
                mm(pY[0:c, h, :], rt[:, h, o:o + c], STs[:, h, :], start=True, stop=False)
            mm(pY[0:c, h, :], MrbT[0:c, h, 0:c], Uc[0:c, h, :], start=first, stop=False)
            mm(pY[0:c, h, :], MrkT[0:c, h, 0:c], VTok[0:c, h, :], start=False, stop=True)
        cp(ytk[0:c], pY[0:c], eng="act")
        pS = P().rearrange("p (a b) -> p a b", b=64)
        for h in range(8):
            mm(pS[0:64, h, :], BhTok[0:c, h, :], Uc[0:c, h, :], start=True, stop=False)
            mm(pS[0:64, h, :], KhTok[0:c, h, :], VTok[0:c, h, :], start=False, stop=True)
        if first:
            cp(STs, pS[0:64])
        else:
            tt(STs, STs, bc(Wt[:, :, o + c - 1:o + c], [64, 8, 64]), ALU.mult)
            tt(STs, STs, pS[0:64], ALU.add)
        pv = P().rearrange("p (a b) -> p a b", b=64)
        for h in range(8):
            tr(pv[0:64, h, 0:c], ytk[0:c, h, :], c)
        cp(yrT[:, :, o:o + c], pv[0:64, :, 0:c], eng="act")

    def rw_post(l, t0, c, bi):
        pv = P().rearrange("p (a b) -> p a b", b=64)
        for h in range(8):
            mm(pv[0:64, h, 0:c], ones[0:64, 0:64], yrT[:, h, 0:c])
        stt(yrT[:, :, 0:c], pv[0:64, :, 0:c], -1.0 / 64, yrT[:, :, 0:c], ALU.mult, ALU.add)
        tt(t8a[:, :, 0:c], yrT[:, :, 0:c], yrT[:, :, 0:c], ALU.mult)
        pv = P().rearrange("p (a b) -> p a b", b=64)
        for h in range(8):
            mm(pv[0:64, h, 0:c], ones[0:64, 0:64], t8a[:, h, 0:c])
        act(t8a[:, :, 0:c], pv[0:64, :, 0:c], AF.Ln, bias=epsc[0:64, 3:4], scale=1.0 / 64)
        act(t8a[:, :, 0:c], t8a[:, :, 0:c], AF.Exp, scale=-0.5)
        tt(yrT[:, :, 0:c], yrT[:, :, 0:c], t8a[:, :, 0:c], ALU.mult)
        tt(yrT[:, :, 0:c], yrT[:, :, 0:c], bc(pcs("gn_w", 8, 64).unsqueeze(2), [64, 8, c]), ALU.mult)
        tt(yrT[:, :, 0:c], yrT[:, :, 0:c], bc(pcs("gn_b", 8, 64).unsqueeze(2), [64, 8, c]), ALU.add)
        tt(yrT[:, :, 0:c], yrT[:, :, 0:c], bon[:, :, 0:c], ALU.add)
        tt(yrT[:, :, 0:c], yrT[:, :, 0:c], zr[:, :, 0:c], ALU.mult)
        S.dma("sp", yb[bi][:, t0:t0 + c].rearrange("(h p) t -> p h t", p=64), yrT[:, :, 0:c])

    def rw_branch(l, bi):
        S.dma("sp", w2sb, w2[l])
        S.dma("sp", a2sb, a2[l])
        mset(prw[:, :, 0:1], 0.0)
        for ci, (t0, c) in enumerate(chunks):
            rw_chunk(l, t0, c, ci == 0, nxt_of(ci))
            rw_post(l, t0, c, bi)
        rw_state_out(p_wkv[l], p_shift[l])
        if with_samples:
            c = NS
            rw_pre(l, TP, c)
            S.dma("sp", B2[64:64 + c, 0:1664], st_shift[l]) if False else None
            stt_ = G[10][0:c, :]
            shT = g8(29, 64)[:, :, 0:32].rearrange("p a b -> p (a b)")[:, 0:0] if False else None
            shT = B2[64:128, 0:416].rearrange("p (a b) -> p a b", b=16) if False else None
            shT = G[25][0:64, 0:416].rearrange("p (a b) -> p a b", b=16)
            for q in range(4):
                nj = 8 if q < 3 else 2
                S.dma("sp", stt_[:, 0:nj * 64], st_shift[l, :, q * 512:q * 512 + nj * 64])
                p = P()
                for j in range(nj):
                    tr(p[0:64, j * 16:(j + 1) * 16], stt_[:, j * 64:(j + 1) * 64], c)
                cp(shT[:, q * 8:q * 8 + nj, :], p[0:64, 0:nj * 16].rearrange("p (a b) -> p a b", b=16))
            rw_shift(c, shT)
            for q in range(4):
                nj = 8 if q < 3 else 2
                p = P()
                for j in range(nj):
                    tr(p[0:c, j * 64:(j + 1) * 64], prw[:, q * 8 + j, 1:1 + c], 64)
                cp(stt_[:, 0:nj * 64], p[0:c, 0:nj * 64], eng="act")
                S.dma("sp", s_shift[l, :, q * 512:q * 512 + nj * 64], stt_[:, 0:nj * 64])
            rw_elem(c, True)
            stg = B2[0:64, 0:512].rearrange("p (a b) -> p a b", b=64)
            S.dma("sp", stg, st_wkv[l, 0].rearrange("h v k -> v h k"))
            for b in range(NS):
                pv = P().rearrange("p (a b) -> p a b", b=64)
                for h in range(8):
                    tr(pv[0:64, h, :], stg[:, h, :], 64)
                cp(STs, pv[0:64])
                if b + 1 < NS:
                    S.dma("sp", stg, st_wkv[l, b + 1].rearrange("h v k -> v h k"))
                rw_core(1, b, False)
                rw_state_out(s_wkv[l, b], None)
            rw_post(l, TP, c, bi)

    def rw_state_out(dst_st, dst_shift):
        pv = P().rearrange("p (a b) -> p a b", b=64)
        for h in range(8):
            tr(pv[0:64, h, :], STs[:, h, :], 64)
        cp(ytk, pv[0:64])
        S.dma("sp", dst_st.rearrange("h v k -> v h k"), ytk)
        if dst_shift is None:
            return
        p = P()
        tr(p[0:26, 0:64], prw[:, :, 0:1].rearrange("p a b -> p (a b)"), 64)
        cp(y1s[0:26, 0:64], p[0:26, 0:64])
        S.dma("sp", dst_shift.rearrange("(j p) -> j p", p=64), y1s[0:26, 0:64])

    for l in range(depth):
        S.dma("sp", pt, ptab[l])
        cur_src[0] = xin if l == 0 else xs[l % 2]
        phase1(l)
        bi = 0
        for b in branches:
            if b == "ssm":
                ssd_branch(l, bi)
                bi += 1
            if b == "gdn":
                gdn_branch(l, bi)
                bi += 1
            if b == "rw":
                rw_branch(l, bi)
                bi += 1
        phase3(l)

    S.finish()
    S.emit()
    return nc, S


_CACHE = {}


def kernel(**inp):
    inp = {k: np.asarray(v) for k, v in inp.items()}
    if "nc" not in _CACHE:
        _CACHE["nc"] = build()[0]
    nc = _CACHE["nc"]
    return _run(nc, inp, 8)


def _in_maps(inp, ncores):
    ptab = np.stack([_build_pt(inp, l) for l in range(DEPTH)])
    maps = []
    for c in range(ncores):
        xin = np.concatenate([inp["meta_tokens"], inp["x_prompt"][c], inp["x_sample"][16 * c:16 * c + 16, 0]], axis=0)
        sl = slice(16 * c, 16 * c + 16)
        maps.append({
            "xin": np.ascontiguousarray(xin, dtype=np.float32),
            "w_in": inp["w_in"], "w_rw_out": inp["w_rw_out"], "w_ssm_out": inp["w_ssm_out"],
            "w_gdn_out": inp["w_gdn_out"], "w_out": inp["w_out"], "rw_w2": inp["rw_w2"], "rw_a2": inp["rw_a2"],
            "ptab": ptab, "final_norm_w": inp["final_norm_w"].reshape(1, D),
            "st_wkv": np.ascontiguousarray(inp["state_rwkv_wkv"][:, sl]),
            "st_shift": np.ascontiguousarray(inp["state_rwkv_shift"][:, sl]),
            "st_ssm": np.ascontiguousarray(inp["state_ssm"][:, sl]),
            "st_sconv": np.ascontiguousarray(inp["state_ssm_conv"][:, sl]),
            "st_gdn": np.ascontiguousarray(inp["state_gdn"][:, sl]),
            "st_gconv": np.ascontiguousarray(inp["state_gdn_conv"][:, sl]),
        })
    return maps


def _run(nc, inp, ncores):
    maps = _in_maps(inp, ncores)
    res = run_bass_kernel_spmd(nc, maps, core_ids=list(range(ncores)))
    R = res.results
    y_prompt = np.stack([R[c]["y_out"][NMETA:TP] for c in range(ncores)])
    y_sample = np.concatenate([R[c]["y_out"][TP:TT] for c in range(ncores)])[:, None, :]

    def pst(name):
        return np.stack([R[c][name] for c in range(ncores)], axis=1)

    def sst(name):
        return np.concatenate([R[c][name] for c in range(ncores)], axis=1)
    return (y_prompt, y_sample, pst("p_wkv"), pst("p_shift"), pst("p_ssm"), pst("p_sconv"), pst("p_gdn"),
            pst("p_gconv"), sst("s_wkv"), sst("s_shift"), sst("s_ssm"), sst("s_sconv"), sst("s_gdn"), sst("s_gconv"))
```

```python
import numpy as np
import concourse.bass as bass
import concourse.mybir as mybir
from concourse.bass_utils import run_bass_kernel_spmd

F32 = mybir.dt.float32
BF16 = mybir.dt.bfloat16
ALU = mybir.AluOpType
AF = mybir.ActivationFunctionType
AX = mybir.AxisListType

DEPTH = 4
D = 1024
NMETA = 16
SEQ = 2048
TP = NMETA + SEQ
NS = 16
TT = TP + NS
DIN = 8848
C_RW, C_SSM, C_GDN, C_GATE = 0, 2176, 3720, 5776
EXPM05 = float(np.exp(-0.5))

SEM_EPOCH = 20000
N_DMA_SLOTS = 6


class _Sem:
    __slots__ = ("h", "val")

    def __init__(self, h):
        self.h = h
        self.val = 0


class _Buf:
    __slots__ = ("w", "r")

    def __init__(self):
        self.w = None
        self.r = {}


class _Eng:
    def __init__(self, name):
        self.name = name
        self.sem = None
        self.prog = []
        self.known = {}
        self.slots = []
        self.slot_i = 0


class Sched:
    def __init__(self, nc):
        self.nc = nc
        self.bufs = {}
        self.E = {n: _Eng(n) for n in ("pe", "dve", "act", "pool", "sp")}
        self.nsem = 0
        for e in self.E.values():
            e.sem = self._new_sem()
        for qn in ("sp", "act", "pool"):
            self.E[qn].slots = [self._new_sem() for _ in range(N_DMA_SLOTS)]

    def _new_sem(self):
        h = self.nc.semaphore(f"s{self.nsem}").__enter__()
        self.nsem += 1
        return _Sem(h)

    def _buf(self, ap):
        key = ap.tensor.name
        b = self.bufs.get(key)
        if b is None:
            b = self.bufs[key] = _Buf()
        return b

    def _need(self, eng, deps, sem, val, same_ok):
        if sem is eng.sem and same_ok:
            return
        if eng.known.get(sem, 0) >= val:
            return
        deps[sem] = max(deps.get(sem, 0), val)

    def _deps(self, eng, reads, writes, same_ok=True):
        deps = {}
        for ap in reads:
            b = self._buf(ap)
            if b.w is not None:
                self._need(eng, deps, b.w[0], b.w[1], False)
        for ap in writes:
            b = self._buf(ap)
            if b.w is not None:
                self._need(eng, deps, b.w[0], b.w[1], same_ok)
            for s, v in b.r.items():
                self._need(eng, deps, s, v, same_ok)
        for s, v in deps.items():
            eng.prog.append(("wait", s, v))
            eng.known[s] = v

    def op(self, en, fn, reads=(), writes=()):
        eng = self.E[en]
        self._deps(eng, reads, writes)
        if eng.sem.val >= SEM_EPOCH:
            eng.sem = self._new_sem()
        s = eng.sem
        s.val += 1
        eng.prog.append(("inst", fn, s, 1))
        for ap in reads:
            self._buf(ap).r[s] = s.val
        for ap in writes:
            b = self._buf(ap)
            b.w = (s, s.val)
            b.r = {}

    def dma(self, qn, out, in_, **kw):
        eng = self.E[qn]
        slot = eng.slots[eng.slot_i % N_DMA_SLOTS]
        eng.slot_i += 1
        if slot.val > 0 and eng.known.get(slot, 0) < slot.val:
            eng.prog.append(("wait", slot, slot.val))
            eng.known[slot] = slot.val
        self._deps(eng, [in_], [out], same_ok=False)
        slot.val += 16
        v = slot.val

        def fn(e, out=out, in_=in_, kw=kw):
            return e.dma_start(out=out, in_=in_, **kw)
        eng.prog.append(("inst", fn, slot, 16))
        self._buf(in_).r[slot] = v
        b = self._buf(out)
        b.w = (slot, v)
        b.r = {}

    def finish(self):
        sp = self.E["sp"]
        for e in self.E.values():
            for s in e.slots:
                if s.val > 0 and sp.known.get(s, 0) < s.val:
                    sp.prog.append(("wait", s, s.val))
                    sp.known[s] = s.val
        for e in self.E.values():
            if e is not sp and e.sem.val > 0:
                sp.prog.append(("wait", e.sem, e.sem.val))

    def emit(self):
        E = self.E

        def replay(eng, h):
            for it in eng.prog:
                if it[0] == "wait":
                    h.wait_ge(it[1].h, it[2])
                else:
                    it[1](h).then_inc(it[2].h, it[3])

        with self.nc.Block() as block:
            @block.tensor
            def _(h):
                replay(E["pe"], h)

            @block.vector
            def _(h):
                replay(E["dve"], h)

            @block.scalar
            def _(h):
                replay(E["act"], h)

            @block.gpsimd
            def _(h):
                replay(E["pool"], h)

            @block.sync
            def _(h):
                replay(E["sp"], h)

    def stats(self):
        return ({n: sum(1 for i in e.prog if i[0] == "inst") for n, e in self.E.items()},
                {n: sum(1 for i in e.prog if i[0] == "wait") for n, e in self.E.items()}, self.nsem)


def _pt_layout():
    cols = {}
    n = 0

    def add(name, k):
        nonlocal n
        cols[name] = n
        n += k
    add("norm_w", 8)
    add("mu", 26); add("w0", 8); add("a0", 8); add("k_k", 8); add("k_a", 8); add("r_k", 8)
    add("gn_w", 8); add("gn_b", 8)
    add("sconv_w", 32); add("sconv_b", 8); add("snorm_w", 4); add("dt_bias", 1); add("a_log", 1)
    add("ssm_d", 4)
    add("gconv_w", 48); add("gdt_bias", 1); add("ga_log", 1); add("gnorm_w", 1)
    return cols, n


PTC, NPC = _pt_layout()


def _build_pt(inp, l):
    pt = np.zeros((128, NPC), np.float32)

    def put128(name, v):
        m = v.size // 128
        pt[:, PTC[name]:PTC[name] + m] = v.reshape(m, 128).T

    def put64(name, v):
        m = v.size // 64
        pt[0:64, PTC[name]:PTC[name] + m] = v.reshape(m, 64).T

    put128("norm_w", inp["norm_w"][l])
    put64("mu", inp["rw_mu"][l]); put64("w0", inp["rw_w0"][l]); put64("a0", inp["rw_a0"][l])
    put64("k_k", inp["rw_k_k"][l]); put64("k_a", inp["rw_k_a"][l]); put64("r_k", inp["rw_r_k"][l].reshape(-1))
    put64("gn_w", inp["rw_gn_w"][l]); put64("gn_b", inp["rw_gn_b"][l])
    put128("sconv_w", inp["ssm_conv_w"][l].reshape(-1))
    put128("sconv_b", inp["ssm_conv_b"][l]); put128("snorm_w", inp["ssm_norm_w"][l])
    pt[0:8, PTC["dt_bias"]] = inp["ssm_dt_bias"][l]
    pt[0:8, PTC["a_log"]] = inp["ssm_a_log"][l]
    put128("ssm_d", np.repeat(inp["ssm_d"][l], 64))
    put128("gconv_w", inp["gdn_conv_w"][l].reshape(-1))
    pt[0:4, PTC["gdt_bias"]] = inp["gdn_dt_bias"][l]
    pt[0:4, PTC["ga_log"]] = inp["gdn_a_log"][l]
    pt[:, PTC["gnorm_w"]] = inp["gdn_norm_w"][l]
    return pt


def build(depth=DEPTH, branches=("rw", "ssm", "gdn"), debug=False, with_samples=True):
    nc = bass.Bass("TRN2", target_bir_lowering=False)
    S = Sched(nc)

    def din(name, shape):
        return nc.dram_tensor(name, list(shape), F32, kind="ExternalInput").ap()

    def dout(name, shape):
        return nc.dram_tensor(name, list(shape), F32, kind="ExternalOutput").ap()

    def dscr(name, shape):
        return nc.dram_tensor(name, list(shape), F32).ap()

    xin = din("xin", [TT, D])
    w_in = din("w_in", [DEPTH, D, DIN])
    w_rwo = din("w_rw_out", [DEPTH, 512, D])
    w_sso = din("w_ssm_out", [DEPTH, 512, D])
    w_gdo = din("w_gdn_out", [DEPTH, 512, D])
    w_out = din("w_out", [DEPTH, D, D])
    w2 = din("rw_w2", [DEPTH, 64, 512])
    a2 = din("rw_a2", [DEPTH, 64, 512])
    ptab = din("ptab", [DEPTH, 128, NPC])
    fnw = din("final_norm_w", [1, D])
    st_wkv = din("st_wkv", [DEPTH, NS, 8, 64, 64])
    st_shift = din("st_shift", [DEPTH, NS, 1664])
    st_ssm = din("st_ssm", [DEPTH, NS, 8, 64, 128])
    st_sconv = din("st_sconv", [DEPTH, NS, 3, 1024])
    st_gdn = din("st_gdn", [DEPTH, NS, 4, 128, 128])
    st_gconv = din("st_gconv", [DEPTH, NS, 3, 1536])

    y_out = dout("y_out", [TT, D])
    p_wkv = dout("p_wkv", [DEPTH, 8, 64, 64])
    p_shift = dout("p_shift", [DEPTH, 1664])
    p_ssm = dout("p_ssm", [DEPTH, 8, 64, 128])
    p_sconv = dout("p_sconv", [DEPTH, 3, 1024])
    p_gdn = dout("p_gdn", [DEPTH, 4, 128, 128])
    p_gconv = dout("p_gconv", [DEPTH, 3, 1536])
    s_wkv = dout("s_wkv", [DEPTH, NS, 8, 64, 64])
    s_shift = dout("s_shift", [DEPTH, NS, 1664])
    s_ssm = dout("s_ssm", [DEPTH, NS, 8, 64, 128])
    s_sconv = dout("s_sconv", [DEPTH, NS, 3, 1024])
    s_gdn = dout("s_gdn", [DEPTH, NS, 4, 128, 128])
    s_gconv = dout("s_gconv", [DEPTH, NS, 3, 1536])
    dbg = dout("dbg", [3, 512, TT]) if debug else None

    xs = [dscr("xs0", [TT, D]), dscr("xs1", [TT, D])]
    projT = dscr("projT", [DIN, TT])
    yb = [dscr(f"yb{i}", [512, TT]) for i in range(3)]

    _n = [0]

    def sb(shape, dt=F32, name=None):
        _n[0] += 1
        return nc.alloc_sbuf_tensor(name or f"t{_n[0]}", list(shape), dt).ap()

    PS = [nc.alloc_psum_tensor(f"ps{i}", [128, 512], F32).ap() for i in range(8)]
    _pi = [0]

    def P():
        _pi[0] += 1
        return PS[_pi[0] % 8]

    def mm(out, lhsT, rhs, start=True, stop=True):
        S.op("pe", lambda e: e.matmul(out, lhsT=lhsT, rhs=rhs, start=start, stop=stop),
             reads=[lhsT, rhs], writes=[out])

    def tr(out, in_, n):
        S.op("pe", lambda e: e.transpose(out, in_, ident[0:n, 0:n]), reads=[in_, ident], writes=[out])

    def act(out, in_, func, bias=None, scale=1.0, accum=None, eng="act"):
        rd = [in_]
        kw = {}
        if bias is not None:
            kw["bias"] = bias
            rd.append(bias)
        if not isinstance(scale, float):
            rd.append(scale)
        if accum is not None:
            kw["accum_out"] = accum
        wr = [out] + ([accum] if accum is not None else [])
        S.op("act", lambda e: e.activation(out=out, in_=in_, func=func, scale=scale, **kw), reads=rd, writes=wr)

    def tt(out, a, b, op, eng="dve"):
        S.op(eng, lambda e: e.tensor_tensor(out=out, in0=a, in1=b, op=op), reads=[a, b], writes=[out])

    def ts(out, a, s1, op0, s2=None, op1=None, eng="dve"):
        rd = [a] + [s for s in (s1, s2) if s is not None and not isinstance(s, float)]
        if op1 is None:
            S.op(eng, lambda e: e.tensor_scalar(out=out, in0=a, scalar1=s1, scalar2=None, op0=op0), reads=rd, writes=[out])
        else:
            S.op(eng, lambda e: e.tensor_scalar(out=out, in0=a, scalar1=s1, scalar2=s2, op0=op0, op1=op1), reads=rd, writes=[out])

    def stt(out, a, s, b, op0, op1, eng="dve"):
        rd = [a, b] + ([] if isinstance(s, float) else [s])
        S.op(eng, lambda e: e.scalar_tensor_tensor(out=out, in0=a, scalar=s, in1=b, op0=op0, op1=op1), reads=rd, writes=[out])

    def cp(out, in_, eng="dve"):
        if eng == "act":
            S.op("act", lambda e: e.copy(out=out, in_=in_), reads=[in_], writes=[out])
        else:
            S.op(eng, lambda e: e.tensor_copy(out=out, in_=in_), reads=[in_], writes=[out])

    def mset(t, v, eng="pool"):
        S.op(eng, lambda e: e.memset(t, v), writes=[t])

    def asel(t, pattern, cmp, fill, base, cm):
        S.op("pool", lambda e: e.affine_select(out=t, in_=t, pattern=pattern, compare_op=cmp, fill=fill,
                                               base=base, channel_multiplier=cm), reads=[t], writes=[t])

    def bc(ap, shape):
        return ap.to_broadcast(list(shape))

    ident = sb([128, 128], name="ident")
    mset(ident, 0.0)
    asel(ident, [[-1, 128]], ALU.not_equal, 1.0, 0, 1)
    ones = sb([128, 128], name="ones")
    mset(ones, 1.0)
    triu = sb([64, 64], name="triu")
    mset(triu, 1.0)
    asel(triu, [[1, 64]], ALU.is_ge, 0.0, 0, -1)
    trius = sb([64, 64], name="trius")
    mset(trius, 1.0)
    asel(trius, [[1, 64]], ALU.is_gt, 0.0, 0, -1)
    trils = sb([64, 64], name="trils")
    mset(trils, 1.0)
    asel(trils, [[-1, 64]], ALU.is_gt, 0.0, 0, 1)
    negU = sb([64, 64], name="negU")
    mset(negU, 0.0)
    asel(negU, [[1, 64]], ALU.is_ge, -30000.0, 0, -1)
    posLs = sb([64, 64], name="posLs")
    mset(posLs, 0.0)
    asel(posLs, [[-1, 64]], ALU.is_gt, 30000.0, 0, 1)
    epsc = sb([128, 4], name="epsc")
    mset(epsc[:, 0:1], 1e-6)
    mset(epsc[:, 1:2], 1.0)
    mset(epsc[:, 2:3], 1e-5)
    mset(epsc[:, 3:4], 64e-5)
    negU8 = sb([64, 8, 64], name="negU8")
    cp(negU8, bc(negU.unsqueeze(1), [64, 8, 64]), eng="pool")
    posL4 = sb([64, 4, 64], name="posL4")
    cp(posL4, bc(posLs.unsqueeze(1), [64, 4, 64]), eng="pool")

    hT_all = sb([128, 8, TT], BF16, name="hT_all")
    WA = sb([128, 8, 1024], BF16, name="WA")
    WB = sb([128, 8, 1024], BF16, name="WB")
    Wo = hT_all[:, :, 0:1024]
    ysb = [sb([128, 4, 512], BF16, name=f"ysb{i}") for i in range(3)]
    mT = sb([128, 8, 512], BF16, name="mT")
    dtraw = sb([8, 64], name="dtraw")
    garaw = sb([4, 64], name="garaw")
    gbraw = sb([4, 64], name="gbraw")
    pt = sb([128, NPC], name="pt")
    xt = sb([128, D], name="xt")
    xr = sb([128, D], name="xr")
    junk = xr
    stat = sb([128, 4], name="stat")
    NG = 32
    G = [sb([128, 512], name=f"g{i}") for i in range(NG)]
    B0 = sb([128, 1690], name="B0"); B1 = sb([128, 1664], name="B1"); B2 = sb([128, 1664], name="B2"); B3 = sb([128, 768], name="B3")

    fnw_bc = B1[:, 0:D]

    def g8(i, rows=128, b=64):
        return G[i][0:rows, :].rearrange("p (a b) -> p a b", b=b)

    def g4(i, rows=128, b=64):
        return G[i][0:rows, 0:4 * b].rearrange("p (a b) -> p a b", b=b)

    def pc(name, k=0, rows=128):
        c0 = PTC[name] + k
        return pt[0:rows, c0:c0 + 1]

    def pcs(name, k, rows=128):
        c0 = PTC[name]
        return pt[0:rows, c0:c0 + k]

    ttiles = [(i * 128, 128) for i in range(16)] + [(2048, 32)]
    chunks = [(i * 64, 64) for i in range(32)] + [(2048, 16)]

    def norm_to_hT(t0, n, l):
        act(junk[0:n], xt[0:n], AF.Square, accum=stat[0:n, 0:1])
        act(stat[0:n, 1:2], stat[0:n, 0:1], AF.Ln, bias=epsc[0:n, 0:1], scale=1.0 / D)
        act(stat[0:n, 2:3], stat[0:n, 1:2], AF.Exp, scale=-0.5)
        ts(xr[0:n], xt[0:n], stat[0:n, 2:3], ALU.mult)
        for half in range(2):
            p = P()
            pv = p.rearrange("p (a b) -> p a b", b=128)
            for j in range(4):
                k = half * 4 + j
                tr(pv[:, j, 0:n], xr[0:n, k * 128:(k + 1) * 128], n)
            tt(hT_all[:, half * 4:half * 4 + 4, t0:t0 + n], pv[:, :, 0:n],
               bc(pcs("norm_w", 8)[:, half * 4:half * 4 + 4].unsqueeze(2), [128, 4, n]), ALU.mult)

    def final_norm(t0, n):
        act(junk[0:n], xt[0:n], AF.Square, accum=stat[0:n, 0:1])
        act(stat[0:n, 1:2], stat[0:n, 0:1], AF.Ln, bias=epsc[0:n, 0:1], scale=1.0 / D)
        act(stat[0:n, 2:3], stat[0:n, 1:2], AF.Exp, scale=-0.5)
        stt(xr[0:n], xt[0:n], stat[0:n, 2:3], fnw_bc[0:n], ALU.mult, ALU.mult)
        S.dma("sp", y_out[t0:t0 + n, :], xr[0:n])

    cbs = B0[:, 0:536].rearrange("p (a b) -> p a b", b=67)
    cacc = g8(0); ctmp = g8(1); xsT = g8(2)
    dtT = sb([8, 2, 64], name="dtT")
    acol = sb([8, 2], name="acol")
    tok8 = sb([64, 16], name="tok8")
    acs = sb([64, 40], name="acs")
    totb = sb([128, 8], name="totb")
    Rt = g8(3, 64); segT = g8(4, 64); xTok = g8(5, 64); xdt = g8(6, 64); xdtd = g8(7, 64)
    BTok = G[8][0:64, 0:256]
    MT = g8(9, 64)
    y1s = G[10][0:64, :]
    ytok = g8(11, 64)
    hst = g8(12)
    zT = g4(13); yT = g4(14); sq = g4(15)
    rs = G[16][:, 0:128].rearrange("p (a b) -> p a b", b=64)
    ybf = sb([128, 4, 64], BF16, name="ybf")
    cur_src = [xin]

    supers = [(0, 512), (512, 512), (1024, 512), (1536, 512), (2048, 32)]
    _stg = [0]

    def phase1(l):
        src = cur_src[0]
        for (t0, n) in ttiles:
            S.dma("sp", xt[0:n], src[t0:t0 + n, :])
            norm_to_hT(t0, n, l)
        gi = 0
        for c0 in range(0, DIN, 1024):
            ncol = min(1024, DIN - c0)
            W = WA if gi % 2 == 0 else WB
            gi += 1
            S.dma("pool", W[:, :, 0:ncol], w_in[l, :, c0:c0 + ncol].rearrange("(k p) c -> p k c", p=128))
            for (s0, sn) in supers:
                for j0 in range(0, ncol, 128):
                    w = min(128, ncol - j0)
                    p = P()
                    for k in range(8):
                        mm(p[0:w, 0:sn], W[:, k, j0:j0 + w], hT_all[:, k, s0:s0 + sn], start=(k == 0), stop=(k == 7))
                    _stg[0] += 1
                    stg = G[_stg[0] % 8]
                    cp(stg[0:w, 0:sn], p[0:w, 0:sn], eng=("act" if _stg[0] % 2 else "dve"))
                    S.dma("sp", projT[c0 + j0:c0 + j0 + w, s0:s0 + sn], stg[0:w, 0:sn])

    def phase3(l):
        wsrc = [w_rwo, w_sso, w_gdo]
        Wb = [WA[:, 0:4, :], WA[:, 4:8, :], WB[:, 0:4, :]]
        for b in range(3):
            S.dma("pool", Wb[b], wsrc[b][l].rearrange("(k p) c -> p k c", p=128))
        S.dma("pool", Wo, w_out[l].rearrange("(k p) c -> p k c", p=128))
        if l == depth - 1:
            S.dma("sp", fnw_bc, fnw.partition_broadcast(128))
        src = cur_src[0]
        dst = xs[(l + 1) % 2]
        gq = 0
        for (s0, sn) in supers:
            for b in range(3):
                S.dma("pool", ysb[b][:, :, 0:sn], yb[b][:, s0:s0 + sn].rearrange("(k p) t -> p k t", p=128))
            for cc in range(8):
                macc = G[8 + (cc % 2)]
                for b in range(3):
                    gq += 1
                    gt = G[10 + (gq % 4)]
                    r0 = C_GATE + b * 1024 + cc * 128
                    S.dma("sp", gt[:, 0:sn], projT[r0:r0 + 128, s0:s0 + sn])
                    act(gt[:, 0:sn], gt[:, 0:sn], AF.Sigmoid)
                    po = P()
                    for k in range(4):
                        mm(po[:, 0:sn], Wb[b][:, k, cc * 128:(cc + 1) * 128], ysb[b][:, k, 0:sn], start=(k == 0), stop=(k == 3))
                    if b == 0:
                        tt(macc[:, 0:sn], po[:, 0:sn], gt[:, 0:sn], ALU.mult)
                    else:
                        tt(gt[:, 0:sn], po[:, 0:sn], gt[:, 0:sn], ALU.mult)
                        if b == 1:
                            tt(macc[:, 0:sn], macc[:, 0:sn], gt[:, 0:sn], ALU.add)
                        else:
                            tt(mT[:, cc, 0:sn], macc[:, 0:sn], gt[:, 0:sn], ALU.add)
            for (t0, n) in ttiles:
                if not (s0 <= t0 < s0 + sn):
                    continue
                o_ = t0 - s0
                S.dma("sp", xt[0:n], src[t0:t0 + n, :])
                for half in range(2):
                    p = P()
                    for k in range(8):
                        mm(p[0:n, :], mT[:, k, o_:o_ + n], Wo[:, k, half * 512:(half + 1) * 512], start=(k == 0), stop=(k == 7))
                    tt(xt[0:n, half * 512:(half + 1) * 512], xt[0:n, half * 512:(half + 1) * 512], p[0:n, :], ALU.add)
                if l == depth - 1:
                    final_norm(t0, n)
                else:
                    S.dma("sp", dst[t0:t0 + n, :], xt[0:n])

    pf = {"ssd": False, "gdn": False, "rw": False}

    def nxt_of(ci):
        if ci + 1 < len(chunks):
            return chunks[ci + 1]
        return (TP, NS) if with_samples else None

    def ssd_pre(l, t0, c, part=None):
        part = part or ("z" if pf["ssd"] else "all")
        if part in ("all", "in"):
            S.dma("sp", cbs[:, :, 3:3 + c], projT[C_SSM + 512:C_SSM + 1536, t0:t0 + c].rearrange("(k p) t -> p k t", p=128))
            S.dma("sp", dtraw[:, 0:c], projT[C_SSM + 1536:C_SSM + 1544, t0:t0 + c])
        if part in ("all", "z"):
            S.dma("sp", zT[:, :, 0:c], projT[C_SSM:C_SSM + 512, t0:t0 + c].rearrange("(k p) t -> p k t", p=128))
            act(zT[:, :, 0:c], zT[:, :, 0:c], AF.Silu)
            pf["ssd"] = False
        if part == "in":
            pf["ssd"] = True
        dt_ps[0] = dtraw

    dt_ps = [None]

    def ssd_dt(c):
        p = dt_ps[0]
        act(dtT[:, 0, 0:c], p[0:8, 0:c], AF.Exp, bias=pc("dt_bias", 0, 8))
        act(dtT[:, 0, 0:c], dtT[:, 0, 0:c], AF.Ln, bias=epsc[0:8, 1:2])
        ts(dtT[:, 1, 0:c], dtT[:, 0, 0:c], acol[:, 1:2], ALU.mult)

    def conv_silu(cb, w_name, nk, c, accv, tmpv, outv, bias_name=None):
        W4 = pcs(w_name, 4 * nk)
        for i in range(4):
            wv = bc(W4[:, i * nk:(i + 1) * nk].unsqueeze(2), [128, nk, c])
            if i == 0:
                tt(accv, cb[:, :, 0:c], wv, ALU.mult)
            else:
                tt(tmpv, cb[:, :, i:i + c], wv, ALU.mult, eng="pool")
                tt(accv, accv, tmpv, ALU.add)
        if bias_name is not None:
            tt(accv, accv, bc(pcs(bias_name, nk).unsqueeze(2), [128, nk, c]), ALU.add)
        act(outv, accv, AF.Silu)

    def ssd_chunk(l, t0, c, first, nxt=None):
        ssd_pre(l, t0, c)
        conv_silu(cbs, "sconv_w", 8, c, cacc[:, :, 0:c], ctmp[:, :, 0:c], xsT[:, :, 0:c], "sconv_b")
        ssd_dt(c)
        cp(ctmp[:, :, 0:3], cbs[:, :, c:c + 3], eng="pool")
        cp(cbs[:, :, 0:3], ctmp[:, :, 0:3], eng="pool")
        if nxt is not None:
            ssd_pre(l, nxt[0], nxt[1], "in")
        ssd_core(c, 0, first)

    def ssd_core(c, o, first):
        p = P()
        tr(p[0:c, 0:8], dtT[:, 0, o:o + c], 8)
        tr(p[0:c, 8:16], dtT[:, 1, o:o + c], 8)
        cp(tok8[0:c], p[0:c, 0:16])
        p = P()
        pv = p.rearrange("p (a b) -> p a b", b=128)
        for j in range(4):
            tr(pv[0:c, j, :], xsT[:, j, o:o + c], 128)
        cp(xTok[0:c].rearrange("p a b -> p (a b)"), p[0:c, :], eng="act")
        p = P()
        for j in range(2):
            tr(p[0:c, j * 128:(j + 1) * 128], xsT[:, 4 + j, o:o + c], 128)
        cp(BTok[0:c], p[0:c, 0:256], eng="act")
        p = P()
        mm(p[0:c, 0:8], triu[0:c, 0:c], tok8[0:c, 8:16])
        cp(acs[0:c, 0:8], p[0:c, 0:8])
        ts(acs[0:c, 8:16], p[0:c, 0:8], -1.0, ALU.mult)
        act(acs[0:c, 16:24], p[0:c, 0:8], AF.Exp)
        tt(Rt[0:c, :, 0:c], bc(tok8[0:c, 8:16].unsqueeze(2), [c, 8, c]),
           bc(triu[0:c, 0:c].unsqueeze(1), [c, 8, c]), ALU.mult)
        pa = P()
        pav = pa.rearrange("p (a b) -> p a b", b=64)
        if c == 64:
            mm(pa[:, :], ones[0:c, :], Rt[0:c].rearrange("p a b -> p (a b)"), start=True, stop=False)
            mm(pa[0:64, :], ident[0:64, 0:64], negU8.rearrange("p a b -> p (a b)"), start=False, stop=True)
        else:
            for h in range(8):
                mm(pav[:, h, 0:c], ones[0:c, :], Rt[0:c, h, 0:c], start=True, stop=False)
                mm(pav[0:c, h, 0:c], ident[0:c, 0:c], negU[0:c, 0:c], start=False, stop=True)
        tt(segT[0:c, :, 0:c], pav[0:c, :, 0:c], bc(acs[0:c, 8:16].unsqueeze(2), [c, 8, c]), ALU.add)
        act(segT[0:c, :, 0:c], segT[0:c, :, 0:c], AF.Exp)
        act(totb[:, :], pav[:, :, c - 1], AF.Exp)
        tt(acs[0:c, 24:32], pav[0:c, :, c - 1], acs[0:c, 8:16], ALU.add)
        act(acs[0:c, 24:32], acs[0:c, 24:32], AF.Exp)
        tt(xdt[0:c], xTok[0:c], bc(tok8[0:c, 0:8].unsqueeze(2), [c, 8, 64]), ALU.mult)
        tt(xdtd[0:c], xdt[0:c], bc(acs[0:c, 24:32].unsqueeze(2), [c, 8, 64]), ALU.mult, eng="pool")
        pc_ = P()
        pcv = pc_.rearrange("p (a b) -> p a b", b=64)
        for g in range(2):
            mm(pcv[0:c, g, 0:c], xsT[:, 4 + g, o:o + c], xsT[:, 6 + g, o:o + c])
        for g in range(2):
            tt(MT[0:c, g * 4:(g + 1) * 4, 0:c], segT[0:c, g * 4:(g + 1) * 4, 0:c],
               bc(pcv[0:c, g:g + 1, 0:c], [c, 4, c]), ALU.mult)
        p1 = P()
        p1v = p1.rearrange("p (a b) -> p a b", b=64)
        for h in range(8):
            mm(p1v[0:c, h, :], MT[0:c, h, 0:c], xdt[0:c, h, :])
        cp(y1s[0:c], p1[0:c, :], eng="act")
        p2 = P()
        if not first:
            for g in range(2):
                mm(p2[0:c, g * 256:(g + 1) * 256], xsT[:, 6 + g, o:o + c],
                   hst[:, g * 4:(g + 1) * 4, :].rearrange("p a b -> p (a b)"))
            tt(ytok[0:c], p2[0:c, :].rearrange("p (a b) -> p a b", b=64),
               bc(acs[0:c, 16:24].unsqueeze(2), [c, 8, 64]), ALU.mult)
            tt(ytok[0:c], ytok[0:c], y1s[0:c].rearrange("p (a b) -> p a b", b=64), ALU.add)
            ysrc = ytok
        else:
            ysrc = y1s.rearrange("p (a b) -> p a b", b=64)
        p3 = P()
        for g in range(2):
            mm(p3[:, g * 256:(g + 1) * 256], BTok[0:c, g * 128:(g + 1) * 128],
               xdtd[0:c, g * 4:(g + 1) * 4, :].rearrange("p a b -> p (a b)"))
        if first:
            cp(hst.rearrange("p a b -> p (a b)"), p3[:, :])
        else:
            tt(hst, hst, bc(totb.unsqueeze(2), [128, 8, 64]), ALU.mult)
            tt(hst.rearrange("p a b -> p (a b)"), hst.rearrange("p a b -> p (a b)"), p3[:, :], ALU.add)
        p4 = P()
        p4v = p4.rearrange("p (a b) -> p a b", b=64)
        for j in range(4):
            tr(p4v[:, j, 0:c], ysrc[0:c, 2 * j:2 * j + 2, :].rearrange("p a b -> p (a b)"), c)
        cp(yT[:, :, o:o + c], p4v[:, 0:4, 0:c], eng="act")

    def ssd_post(l, t0, c, bi):
        tt(sq[:, :, 0:c], xsT[:, 0:4, 0:c], bc(pcs("ssm_d", 4).unsqueeze(2), [128, 4, c]), ALU.mult)
        tt(yT[:, :, 0:c], yT[:, :, 0:c], sq[:, :, 0:c], ALU.add)
        tt(yT[:, :, 0:c], yT[:, :, 0:c], zT[:, :, 0:c], ALU.mult)
        tt(sq[:, :, 0:c], yT[:, :, 0:c], yT[:, :, 0:c], ALU.mult)
        p = P()
        pv = p.rearrange("p (a b) -> p a b", b=64)
        for g in range(2):
            mm(pv[:, g, 0:c], ones[:, :], sq[:, 2 * g, 0:c], start=True, stop=False)
            mm(pv[:, g, 0:c], ones[:, :], sq[:, 2 * g + 1, 0:c], start=False, stop=True)
        act(rs[:, :, 0:c], pv[:, 0:2, 0:c], AF.Ln, bias=epsc[:, 2:3], scale=1.0 / 256)
        act(rs[:, :, 0:c], rs[:, :, 0:c], AF.Exp, scale=-0.5)
        for g in range(2):
            tt(yT[:, 2 * g:2 * g + 2, 0:c], yT[:, 2 * g:2 * g + 2, 0:c], bc(rs[:, g:g + 1, 0:c], [128, 2, c]), ALU.mult)
        tt(yT[:, :, 0:c], yT[:, :, 0:c], bc(pcs("snorm_w", 4).unsqueeze(2), [128, 4, c]), ALU.mult)
        S.dma("sp", yb[bi][:, t0:t0 + c].rearrange("(k p) t -> p k t", p=128), yT[:, :, 0:c])

    def ssd_branch(l, bi):
        act(acol[:, 0:1], pc("a_log", 0, 8), AF.Exp)
        ts(acol[:, 1:2], acol[:, 0:1], -1.0, ALU.mult)
        mset(cbs[:, :, 0:3], 0.0)
        for ci, (t0, c) in enumerate(chunks):
            ssd_chunk(l, t0, c, ci == 0, nxt_of(ci))
            ssd_post(l, t0, c, bi)
        ssd_state_out(p_ssm[l], p_sconv[l])
        if with_samples:
            c = NS
            ssd_pre(l, TP, c)
            S.dma("sp", B1[0:48, 0:1024], st_sconv[l].rearrange("b i c -> (b i) c"))
            p = P()
            for k in range(8):
                tr(p[:, k * 48:(k + 1) * 48], B1[0:48, k * 128:(k + 1) * 128], 48)
            sbT = B2[:, 0:384].rearrange("p (k b i) -> p k b i", k=8, i=3)
            cp(B2[:, 0:384], p[:, 0:384])
            W4 = pcs("sconv_w", 32)
            for i in range(4):
                src = sbT[:, :, :, i] if i < 3 else cbs[:, :, 3:3 + c]
                wv = bc(W4[:, i * 8:(i + 1) * 8].unsqueeze(2), [128, 8, c])
                if i == 0:
                    tt(cacc[:, :, 0:c], src, wv, ALU.mult)
                else:
                    tt(ctmp[:, :, 0:c], src, wv, ALU.mult, eng="pool")
                    tt(cacc[:, :, 0:c], cacc[:, :, 0:c], ctmp[:, :, 0:c], ALU.add)
            tt(cacc[:, :, 0:c], cacc[:, :, 0:c], bc(pcs("sconv_b", 8).unsqueeze(2), [128, 8, c]), ALU.add)
            act(xsT[:, :, 0:c], cacc[:, :, 0:c], AF.Silu)
            ssd_dt(c)
            S.dma("sp", s_sconv[l, :, 0:2, :], st_sconv[l, :, 1:3, :])
            for half in range(2):
                p = P()
                for j in range(4):
                    tr(p[0:c, j * 128:(j + 1) * 128], cbs[:, half * 4 + j, 3:3 + c], 128)
                cp(B1[0:c, half * 512:(half + 1) * 512], p[0:c, :], eng="act")
            S.dma("sp", s_sconv[l, :, 2, :], B1[0:c, 0:1024])
            stg = g8(17, 64, 128)
            stg2 = g8(18, 64, 128)

            def ld_state(b):
                S.dma("sp", stg[:, 0:4, :], st_ssm[l, b, 0:4].rearrange("h p n -> p h n"))
                S.dma("sp", stg2[:, 0:4, :], st_ssm[l, b, 4:8].rearrange("h p n -> p h n"))
            ld_state(0)
            for b in range(NS):
                p = P()
                pv = p.rearrange("p (a b) -> p a b", b=64)
                for h in range(4):
                    tr(pv[:, h, :], stg[:, h, :], 64)
                    tr(pv[:, 4 + h, :], stg2[:, h, :], 64)
                cp(hst, pv)
                if b + 1 < NS:
                    ld_state(b + 1)
                ssd_core(1, b, False)
                ssd_state_out(s_ssm[l, b], None, b)
            ssd_post(l, TP, c, bi)

    def ssd_state_out(dst_st, dst_conv, par=0):
        for half in range(2):
            p = P()
            for h in range(4):
                tr(p[0:64, h * 128:(h + 1) * 128], hst[:, half * 4 + h, :], 128)
            so = G[21 + 2 * (par % 2) + half][0:64, :]
            cp(so, p[0:64, :], eng=("act" if half else "dve"))
            S.dma("sp", dst_st[half * 4:(half + 1) * 4].rearrange("h p n -> p h n"), so.rearrange("p (h n) -> p h n", n=128))
        if dst_conv is None:
            return
        for k in range(8):
            p = P()
            tr(p[0:3, 0:128], cbs[:, k, 0:3], 128)
            cp(y1s[0:3, 0:128], p[0:3, 0:128])
            S.dma("sp", dst_conv[:, k * 128:(k + 1) * 128], y1s[0:3, 0:128])

    def nsteps_for(c):
        n = 0
        while (1 << (n + 1)) < c:
            n += 1
        return n

    def cs_bc(R, nh, c, mask_nh, mask2d):
        p = P()
        pv = p.rearrange("p (a b) -> p a b", b=64)
        if c == 64:
            mm(p[:, 0:nh * 64], ones[0:c, :], R[0:c].rearrange("p a b -> p (a b)"), start=True, stop=False)
            mm(p[0:64, 0:nh * 64], ident[0:64, 0:64], mask_nh.rearrange("p a b -> p (a b)"), start=False, stop=True)
        else:
            for h in range(nh):
                mm(pv[:, h, 0:c], ones[0:c, :], R[0:c, h, 0:c], start=True, stop=False)
                mm(pv[0:c, h, 0:c], ident[0:c, 0:c], mask2d[0:c, 0:c], start=False, stop=True)
        return pv

    def solve(X, XT, Xb, XTb, Sm, nh, c):
        tt(Sm[0:c, :, 0:c], XT[0:c, :, 0:c], bc(ident[0:c, 0:c].unsqueeze(1), [c, nh, c]), ALU.add)
        ns = nsteps_for(c)
        cur, curT, nxt, nxtT = X, XT, Xb, XTb
        for s_ in range(ns):
            last = s_ == ns - 1
            pX = P()
            pXv = pX.rearrange("p (a b) -> p a b", b=64)
            for h in range(nh):
                mm(pXv[0:c, h, 0:c], curT[0:c, h, 0:c], cur[0:c, h, 0:c])
            cp(nxt[0:c, :, 0:c], pXv[0:c, 0:nh, 0:c], eng="act")
            if not last:
                pT = P()
                pTv = pT.rearrange("p (a b) -> p a b", b=64)
                for h in range(nh):
                    mm(pTv[0:c, h, 0:c], cur[0:c, h, 0:c], curT[0:c, h, 0:c])
                cp(nxtT[0:c, :, 0:c], pTv[0:c, 0:nh, 0:c])
            pS = P()
            pSv = pS.rearrange("p (a b) -> p a b", b=64)
            for h in range(nh):
                mm(pSv[0:c, h, 0:c], nxt[0:c, h, 0:c], Sm[0:c, h, 0:c])
            tt(Sm[0:c, :, 0:c], Sm[0:c, :, 0:c], pSv[0:c, 0:nh, 0:c], ALU.add)
            cur, curT, nxt, nxtT = nxt, nxtT, cur, curT

    XA = g8(21, 64); XTA = g8(22, 64); XB = g8(23, 64); XTB = g8(24, 64); SM = g8(25, 64)

    cbg = B0[:, 0:804].rearrange("p (a b) -> p a b", b=67)
    gacc = B1[:, 0:768].rearrange("p (a b) -> p a b", b=64)
    gtm = B2[:, 0:768].rearrange("p (a b) -> p a b", b=64)
    qkv = B3[:, 0:768].rearrange("p (a b) -> p a b", b=64)
    gbT = sb([4, 2, 64], name="gbT")
    gacol = sb([4, 2], name="gacol")
    gbTok = sb([64, 8], name="gbTok")
    gst = sb([64, 32], name="gst")
    gtotb = sb([128, 4], name="gtotb")
    Rg = g4(0, 64); decT = g4(1, 64); dec2 = g4(2, 64); qkT = g4(3, 64)
    kTok = g8(4, 64, 128); vTok = g8(5, 64, 128); rvk = g8(6, 64, 128); rkk = g8(7, 64, 128); kdk = g8(8, 64, 128)
    usb = g8(9, 64, 128); vnew = g8(11, 64, 128); osb = g8(12, 64, 128); o2s = g8(16, 64, 128)
    wcT = g4(17)
    Sg = g8(18, 128, 128)
    gsq = g8(19); grs = g8(20)

    def gdn_chunk(l, t0, c, first, nxt=None):
        gdn_pre(l, t0, c)
        gdn_chunk2(l, t0, c, first, nxt)

    def gdn_pre(l, t0, c, part=None):
        part = part or ("z" if pf["gdn"] else "all")
        if part in ("all", "in"):
            S.dma("sp", cbg[:, :, 3:3 + c], projT[C_GDN:C_GDN + 1536, t0:t0 + c].rearrange("(k p) t -> p k t", p=128))
            S.dma("sp", garaw[:, 0:c], projT[C_GDN + 2048:C_GDN + 2052, t0:t0 + c])
            S.dma("sp", gbraw[:, 0:c], projT[C_GDN + 2052:C_GDN + 2056, t0:t0 + c])
        if part in ("all", "z"):
            S.dma("sp", zT[:, :, 0:c], projT[C_GDN + 1536:C_GDN + 2048, t0:t0 + c].rearrange("(k p) t -> p k t", p=128))
            act(zT[:, :, 0:c], zT[:, :, 0:c], AF.Silu)
            pf["gdn"] = False
        if part == "in":
            pf["gdn"] = True
        g_ps[0] = garaw
        g_ps[1] = gbraw

    g_ps = [None, None]

    def gdn_gb(c):
        act(gbT[:, 1, 0:c], g_ps[1][0:4, 0:c], AF.Sigmoid)
        p = g_ps[0]
        act(gbT[:, 0, 0:c], p[0:4, 0:c], AF.Exp, bias=pc("gdt_bias", 0, 4))
        act(gbT[:, 0, 0:c], gbT[:, 0, 0:c], AF.Ln, bias=epsc[0:4, 1:2])
        ts(gbT[:, 0, 0:c], gbT[:, 0, 0:c], gacol[:, 1:2], ALU.mult)

    def gdn_chunk2(l, t0, c, first, nxt=None):
        conv_silu(cbg, "gconv_w", 12, c, gacc[:, :, 0:c], gtm[:, :, 0:c], qkv[:, :, 0:c], None)
        gdn_gb(c)
        cp(gtm[:, :, 0:3], cbg[:, :, c:c + 3], eng="pool")
        cp(cbg[:, :, 0:3], gtm[:, :, 0:3], eng="pool")
        if nxt is not None:
            gdn_pre(l, nxt[0], nxt[1], "in")
        gdn_norm(c)
        gdn_core(c, 0, first)

    def gdn_norm(c):
        tt(gsq[:, :, 0:c], qkv[:, 0:8, 0:c], qkv[:, 0:8, 0:c], ALU.mult)
        p = P()
        pv = p.rearrange("p (a b) -> p a b", b=64)
        for j in range(8):
            mm(pv[:, j, 0:c], ones[:, :], gsq[:, j, 0:c])
        act(grs[:, :, 0:c], pv[:, :, 0:c], AF.Ln, bias=epsc[:, 0:1])
        act(grs[:, :, 0:c], grs[:, :, 0:c], AF.Exp, scale=-0.5)
        tt(qkv[:, 0:8, 0:c], qkv[:, 0:8, 0:c], grs[:, :, 0:c], ALU.mult)
        ts(qkv[:, 0:4, 0:c], qkv[:, 0:4, 0:c], float(128 ** -0.5), ALU.mult)

    def gdn_core(c, o, first):
        p = P()
        pv = p.rearrange("p (a b) -> p a b", b=128)
        for h in range(4):
            tr(pv[0:c, h, :], qkv[:, 4 + h, o:o + c], 128)
        cp(kTok[0:c], pv[0:c], eng="act")
        p = P()
        pv = p.rearrange("p (a b) -> p a b", b=128)
        for h in range(4):
            tr(pv[0:c, h, :], qkv[:, 8 + h, o:o + c], 128)
        cp(vTok[0:c], pv[0:c], eng="act")
        p = P()
        tr(p[0:c, 0:4], gbT[:, 0, o:o + c], 4)
        tr(p[0:c, 4:8], gbT[:, 1, o:o + c], 4)
        cp(gbTok[0:c], p[0:c, 0:8])
        p = P()
        mm(p[0:c, 0:4], triu[0:c, 0:c], gbTok[0:c, 0:4])
        cp(gst[0:c, 0:4], p[0:c, 0:4])
        ts(gst[0:c, 4:8], p[0:c, 0:4], -1.0, ALU.mult)
        act(gst[0:c, 8:12], p[0:c, 0:4], AF.Exp)
        tt(gst[0:c, 16:20], gst[0:c, 8:12], gbTok[0:c, 4:8], ALU.mult)
        ts(gst[0:c, 20:24], gbTok[0:c, 4:8], -1.0, ALU.mult)
        tt(Rg[0:c, :, 0:c], bc(gbTok[0:c, 0:4].unsqueeze(2), [c, 4, c]), bc(triu[0:c, 0:c].unsqueeze(1), [c, 4, c]), ALU.mult)
        pa = cs_bc(Rg, 4, c, negU8[:, 0:4, :], negU)
        tt(decT[0:c, :, 0:c], pa[0:c, 0:4, 0:c], bc(gst[0:c, 4:8].unsqueeze(2), [c, 4, c]), ALU.add)
        act(decT[0:c, :, 0:c], decT[0:c, :, 0:c], AF.Exp)
        act(gtotb[:, :], pa[:, 0:4, c - 1], AF.Exp)
        tt(gst[0:c, 12:16], pa[0:c, 0:4, c - 1], gst[0:c, 4:8], ALU.add)
        act(gst[0:c, 12:16], gst[0:c, 12:16], AF.Exp)
        one = (c == 1 and not first)
        pQ = P()
        pQv = pQ.rearrange("p (a b) -> p a b", b=64)
        if one:
            for h in range(4):
                mm(pQv[0:c, h, 0:c], qkv[:, 4 + h, o:o + c], qkv[:, h, o:o + c])
            tt(qkT[0:c, :, 0:c], pQv[0:c, 0:4, 0:c], decT[0:c, :, 0:c], ALU.mult)
        else:
            pb = cs_bc(Rg, 4, c, posL4, posLs)
            tt(dec2[0:c, :, 0:c], pb[0:c, 0:4, 0:c], bc(gst[0:c, 4:8].unsqueeze(2), [c, 4, c]), ALU.add)
            act(dec2[0:c, :, 0:c], dec2[0:c, :, 0:c], AF.Exp, scale=-1.0)
            pG = P()
            pGv = pG.rearrange("p (a b) -> p a b", b=64)
            for h in range(4):
                mm(pGv[0:c, h, 0:c], qkv[:, 4 + h, o:o + c], qkv[:, 4 + h, o:o + c])
                mm(pQv[0:c, h, 0:c], qkv[:, 4 + h, o:o + c], qkv[:, h, o:o + c])
            tt(qkT[0:c, :, 0:c], pQv[0:c, 0:4, 0:c], decT[0:c, :, 0:c], ALU.mult)
            tt(XA[0:c, 0:4, 0:c], pGv[0:c, 0:4, 0:c], dec2[0:c, :, 0:c], ALU.mult)
            tt(XA[0:c, 0:4, 0:c], XA[0:c, 0:4, 0:c], bc(gst[0:c, 20:24].unsqueeze(2), [c, 4, c]), ALU.mult)
            p = P()
            pv = p.rearrange("p (a b) -> p a b", b=64)
            for h in range(4):
                tr(pv[0:c, h, 0:c], XA[0:c, h, 0:c], c)
            cp(XTA[0:c, 0:4, 0:c], pv[0:c, 0:4, 0:c])
            solve(XA[:, 0:4], XTA[:, 0:4], XB[:, 0:4], XTB[:, 0:4], SM[:, 0:4], 4, c)
        tt(rvk[0:c], vTok[0:c], bc(gbTok[0:c, 4:8].unsqueeze(2), [c, 4, 128]), ALU.mult)
        tt(rkk[0:c], kTok[0:c], bc(gst[0:c, 16:20].unsqueeze(2), [c, 4, 128]), ALU.mult, eng="pool")
        tt(kdk[0:c], kTok[0:c], bc(gst[0:c, 12:16].unsqueeze(2), [c, 4, 128]), ALU.mult, eng="pool")
        pW = P()
        pWv = pW.rearrange("p (a b) -> p a b", b=64)
        if one:
            cp(usb[0:c], rvk[0:c], eng="act")
            for h in range(4):
                mm(pWv[:, h, 0:c], rkk[0:c, h, :], ident[0:c, 0:c])
        else:
            pU = P()
            pUv = pU.rearrange("p (a b) -> p a b", b=128)
            for h in range(4):
                mm(pUv[0:c, h, :], SM[0:c, h, 0:c], rvk[0:c, h, :])
            cp(usb[0:c], pUv[0:c], eng="act")
            for h in range(4):
                mm(pWv[:, h, 0:c], rkk[0:c, h, :], SM[0:c, h, 0:c])
        cp(wcT[:, :, 0:c], pWv[:, 0:4, 0:c], eng="act")
        if not first:
            pws = P()
            pwv = pws.rearrange("p (a b) -> p a b", b=128)
            for h in range(4):
                mm(pwv[0:c, h, :], wcT[:, h, 0:c], Sg[:, h, :])
            tt(vnew[0:c], usb[0:c], pwv[0:c], ALU.subtract)
            vn = vnew
        else:
            vn = usb
        pO2 = P()
        pO2v = pO2.rearrange("p (a b) -> p a b", b=128)
        for h in range(4):
            mm(pO2v[0:c, h, :], qkT[0:c, h, 0:c], vn[0:c, h, :])
        if not first:
            cp(o2s[0:c], pO2v[0:c], eng="act")
            pO1 = P()
            pO1v = pO1.rearrange("p (a b) -> p a b", b=128)
            for h in range(4):
                mm(pO1v[0:c, h, :], qkv[:, h, o:o + c], Sg[:, h, :])
            tt(osb[0:c], pO1v[0:c], bc(gst[0:c, 8:12].unsqueeze(2), [c, 4, 128]), ALU.mult)
            tt(osb[0:c], osb[0:c], o2s[0:c], ALU.add)
        else:
            cp(osb[0:c], pO2v[0:c], eng="act")
        pS = P()
        pSv = pS.rearrange("p (a b) -> p a b", b=128)
        for h in range(4):
            mm(pSv[:, h, :], kdk[0:c, h, :], vn[0:c, h, :])
        if first:
            cp(Sg, pSv)
        else:
            tt(Sg, Sg, bc(gtotb.unsqueeze(2), [128, 4, 128]), ALU.mult)
            tt(Sg, Sg, pSv, ALU.add)
        p4 = P()
        p4v = p4.rearrange("p (a b) -> p a b", b=64)
        for h in range(4):
            tr(p4v[:, h, 0:c], osb[0:c, h, :], c)
        cp(yT[:, :, o:o + c], p4v[:, 0:4, 0:c], eng="act")

    def gdn_post(l, t0, c, bi):
        tt(sq[:, :, 0:c], yT[:, :, 0:c], yT[:, :, 0:c], ALU.mult)
        p = P()
        pv = p.rearrange("p (a b) -> p a b", b=64)
        for h in range(4):
            mm(pv[:, h, 0:c], ones[:, :], sq[:, h, 0:c])
        act(sq[:, :, 0:c], pv[:, 0:4, 0:c], AF.Ln, bias=epsc[:, 0:1], scale=1.0 / 128)
        act(sq[:, :, 0:c], sq[:, :, 0:c], AF.Exp, scale=-0.5)
        tt(yT[:, :, 0:c], yT[:, :, 0:c], sq[:, :, 0:c], ALU.mult)
        tt(yT[:, :, 0:c], yT[:, :, 0:c], zT[:, :, 0:c], ALU.mult)
        ts(yT[:, :, 0:c], yT[:, :, 0:c], pc("gnorm_w"), ALU.mult)
        S.dma("sp", yb[bi][:, t0:t0 + c].rearrange("(k p) t -> p k t", p=128), yT[:, :, 0:c])

    def gdn_branch(l, bi):
        nonlocal Sg
        act(gacol[:, 0:1], pc("ga_log", 0, 4), AF.Exp)
        ts(gacol[:, 1:2], gacol[:, 0:1], -1.0, ALU.mult)
        mset(cbg[:, :, 0:3], 0.0)
        for ci, (t0, c) in enumerate(chunks):
            gdn_chunk(l, t0, c, ci == 0, nxt_of(ci))
            gdn_post(l, t0, c, bi)
        gdn_state_out(p_gdn[l], p_gconv[l])
        if with_samples:
            c = NS
            gdn_pre(l, TP, c)
            S.dma("sp", B1[0:48, 0:1536], st_gconv[l].rearrange("b i c -> (b i) c"))
            sbT = B2[:, 0:576].rearrange("p (k b i) -> p k b i", k=12, i=3)
            for half in range(2):
                p = P()
                for k in range(6):
                    kk_ = half * 6 + k
                    tr(p[:, k * 48:(k + 1) * 48], B1[0:48, kk_ * 128:(kk_ + 1) * 128], 48)
                cp(B2[:, half * 288:(half + 1) * 288], p[:, 0:288])
            W4 = pcs("gconv_w", 48)
            g16 = G[26][:, 0:192].rearrange("p (a b) -> p a b", b=16)
            t16 = G[27][:, 0:192].rearrange("p (a b) -> p a b", b=16)
            for i in range(4):
                src = sbT[:, :, :, i] if i < 3 else cbg[:, :, 3:3 + c]
                wv = bc(W4[:, i * 12:(i + 1) * 12].unsqueeze(2), [128, 12, c])
                if i == 0:
                    tt(g16, src, wv, ALU.mult)
                else:
                    tt(t16, src, wv, ALU.mult, eng="pool")
                    tt(g16, g16, t16, ALU.add)
            act(qkv[:, :, 0:c], g16, AF.Silu)
            gdn_gb(c)
            S.dma("sp", s_gconv[l, :, 0:2, :], st_gconv[l, :, 1:3, :])
            for g3 in range(3):
                p = P()
                for j in range(4):
                    tr(p[0:c, j * 128:(j + 1) * 128], cbg[:, g3 * 4 + j, 3:3 + c], 128)
                cp(B1[0:c, g3 * 512:(g3 + 1) * 512], p[0:c, :], eng="act")
            S.dma("sp", s_gconv[l, :, 2, :], B1[0:c, 0:1536])
            gdn_norm(c)
            SGS = [g8(18, 128, 128), g8(28, 128, 128)]
            S.dma("sp", SGS[0], st_gdn[l, 0].rearrange("h k v -> k h v"))
            for b in range(NS):
                if b + 1 < NS:
                    S.dma("sp", SGS[(b + 1) % 2], st_gdn[l, b + 1].rearrange("h k v -> k h v"))
                Sg = SGS[b % 2]
                gdn_core(1, b, False)
                gdn_state_out(s_gdn[l, b], None)
            Sg = SGS[0]
            gdn_post(l, TP, c, bi)

    def gdn_state_out(dst_st, dst_conv):
        for h in range(4):
            S.dma("sp", dst_st[h], Sg[:, h, :])
        if dst_conv is None:
            return
        for k in range(12):
            p = P()
            tr(p[0:3, 0:128], cbg[:, k, 0:3], 128)
            cp(y1s[0:3, 0:128], p[0:3, 0:128])
            S.dma("sp", dst_conv[:, k * 128:(k + 1) * 128], y1s[0:3, 0:128])

    prw = B0[0:64, 0:1690].rearrange("p (a b) -> p a b", b=65)
    uT = B1[0:64, 0:1664].rearrange("p (a b) -> p a b", b=64)
    dsh = B2[0:64, 0:1664].rearrange("p (a b) -> p a b", b=64)
    zr = g8(0, 64)
    w2sb = G[30][0:64, :]
    a2sb = G[31][0:64, :]
    thw = sb([64, 64], name="thw")
    ldT = g8(1, 64); aT = g8(2, 64); kkT = g8(3, 64); kT_ = g8(4, 64); t8a = g8(5, 64); t8b = g8(6, 64)
    bon = g8(7, 64); lw = g8(8, 64)
    rmask = sb([64, 8, 64], name="rmask")
    mset(rmask, 1.0)
    mset(rmask[:, :, 0:1], 0.0)
    Wt = g8(9, 64); rt = g8(11, 64); at = g8(12, 64); bt = g8(13, 64); kt = g8(14, 64); bh = g8(15, 64); kh = g8(16, 64)
    BhTok = g8(17, 64); KhTok = g8(18, 64); VTok = g8(19, 64); AtTok = g8(20, 64)
    LakT = g8(26, 64); MrbT = g8(27, 64); MrkT = g8(28, 64); STs = g8(29, 64)
    Zsb = g8(5, 64); U1 = g8(6, 64); Usb = g8(1, 64); PTs = g8(2, 64); ytk = g8(3, 64); yrT = g8(4, 64)
    ybf8 = sb([64, 8, 64], BF16, name="ybf8")

    def mm8(pv, lhs, rhs, c, msk, out, o=0):
        for h in range(8):
            mm(pv[0:c, h, 0:c], lhs[:, h, o:o + c], rhs[:, h, o:o + c])
        tt(out[0:c, :, 0:c], pv[0:c, :, 0:c], bc(msk[0:c, 0:c].unsqueeze(1), [c, 8, c]), ALU.mult)

    def to_tok(dst, src, c, o=0):
        p = P()
        pv = p.rearrange("p (a b) -> p a b", b=64)
        for h in range(8):
            tr(pv[0:c, h, :], src[:, h, o:o + c], 64)
        cp(dst[0:c], pv[0:c], eng="act")

    def rw_chunk(l, t0, c, first, nxt=None):
        rw_pre(l, t0, c)
        rw_chunk2(l, t0, c, first, nxt)

    def rw_pre(l, t0, c, part=None):
        part = part or ("z" if pf["rw"] else "all")
        if part in ("all", "in"):
            S.dma("sp", prw[:, :, 1:1 + c], projT[0:1664, t0:t0 + c].rearrange("(j p) t -> p j t", p=64))
        if part in ("all", "z"):
            S.dma("sp", zr[:, :, 0:c], projT[1664:2176, t0:t0 + c].rearrange("(j p) t -> p j t", p=64))
            act(zr[:, :, 0:c], zr[:, :, 0:c], AF.Silu)
            pf["rw"] = False
        if part == "in":
            pf["rw"] = True

    def rw_chunk2(l, t0, c, first, nxt=None):
        rw_shift(c, None)
        if nxt is not None:
            rw_pre(l, nxt[0], nxt[1], "in")
        rw_elem(c, False)
        rw_core(c, 0, first)

    def rw_shift(c, shT):
        tt(dsh[:, :, 0:c], (prw[:, :, 0:c] if shT is None else shT), prw[:, :, 1:1 + c], ALU.subtract)
        tt(dsh[:, :, 0:c], dsh[:, :, 0:c], bc(pcs("mu", 26, 64).unsqueeze(2), [64, 26, c]), ALU.mult)
        tt(uT[:, :, 0:c], dsh[:, :, 0:c], prw[:, :, 1:1 + c], ALU.add)
        cp(dsh[:, :, 0:1], prw[:, :, c:c + 1], eng="pool")
        cp(prw[:, :, 0:1], dsh[:, :, 0:1], eng="pool")

    def rw_elem(c, sample):
        r_, k0, v_ = uT[:, 0:8], uT[:, 8:16], uT[:, 16:24]
        act(thw[:, 0:c], uT[:, 24, 0:c], AF.Tanh)
        p = P()
        pv = p.rearrange("p (a b) -> p a b", b=64)
        for h in range(8):
            mm(pv[0:64, h, 0:c], w2sb[:, h * 64:(h + 1) * 64], thw[:, 0:c])
        tt(ldT[:, :, 0:c], pv[0:64, :, 0:c], bc(pcs("w0", 8, 64).unsqueeze(2), [64, 8, c]), ALU.add)
        act(ldT[:, :, 0:c], ldT[:, :, 0:c], AF.Sigmoid)
        ts(ldT[:, :, 0:c], ldT[:, :, 0:c], -EXPM05, ALU.mult)
        p = P()
        pv = p.rearrange("p (a b) -> p a b", b=64)
        for h in range(8):
            mm(pv[0:64, h, 0:c], a2sb[:, h * 64:(h + 1) * 64], uT[:, 25, 0:c])
        tt(aT[:, :, 0:c], pv[0:64, :, 0:c], bc(pcs("a0", 8, 64).unsqueeze(2), [64, 8, c]), ALU.add)
        act(aT[:, :, 0:c], aT[:, :, 0:c], AF.Sigmoid)
        tt(kkT[:, :, 0:c], k0[:, :, 0:c], bc(pcs("k_k", 8, 64).unsqueeze(2), [64, 8, c]), ALU.mult)
        tt(t8a[:, :, 0:c], kkT[:, :, 0:c], kkT[:, :, 0:c], ALU.mult)
        p = P()
        pv = p.rearrange("p (a b) -> p a b", b=64)
        for h in range(8):
            mm(pv[0:64, h, 0:c], ones[0:64, 0:64], t8a[:, h, 0:c])
        act(t8a[:, :, 0:c], pv[0:64, :, 0:c], AF.Ln, bias=epsc[0:64, 0:1])
        act(t8a[:, :, 0:c], t8a[:, :, 0:c], AF.Exp, scale=-0.5)
        tt(kkT[:, :, 0:c], kkT[:, :, 0:c], t8a[:, :, 0:c], ALU.mult)
        ts(t8a[:, :, 0:c], aT[:, :, 0:c], -1.0, ALU.add)
        tt(t8a[:, :, 0:c], t8a[:, :, 0:c], bc(pcs("k_a", 8, 64).unsqueeze(2), [64, 8, c]), ALU.mult)
        stt(kT_[:, :, 0:c], t8a[:, :, 0:c], 1.0, k0[:, :, 0:c], ALU.add, ALU.mult)
        tt(t8a[:, :, 0:c], r_[:, :, 0:c], kT_[:, :, 0:c], ALU.mult)
        tt(t8a[:, :, 0:c], t8a[:, :, 0:c], bc(pcs("r_k", 8, 64).unsqueeze(2), [64, 8, c]), ALU.mult)
        p = P()
        pv = p.rearrange("p (a b) -> p a b", b=64)
        for h in range(8):
            mm(pv[0:64, h, 0:c], ones[0:64, 0:64], t8a[:, h, 0:c])
        tt(bon[:, :, 0:c], pv[0:64, :, 0:c], v_[:, :, 0:c], ALU.mult)
        if sample:
            cp(lw[:, :, 0:c], ldT[:, :, 0:c])
            act(Wt[:, :, 0:c], lw[:, :, 0:c], AF.Exp)
            tt(rt[:, :, 0:c], r_[:, :, 0:c], Wt[:, :, 0:c], ALU.mult)
            act(t8a[:, :, 0:c], lw[:, :, 0:c], AF.Exp, scale=-1.0)
            tt(t8b[:, :, 0:c], kkT[:, :, 0:c], aT[:, :, 0:c], ALU.mult)
            tt(bt[:, :, 0:c], t8b[:, :, 0:c], t8a[:, :, 0:c], ALU.mult)
            tt(kt[:, :, 0:c], kT_[:, :, 0:c], t8a[:, :, 0:c], ALU.mult)
            ts(at[:, :, 0:c], kkT[:, :, 0:c], -1.0, ALU.mult)
            cp(bh[:, :, 0:c], t8b[:, :, 0:c])
            cp(kh[:, :, 0:c], kT_[:, :, 0:c])
            return
        if c == 64:
            S.op("dve", lambda e: e.tensor_tensor_scan(out=lw.rearrange("p a b -> p (a b)"), data0=rmask.rearrange("p a b -> p (a b)"),
                                                       data1=ldT.rearrange("p a b -> p (a b)"), initial=0.0,
                                                       op0=ALU.mult, op1=ALU.add), reads=[rmask, ldT], writes=[lw])
        else:
            for h in range(8):
                S.op("dve", lambda e, h=h: e.tensor_tensor_scan(out=lw[:, h, 0:c], data0=rmask[:, h, 0:c], data1=ldT[:, h, 0:c],
                                                                initial=0.0, op0=ALU.mult, op1=ALU.add), reads=[rmask, ldT], writes=[lw])
        act(Wt[:, :, 0:c], lw[:, :, 0:c], AF.Exp)
        tt(rt[:, :, 0:c], r_[:, :, 0:c], Wt[:, :, 0:c], ALU.mult)
        act(t8a[:, :, 0:c], lw[:, :, 0:c], AF.Exp, scale=-1.0)
        tt(t8b[:, :, 0:c], kkT[:, :, 0:c], aT[:, :, 0:c], ALU.mult)
        tt(bt[:, :, 0:c], t8b[:, :, 0:c], t8a[:, :, 0:c], ALU.mult)
        tt(kt[:, :, 0:c], kT_[:, :, 0:c], t8a[:, :, 0:c], ALU.mult)
        tt(t8a[:, :, 0:c], lw[:, :, 0:c], ldT[:, :, 0:c], ALU.subtract)
        act(t8a[:, :, 0:c], t8a[:, :, 0:c], AF.Exp)
        stt(at[:, :, 0:c], kkT[:, :, 0:c], -1.0, t8a[:, :, 0:c], ALU.mult, ALU.mult)
        tt(t8a[:, :, 0:c], bc(lw[:, :, c - 1:c], [64, 8, c]), lw[:, :, 0:c], ALU.subtract)
        act(t8a[:, :, 0:c], t8a[:, :, 0:c], AF.Exp)
        tt(bh[:, :, 0:c], t8b[:, :, 0:c], t8a[:, :, 0:c], ALU.mult)
        tt(kh[:, :, 0:c], kT_[:, :, 0:c], t8a[:, :, 0:c], ALU.mult)

    def rw_core(c, o, first):
        r_ = uT[:, 0:8]
        v_ = uT[:, 16:24]
        to_tok(BhTok, bh, c, o)
        to_tok(KhTok, kh, c, o)
        to_tok(VTok, v_, c, o)
        if c == 1 and not first:
            pv = P().rearrange("p (a b) -> p a b", b=64)
            mm8(pv, bt, rt, c, triu, MrbT, o)
            pv = P().rearrange("p (a b) -> p a b", b=64)
            mm8(pv, kt, rt, c, triu, MrkT, o)
            pv = P().rearrange("p (a b) -> p a b", b=64)
            for h in range(8):
                mm(pv[0:c, h, :], at[:, h, o:o + c], STs[:, h, :])
            cp(Usb[0:c], pv[0:c], eng="act")
            Uc = Usb
        else:
            to_tok(AtTok, at, c, o)
            pv = P().rearrange("p (a b) -> p a b", b=64)
            mm8(pv, at, bt, c, trils, XA, o)
            pv = P().rearrange("p (a b) -> p a b", b=64)
            mm8(pv, bt, at, c, trius, XTA, o)
            pv = P().rearrange("p (a b) -> p a b", b=64)
            mm8(pv, kt, at, c, trius, LakT, o)
            pv = P().rearrange("p (a b) -> p a b", b=64)
            mm8(pv, bt, rt, c, triu, MrbT, o)
            pv = P().rearrange("p (a b) -> p a b", b=64)
            mm8(pv, kt, rt, c, triu, MrkT, o)
            solve(XA, XTA, XB, XTB, SM, 8, c)
            pv = P().rearrange("p (a b) -> p a b", b=64)
            for h in range(8):
                mm(pv[0:c, h, :], LakT[0:c, h, 0:c], VTok[0:c, h, :])
            cp(Zsb[0:c], pv[0:c], eng="act")
            pv = P().rearrange("p (a b) -> p a b", b=64)
            for h in range(8):
                mm(pv[0:c, h, :], SM[0:c, h, 0:c], Zsb[0:c, h, :])
            cp(U1[0:c], pv[0:c], eng="act")
            pv = P().rearrange("p (a b) -> p a b", b=64)
            for h in range(8):
                mm(pv[0:64, h, 0:c], AtTok[0:c, h, :], SM[0:c, h, 0:c])
            cp(PTs[:, :, 0:c], pv[0:64, :, 0:c], eng="act")
            if not first:
                pv = P().rearrange("p (a b) -> p a b", b=64)
                for h in range(8):
                    mm(pv[0:c, h, :], PTs[:, h, 0:c], STs[:, h, :])
                tt(Usb[0:c], U1[0:c], pv[0:c], ALU.add)
                Uc = Usb
            else:
                Uc = U1
        pY = P().rearrange("p (a b) -> p a b", b=64)
        for h in range(8):
            if not first:
                mm(pY[0:c, h, :], rt[:, h, o:o + c], STs[:, h, :], start=True, stop=False)
            mm(pY[0:c, h, :], MrbT[0:c, h, 0:c], Uc[0:c, h, :], start=first, stop=False)
            mm(pY[0:c, h, :], MrkT[0:c, h, 0:c], VTok[0:c, h, :], start=False, stop=True)
        cp(ytk[0:c], pY[0:c], eng="act")
        pS = P().rearrange("p (a b) -> p a b", b=64)
        for h in range(8):
            mm(pS[0:64, h, :], BhTok[0:c, h, :], Uc[0:c, h, :], start=True, stop=False)
            mm(pS[0:64, h, :], KhTok[0:c, h, :], VTok[0:c, h, :], start=False, stop=True)
        if first:
            cp(STs, pS[0:64])
        else:
            tt(STs, STs, bc(Wt[:, :, o + c - 1:o + c], [64, 8, 64]), ALU.mult)
            tt(STs, STs, pS[0:64], ALU.add)
        pv = P().rearrange("p (a b) -> p a b", b=64)
        for h in range(8):
            tr(pv[0:64, h, 0:c], ytk[0:c, h, :], c)
        cp(yrT[:, :, o:o + c], pv[0:64, :, 0:c], eng="act")

    def rw_post(l, t0, c, bi):
        pv = P().rearrange("p (a b) -> p a b", b=64)
        for h in range(8):
            mm(pv[0:64, h, 0:c], ones[0:64, 0:64], yrT[:, h, 0:c])
        stt(yrT[:, :, 0:c], pv[0:64, :, 0:c], -1.0 / 64, yrT[:, :, 0:c], ALU.mult, ALU.add)
        tt(t8a[:, :, 0:c], yrT[:, :, 0:c], yrT[:, :, 0:c], ALU.mult)
        pv = P().rearrange("p (a b) -> p a b", b=64)
        for h in range(8):
            mm(pv[0:64, h, 0:c], ones[0:64, 0:64], t8a[:, h, 0:c])
        act(t8a[:, :, 0:c], pv[0:64, :, 0:c], AF.Ln, bias=epsc[0:64, 3:4], scale=1.0 / 64)
        act(t8a[:, :, 0:c], t8a[:, :, 0:c], AF.Exp, scale=-0.5)
        tt(yrT[:, :, 0:c], yrT[:, :, 0:c], t8a[:, :, 0:c], ALU.mult)
        tt(yrT[:, :, 0:c], yrT[:, :, 0:c], bc(pcs("gn_w", 8, 64).unsqueeze(2), [64, 8, c]), ALU.mult)
        tt(yrT[:, :, 0:c], yrT[:, :, 0:c], bc(pcs("gn_b", 8, 64).unsqueeze(2), [64, 8, c]), ALU.add)
        tt(yrT[:, :, 0:c], yrT[:, :, 0:c], bon[:, :, 0:c], ALU.add)
        tt(yrT[:, :, 0:c], yrT[:, :, 0:c], zr[:, :, 0:c], ALU.mult)
        S.dma("sp", yb[bi][:, t0:t0 + c].rearrange("(h p) t -> p h t", p=64), yrT[:, :, 0:c])

    def rw_branch(l, bi):
        S.dma("sp", w2sb, w2[l])
        S.dma("sp", a2sb, a2[l])
        mset(prw[:, :, 0:1], 0.0)
        for ci, (t0, c) in enumerate(chunks):
            rw_chunk(l, t0, c, ci == 0, nxt_of(ci))
            rw_post(l, t0, c, bi)
        rw_state_out(p_wkv[l], p_shift[l])
        if with_samples:
            c = NS
            rw_pre(l, TP, c)
            S.dma("sp", B2[64:64 + c, 0:1664], st_shift[l]) if False else None
            stt_ = G[10][0:c, :]
            shT = g8(29, 64)[:, :, 0:32].rearrange("p a b -> p (a b)")[:, 0:0] if False else None
            shT = B2[64:128, 0:416].rearrange("p (a b) -> p a b", b=16) if False else None
            shT = G[25][0:64, 0:416].rearrange("p (a b) -> p a b", b=16)
            for q in range(4):
                nj = 8 if q < 3 else 2
                S.dma("sp", stt_[:, 0:nj * 64], st_shift[l, :, q * 512:q * 512 + nj * 64])
                p = P()
                for j in range(nj):
                    tr(p[0:64, j * 16:(j + 1) * 16], stt_[:, j * 64:(j + 1) * 64], c)
                cp(shT[:, q * 8:q * 8 + nj, :], p[0:64, 0:nj * 16].rearrange("p (a b) -> p a b", b=16))
            rw_shift(c, shT)
            for q in range(4):
                nj = 8 if q < 3 else 2
                p = P()
                for j in range(nj):
                    tr(p[0:c, j * 64:(j + 1) * 64], prw[:, q * 8 + j, 1:1 + c], 64)
                cp(stt_[:, 0:nj * 64], p[0:c, 0:nj * 64], eng="act")
                S.dma("sp", s_shift[l, :, q * 512:q * 512 + nj * 64], stt_[:, 0:nj * 64])
            rw_elem(c, True)
            stg = B2[0:64, 0:512].rearrange("p (a b) -> p a b", b=64)
            S.dma("sp", stg, st_wkv[l, 0].rearrange("h v k -> v h k"))
            for b in range(NS):
                pv = P().rearrange("p (a b) -> p a b", b=64)
                for h in range(8):
                    tr(pv[0:64, h, :], stg[:, h, :], 64)
                cp(STs, pv[0:64])
                if b + 1 < NS:
                    S.dma("sp", stg, st_wkv[l, b + 1].rearrange("h v k -> v h k"))
                rw_core(1, b, False)
                rw_state_out(s_wkv[l, b], None)
            rw_post(l, TP, c, bi)

    def rw_state_out(dst_st, dst_shift):
        pv = P().rearrange("p (a b) -> p a b", b=64)
        for h in range(8):
            tr(pv[0:64, h, :], STs[:, h, :], 64)
        cp(ytk, pv[0:64])
        S.dma("sp", dst_st.rearrange("h v k -> v h k"), ytk)
        if dst_shift is None:
            return
        p = P()
        tr(p[0:26, 0:64], prw[:, :, 0:1].rearrange("p a b -> p (a b)"), 64)
        cp(y1s[0:26, 0:64], p[0:26, 0:64])
        S.dma("sp", dst_shift.rearrange("(j p) -> j p", p=64), y1s[0:26, 0:64])

    for l in range(depth):
        S.dma("sp", pt, ptab[l])
        cur_src[0] = xin if l == 0 else xs[l % 2]
        phase1(l)
        bi = 0
        for b in branches:
            if b == "ssm":
                ssd_branch(l, bi)
                bi += 1
            if b == "gdn":
                gdn_branch(l, bi)
                bi += 1
            if b == "rw":
                rw_branch(l, bi)
                bi += 1
        phase3(l)

    S.finish()
    S.emit()
    return nc, S


_CACHE = {}


def kernel(**inp):
    inp = {k: np.asarray(v) for k, v in inp.items()}
    if "nc" not in _CACHE:
        _CACHE["nc"] = build()[0]
    nc = _CACHE["nc"]
    return _run(nc, inp, 8)


def _in_maps(inp, ncores):
    ptab = np.stack([_build_pt(inp, l) for l in range(DEPTH)])
    maps = []
    for c in range(ncores):
        xin = np.concatenate([inp["meta_tokens"], inp["x_prompt"][c], inp["x_sample"][16 * c:16 * c + 16, 0]], axis=0)
        sl = slice(16 * c, 16 * c + 16)
        maps.append({
            "xin": np.ascontiguousarray(xin, dtype=np.float32),
            "w_in": inp["w_in"], "w_rw_out": inp["w_rw_out"], "w_ssm_out": inp["w_ssm_out"],
            "w_gdn_out": inp["w_gdn_out"], "w_out": inp["w_out"], "rw_w2": inp["rw_w2"], "rw_a2": inp["rw_a2"],
            "ptab": ptab, "final_norm_w": inp["final_norm_w"].reshape(1, D),
            "st_wkv": np.ascontiguousarray(inp["state_rwkv_wkv"][:, sl]),
            "st_shift": np.ascontiguousarray(inp["state_rwkv_shift"][:, sl]),
            "st_ssm": np.ascontiguousarray(inp["state_ssm"][:, sl]),
            "st_sconv": np.ascontiguousarray(inp["state_ssm_conv"][:, sl]),
            "st_gdn": np.ascontiguousarray(inp["state_gdn"][:, sl]),
            "st_gconv": np.ascontiguousarray(inp["state_gdn_conv"][:, sl]),
        })
    return maps


def _run(nc, inp, ncores):
    maps = _in_maps(inp, ncores)
    res = run_bass_kernel_spmd(nc, maps, core_ids=list(range(ncores)))
    R = res.results
    y_prompt = np.stack([R[c]["y_out"][NMETA:TP] for c in range(ncores)])
    y_sample = np.concatenate([R[c]["y_out"][TP:TT] for c in range(ncores)])[:, None, :]

    def pst(name):
        return np.stack([R[c][name] for c in range(ncores)], axis=1)

    def sst(name):
        return np.concatenate([R[c][name] for c in range(ncores)], axis=1)
    return (y_prompt, y_sample, pst("p_wkv"), pst("p_shift"), pst("p_ssm"), pst("p_sconv"), pst("p_gdn"),
            pst("p_gconv"), sst("s_wkv"), sst("s_shift"), sst("s_ssm"), sst("s_sconv"), sst("s_gdn"), sst("s_gconv"))
```
